# Optimizing a Trainium2 kernel written in Bass

```python
import math
import jax, jax.numpy as jnp
from jax import lax
import numpy as np


D_MODEL = 1024
BATCH = 4
SEQ = 8192
DEPTH = 1
DEC_BATCH = 16
DEC_SEQ = 32
PAST_LEN = 2048

CHUNK = 64
LEFT_CHUNKS = 8
WINDOW = CHUNK * LEFT_CHUNKS
BAND = WINDOW + CHUNK
D_MIX = D_MODEL
D_ATTN = D_MIX // 2
D_SSM = D_MIX - D_ATTN
ATTN_HEAD_DIM = 64
ATTN_HEADS = D_ATTN // ATTN_HEAD_DIM
ATTN_SCALE = ATTN_HEAD_DIM ** -0.5
MAX_REL = 128
N_REL = 2 * MAX_REL + 1
NEG_INF = -1e30
SSM_HEAD_DIM = 64
SSM_HEADS = D_SSM // SSM_HEAD_DIM
SSM_GROUPS = 2
SSM_HEADS_PER_GROUP = SSM_HEADS // SSM_GROUPS
SSM_STATE = 128
SSM_CONV = 4
SSM_CONV_DIM = D_SSM + 2 * SSM_GROUPS * SSM_STATE
D_IN_PROJ = 3 * D_ATTN + D_SSM + SSM_CONV_DIM + SSM_HEADS
SPLITS = (D_ATTN, 2 * D_ATTN, 3 * D_ATTN, 3 * D_ATTN + D_SSM, 3 * D_ATTN + D_SSM + SSM_CONV_DIM)
D_FF = 2688
FFN_CONV = 3
ALPHA = (2.0 * DEPTH) ** 0.25
BETA_INIT = (8.0 * DEPTH) ** -0.25
LN_EPS = 1e-5
RMS_EPS = 1e-5

kernel_name = 'hybrid_chunk_attn_ssd_convffn_step'


def layer_norm(x, g, b):
    xf = x.astype(jnp.float32)
    mu = jnp.mean(xf, axis=-1, keepdims=True)
    var = jnp.mean(jnp.square(xf - mu), axis=-1, keepdims=True)
    return ((xf - mu) * lax.rsqrt(var + LN_EPS) * g.astype(jnp.float32) + b.astype(jnp.float32)).astype(x.dtype)


def rms_norm(x, g):
    xf = x.astype(jnp.float32)
    return (xf * lax.rsqrt(jnp.mean(jnp.square(xf), axis=-1, keepdims=True) + RMS_EPS) * g.astype(jnp.float32)).astype(x.dtype)


def causal_dwconv(x, past, w, b):
    K = w.shape[0]
    L = x.shape[1]
    xp = jnp.concatenate([past.astype(x.dtype), x], axis=1)
    out = b + w[0] * xp[:, 0:L]
    for t in range(1, K):
        out = out + w[t] * xp[:, t:t + L]
    return out, xp[:, L:]


def band_attention(q, k, v, rel, valid, rel_bias):
    idx = jnp.clip(rel, -MAX_REL, MAX_REL) + MAX_REL
    bias = rel_bias[:, idx].astype(jnp.float32)
    s = jnp.einsum('bqhd,bkhd->bhqk', q, k).astype(jnp.float32) * ATTN_SCALE + bias
    if valid is not None:
        s = jnp.where(valid, s, NEG_INF)
    p = jax.nn.softmax(s, axis=-1).astype(v.dtype)
    return jnp.einsum('bhqk,bkhd->bqhd', p, v)


def attn_prompt(q, k, v, rel_bias):
    L = q.shape[1]
    n_chunks = L // CHUNK
    pad = ((0, 0), (WINDOW, 0), (0, 0), (0, 0))
    k_pad = jnp.pad(k, pad)
    v_pad = jnp.pad(v, pad)
    rel = (WINDOW + jnp.arange(CHUNK)[:, None]) - jnp.arange(BAND)[None, :]

    def one_chunk(c):
        start = c * CHUNK
        q_c = lax.dynamic_slice_in_dim(q, start, CHUNK, axis=1)
        k_c = lax.dynamic_slice_in_dim(k_pad, start, BAND, axis=1)
        v_c = lax.dynamic_slice_in_dim(v_pad, start, BAND, axis=1)
        valid = (start - WINDOW + jnp.arange(BAND))[None, :] >= 0
        return band_attention(q_c, k_c, v_c, rel, valid, rel_bias)

    out = lax.map(one_chunk, jnp.arange(n_chunks))
    return jnp.moveaxis(out, 0, 1).reshape(q.shape)


def attn_sample(q, k, v, k_past, v_past, rel_bias):
    n_past = k_past.shape[1]
    Lq = q.shape[1]
    kk = jnp.concatenate([k_past.astype(k.dtype), k], axis=1)
    vv = jnp.concatenate([v_past.astype(v.dtype), v], axis=1)
    rel = (n_past + jnp.arange(Lq)[:, None]) - jnp.arange(n_past + Lq)[None, :]
    return band_attention(q, kk, vv, rel, None, rel_bias)


def ssd_scan(x, dt, A, Bm, Cm, h0):
    f32 = jnp.float32
    bsz, L = x.shape[:2]
    G, R, P, N = SSM_GROUPS, SSM_HEADS_PER_GROUP, SSM_HEAD_DIM, SSM_STATE
    n_chunks = -(-L // CHUNK)
    pad_len = n_chunks * CHUNK - L

    def to_chunks(t):
        t = jnp.pad(t, [(0, 0), (0, pad_len)] + [(0, 0)] * (t.ndim - 2))
        return t.reshape((bsz, n_chunks, CHUNK) + t.shape[2:])

    x_c = to_chunks(x.astype(f32) * dt[..., None]).reshape(bsz, n_chunks, CHUNK, G, R, P)
    a_c = to_chunks(dt * A).reshape(bsz, n_chunks, CHUNK, G, R)
    B_c = to_chunks(Bm.astype(f32))
    C_c = to_chunks(Cm.astype(f32))
    a_cum = jnp.cumsum(a_c, axis=2)
    causal = jnp.tril(jnp.ones((CHUNK, CHUNK), bool))[:, :, None, None]
    seg = a_cum[:, :, :, None] - a_cum[:, :, None, :]
    Lmat = jnp.exp(jnp.where(causal, seg, -jnp.inf))
    CB = jnp.einsum('bclgn,bcsgn->bclsg', C_c, B_c)
    y_diag = jnp.einsum('bclsg,bclsgr,bcsgrp->bclgrp', CB, Lmat, x_c)
    decay_states = jnp.exp(a_cum[:, :, -1:] - a_cum)
    states = jnp.einsum('bcsgn,bcsgr,bcsgrp->bcgrpn', B_c, decay_states, x_c)
    chunk_decay = jnp.exp(a_cum[:, :, -1])

    def step(h, inp):
        st, dec = inp
        return h * dec[..., None, None] + st, h

    h_init = h0.astype(f32).reshape(bsz, G, R, P, N)
    h_final, prev = lax.scan(step, h_init, (jnp.moveaxis(states, 1, 0), jnp.moveaxis(chunk_decay, 1, 0)))
    prev = jnp.moveaxis(prev, 0, 1)
    y_off = jnp.einsum('bclgn,bcgrpn,bclgr->bclgrp', C_c, prev, jnp.exp(a_cum))
    y = (y_diag + y_off).reshape(bsz, n_chunks * CHUNK, SSM_HEADS, P)[:, :L]
    return y, h_final.reshape(bsz, SSM_HEADS, P, N)


def ssd_mixer(z, xBC, dt_raw, conv_past, h0, conv_w, conv_b, dt_bias, A_log, D_skip, norm_g):
    bsz, L = z.shape[:2]
    xBC, conv_new = causal_dwconv(xBC, conv_past, conv_w, conv_b)
    xBC = jax.nn.silu(xBC)
    xs, Bm, Cm = jnp.split(xBC, [D_SSM, D_SSM + SSM_GROUPS * SSM_STATE], axis=-1)
    xh = xs.reshape(bsz, L, SSM_HEADS, SSM_HEAD_DIM)
    Bm = Bm.reshape(bsz, L, SSM_GROUPS, SSM_STATE)
    Cm = Cm.reshape(bsz, L, SSM_GROUPS, SSM_STATE)
    dt = jax.nn.softplus(dt_raw.astype(jnp.float32) + dt_bias.astype(jnp.float32))
    A = -jnp.exp(A_log.astype(jnp.float32))
    y, h_new = ssd_scan(xh, dt, A, Bm, Cm, h0)
    y = y + xh.astype(jnp.float32) * D_skip.astype(jnp.float32)[:, None]
    y = y.reshape(bsz, L, D_SSM) * jax.nn.silu(z.astype(jnp.float32))
    return rms_norm(y, norm_g).astype(z.dtype), conv_new, h_new


def hybrid_layer(x, k_past, v_past, h0, conv_past, ffn_past,
                 w_in, rel_bias, attn_norm_g, ssm_conv_w, ssm_conv_b, ssm_dt_bias, ssm_A_log,
                 ssm_D, ssm_norm_g, w_out, ln1_g, ln1_b, w_up, ffn_conv_w, ffn_conv_b, w_down,
                 ln2_g, ln2_b):
    bsz, L, _ = x.shape
    if h0 is None:
        h0 = jnp.zeros((bsz, SSM_HEADS, SSM_HEAD_DIM, SSM_STATE), jnp.float32)
        conv_past = jnp.zeros((bsz, SSM_CONV - 1, SSM_CONV_DIM), x.dtype)
        ffn_past = jnp.zeros((bsz, FFN_CONV - 1, 2 * D_FF), x.dtype)
    proj = x @ w_in
    q, k, v, z, xBC, dt_raw = jnp.split(proj, SPLITS, axis=-1)
    q = q.reshape(bsz, L, ATTN_HEADS, ATTN_HEAD_DIM)
    k = k.reshape(bsz, L, ATTN_HEADS, ATTN_HEAD_DIM)
    v = v.reshape(bsz, L, ATTN_HEADS, ATTN_HEAD_DIM)
    if k_past is None:
        a = attn_prompt(q, k, v, rel_bias)
        n_keep = min(WINDOW, L)
        k_new, v_new = k[:, L - n_keep:], v[:, L - n_keep:]
    else:
        a = attn_sample(q, k, v, k_past, v_past, rel_bias)
        k_new, v_new = k, v
    a = rms_norm(a.reshape(bsz, L, D_ATTN), attn_norm_g)
    s, conv_new, h_new = ssd_mixer(z, xBC, dt_raw, conv_past, h0, ssm_conv_w, ssm_conv_b,
                                   ssm_dt_bias, ssm_A_log, ssm_D, ssm_norm_g)
    mix = jnp.concatenate([a, s], axis=-1) @ w_out
    x = layer_norm(ALPHA * x + mix, ln1_g, ln1_b)
    h, ffn_new = causal_dwconv(x @ w_up, ffn_past, ffn_conv_w, ffn_conv_b)
    hv, hg = jnp.split(h, [D_FF], axis=-1)
    f = (hv * jax.nn.silu(hg)) @ w_down
    x = layer_norm(ALPHA * x + f, ln2_g, ln2_b)
    return x, (k_new, v_new, h_new, conv_new, ffn_new)


def setup_inputs(seed: int = 0) -> dict:
    key = jax.random.key(seed)
    ks = jax.random.split(key, 26)
    f32 = jnp.float32

    def nrm(k, shape, scale):
        return jax.random.normal(k, shape, f32) * scale

    n_cache = min(WINDOW, PAST_LEN)
    dt0 = jnp.exp(jax.random.uniform(ks[9], (DEPTH, SSM_HEADS), f32, math.log(1e-3), math.log(1e-1)))
    return {
        'x_prompt': nrm(ks[0], (BATCH, SEQ, D_MODEL), 1.0),
        'x_sample': nrm(ks[1], (DEC_BATCH, DEC_SEQ, D_MODEL), 1.0),
        'cache_attn_k': nrm(ks[2], (DEPTH, DEC_BATCH, n_cache, ATTN_HEADS, ATTN_HEAD_DIM), 1.0),
        'cache_attn_v': nrm(ks[3], (DEPTH, DEC_BATCH, n_cache, ATTN_HEADS, ATTN_HEAD_DIM), 1.0),
        'state_ssm': nrm(ks[4], (DEPTH, DEC_BATCH, SSM_HEADS, SSM_HEAD_DIM, SSM_STATE), 0.5),
        'state_ssm_conv': nrm(ks[5], (DEPTH, DEC_BATCH, SSM_CONV - 1, SSM_CONV_DIM), 1.0),
        'state_ffn_conv': nrm(ks[6], (DEPTH, DEC_BATCH, FFN_CONV - 1, 2 * D_FF), 1.0),
        'w_in': nrm(ks[7], (DEPTH, D_MODEL, D_IN_PROJ), D_MODEL ** -0.5),
        'rel_bias': nrm(ks[8], (DEPTH, ATTN_HEADS, N_REL), 0.5),
        'attn_norm_g': 1.0 + nrm(ks[10], (DEPTH, D_ATTN), 0.02),
        'ssm_conv_w': nrm(ks[11], (DEPTH, SSM_CONV, SSM_CONV_DIM), SSM_CONV ** -0.5),
        'ssm_conv_b': nrm(ks[12], (DEPTH, SSM_CONV_DIM), 0.02),
        'ssm_dt_bias': dt0 + jnp.log(-jnp.expm1(-dt0)),
        'ssm_A_log': jnp.log(jax.random.uniform(ks[13], (DEPTH, SSM_HEADS), f32, 1.0, 16.0)),
        'ssm_D': 1.0 + nrm(ks[14], (DEPTH, SSM_HEADS), 0.02),
        'ssm_norm_g': 1.0 + nrm(ks[15], (DEPTH, D_SSM), 0.02),
        'w_out': nrm(ks[16], (DEPTH, D_MIX, D_MODEL), D_MIX ** -0.5 * BETA_INIT),
        'ln1_g': 1.0 + nrm(ks[17], (DEPTH, D_MODEL), 0.02),
        'ln1_b': nrm(ks[18], (DEPTH, D_MODEL), 0.02),
        'w_up': nrm(ks[19], (DEPTH, D_MODEL, 2 * D_FF), D_MODEL ** -0.5),
        'ffn_conv_w': nrm(ks[20], (DEPTH, FFN_CONV, 2 * D_FF), FFN_CONV ** -0.5),
        'ffn_conv_b': nrm(ks[21], (DEPTH, 2 * D_FF), 0.02),
        'w_down': nrm(ks[22], (DEPTH, D_FF, D_MODEL), D_FF ** -0.5 * BETA_INIT),
        'ln2_g': 1.0 + nrm(ks[23], (DEPTH, D_MODEL), 0.02),
        'ln2_b': nrm(ks[24], (DEPTH, D_MODEL), 0.02),
    }


def reference(x_prompt, x_sample, cache_attn_k, cache_attn_v, state_ssm, state_ssm_conv, state_ffn_conv,
              w_in, rel_bias, attn_norm_g, ssm_conv_w, ssm_conv_b, ssm_dt_bias, ssm_A_log, ssm_D,
              ssm_norm_g, w_out, ln1_g, ln1_b, w_up, ffn_conv_w, ffn_conv_b, w_down, ln2_g, ln2_b):
    yp, ys = x_prompt, x_sample
    st_p, st_s = [], []
    for l in range(DEPTH):
        params = (w_in[l], rel_bias[l], attn_norm_g[l], ssm_conv_w[l], ssm_conv_b[l], ssm_dt_bias[l],
                  ssm_A_log[l], ssm_D[l], ssm_norm_g[l], w_out[l], ln1_g[l], ln1_b[l], w_up[l],
                  ffn_conv_w[l], ffn_conv_b[l], w_down[l], ln2_g[l], ln2_b[l])
        yp, sp = hybrid_layer(yp, None, None, None, None, None, *params)
        ys, ss = hybrid_layer(ys, cache_attn_k[l], cache_attn_v[l], state_ssm[l], state_ssm_conv[l],
                              state_ffn_conv[l], *params)
        st_p.append(sp)
        st_s.append(ss)
    new_k_prompt = jnp.stack([s[0] for s in st_p])
    new_v_prompt = jnp.stack([s[1] for s in st_p])
    new_k_sample = jnp.stack([s[0] for s in st_s])
    new_v_sample = jnp.stack([s[1] for s in st_s])
    new_ssm_prompt = jnp.stack([s[2] for s in st_p])
    new_ssm_sample = jnp.stack([s[2] for s in st_s])
    new_ssm_conv_prompt = jnp.stack([s[3] for s in st_p])
    new_ssm_conv_sample = jnp.stack([s[3] for s in st_s])
    new_ffn_conv_prompt = jnp.stack([s[4] for s in st_p])
    new_ffn_conv_sample = jnp.stack([s[4] for s in st_s])
    return (yp, ys, new_k_prompt, new_v_prompt, new_k_sample, new_v_sample,
            new_ssm_prompt, new_ssm_sample, new_ssm_conv_prompt, new_ssm_conv_sample,
            new_ffn_conv_prompt, new_ffn_conv_sample)
```

```python
import numpy as np
from contextlib import ExitStack
import concourse.bass as bass
import concourse.mybir as mybir
from concourse.bass_utils import run_bass_kernel_spmd

F32 = mybir.dt.float32
BF16 = mybir.dt.bfloat16
AF = mybir.ActivationFunctionType
ALU = mybir.AluOpType

D = 1024
KT = 8
NPROJ = 3080
DFF = 2688
NF = 21
C_Q, C_K, C_V, C_Z, C_XBC, C_DT = 0, 512, 1024, 1536, 2048, 3072
ALPHA = 2.0 ** 0.25
EPS = 1e-5
NBLK = 64
B_KV = 27
B_FULL = 31
B_MAIN = 32
CP_AG, CP_SG, CP_CW, CP_CB, CP_FW, CP_FB, CP_N = 0, 4, 8, 40, 48, 174, 216


class Sched:
    def __init__(self, nc, es, ndma=64):
        self.nc = nc
        self.engs = {'pe': nc.tensor, 'act': nc.scalar, 'dve': nc.vector, 'pool': nc.gpsimd, 'sp': nc.sync}
        self.sem = {k: es.enter_context(nc.semaphore('s_' + k)) for k in self.engs}
        self.cnt = {k: 0 for k in self.engs}
        self.seen = {k: {} for k in self.engs}
        self.dsem = [es.enter_context(nc.semaphore('d%d' % i)) for i in range(ndma)]
        self.dval = [0] * ndma
        self.dnext = 0
        self.npool = 16
        self.pool_used = 0
        self.lastw = {}
        self.readers = {}
        self.nwait = 0
        self.pre_barrier = None

    def _wait(self, e, tok):
        kind, src, val = tok
        if kind == 'e' and src == e and e in ('pe', 'sp'):
            return
        key = (kind, src)
        if self.seen[e].get(key, 0) >= val:
            return
        sem = self.sem[src] if kind == 'e' else self.dsem[src]
        self.engs[e].wait_ge(sem, val)
        self.seen[e][key] = val
        self.nwait += 1

    def _deps(self, e, reads, writes):
        for k in reads:
            t = self.lastw.get(k)
            if t is not None:
                self._wait(e, t)
            if isinstance(k, tuple) and k[0] == 'bk':
                for (kind, src), val in self.readers.get(k, {}).items():
                    if src != e:
                        self._wait(e, (kind, src, val))
        for k in writes:
            t = self.lastw.get(k)
            if t is not None:
                self._wait(e, t)
            for (kind, src), val in self.readers.get(k, {}).items():
                self._wait(e, (kind, src, val))

    def _commit(self, tok, reads, writes):
        kind, src, val = tok
        for k in reads:
            d = self.readers.setdefault(k, {})
            if d.get((kind, src), 0) < val:
                d[(kind, src)] = val
        for k in writes:
            self.lastw[k] = tok
            self.readers[k] = {}

    def op(self, e, fn, reads=(), writes=(), signal=True):
        self._deps(e, reads, writes)
        ins = fn(self.engs[e])
        if signal:
            self.cnt[e] += 1
            ins.then_inc(self.sem[e], 1)
            tok = ('e', e, self.cnt[e])
        else:
            tok = ('e', e, self.cnt[e] + 1)
        self._commit(tok, reads, writes)

    def dma(self, q, out, in_, reads=(), writes=(), **kw):
        if q == 'pool':
            i = len(self.dsem) - 1 - self.pool_used
            self.pool_used += 1
            assert self.pool_used <= self.npool and self.dval[i] == 0
        else:
            i = self.dnext
            self.dnext = (self.dnext + 1) % (len(self.dsem) - self.npool)
        if self.dval[i] > 0:
            self._wait(q, ('d', i, self.dval[i]))
        self._deps(q, reads, writes)
        self.dval[i] += 16
        self.engs[q].dma_start(out=out, in_=in_, **kw).then_inc(self.dsem[i], 16)
        self._commit(('d', i, self.dval[i]), reads, writes)

    def barrier(self, skip_pool_dma=False):
        if self.pre_barrier is not None:
            self.pre_barrier()
        nhw = len(self.dsem) - self.npool
        for i, v in enumerate(self.dval):
            if v > 0 and not (skip_pool_dma and i >= nhw):
                self._wait('sp', ('d', i, v))
        self.cnt['sp'] += 1
        self.engs['sp'].sem_inc(self.sem['sp'], 1)
        ce = ('pe', 'act', 'dve', 'pool')
        for e in ce:
            for f in ce + ('sp',):
                if self.cnt[f] > 0:
                    self._wait(e, ('e', f, self.cnt[f]))
        for f in ce:
            if self.cnt[f] > 0:
                self._wait('sp', ('e', f, self.cnt[f]))
        keep = {}
        if skip_pool_dma:
            keep = {k: t for k, t in self.lastw.items() if t[0] == 'd' and t[1] >= nhw}
        self.lastw.clear()
        self.lastw.update(keep)
        self.readers.clear()

    def finish(self):
        for i, v in enumerate(self.dval):
            if v > 0:
                self._wait('sp', ('d', i, v))


class NS:
    pass


def build_program(do_sample=True, nblk=NBLK, do_prompt=True, dbg=False, stage=None):
    nc = bass.Bass("TRN2", target_bir_lowering=False)

    def din(name, shape, dt=F32):
        return nc.dram_tensor(name, list(shape), dt, kind="ExternalInput").ap()

    def dout(name, shape, dt=F32):
        return nc.dram_tensor(name, list(shape), dt, kind="ExternalOutput").ap()

    def dint(name, shape, dt):
        return nc.dram_tensor(name, list(shape), dt, kind="Internal").ap()

    xs = din("xs", [NBLK * 128, D])
    flag = din("flag", [128, 1])
    xq = din("xq", [64, D])
    ck = din("ck", [2, 512, 512])
    cv = din("cv", [2, 512, 512])
    hs0 = din("hs0", [2, 512, 128])
    cs0 = din("cs0", [6, D])
    fs0 = din("fs0", [4, 2 * DFF])
    w_in = din("w_in", [D, NPROJ])
    w_out = din("w_out", [D, D])
    w_up = din("w_up", [D, 2 * DFF])
    w_down = din("w_down", [DFF, D])
    biasT_d = din("biasT", [128, 2, 8, 128])
    biasN_d = din("biasN", [64, 8, 32])
    cbias_d = din("cbias", [1, 8])
    colp_d = din("colp", [128, CP_N])
    rowp_d = din("rowp", [1, 24])
    lnp_d = din("lnp", [1, 4 * D])
    consts_d = din("consts", [128, 8, 128])

    y_o = dout("y_o", [4096, D])
    ys_o = dout("ys_o", [64, D])
    nk_o = dout("nk_o", [512, 512])
    nv_o = dout("nv_o", [512, 512])
    nks_o = dout("nks_o", [64, 512])
    nvs_o = dout("nvs_o", [64, 512])
    ssm_o = dout("ssm_o", [512, 128])
    ssms_o = dout("ssms_o", [2, 512, 128])
    sc_o = dout("sc_o", [3, D])
    scs_o = dout("scs_o", [6, D])
    fc_o = dout("fc_o", [2, 2 * DFF])
    fcs_o = dout("fcs_o", [4, 2 * DFF])

    dbg_o = dout("dbg_o", [128, 16, D]) if dbg else None

    def dump(slot, ap, npart, w, key):
        if dbg:
            sch.dma('sp', dbg_o[0:npart, slot, 0:w], ap, reads=[key])

    wb_in = dint("wb_in", [D, NPROJ], BF16)
    wb_out = dint("wb_out", [D, D], BF16)
    wb_up = dint("wb_up", [D, 2 * DFF], BF16)
    wb_down = dint("wb_down", [DFF, D], BF16)

    es = ExitStack()
    with es:
        sch = Sched(nc, es)

        uid = {'n': 0}

        def sbuf(stack, name, shape, dt=F32):
            uid['n'] += 1
            return stack.enter_context(nc.sbuf_tensor("s%d_%s" % (uid['n'], name), list(shape), dt))

        banks = [es.enter_context(nc.psum_tensor("bk%d" % i, [128, 512], F32)) for i in range(8)]
        bstate = {'n': 0}

        def bank():
            nb_ = bstate.get('nb', 6)
            i = bstate['n'] % nb_
            bstate['n'] = (i + 1) % nb_
            return banks[i], ('bk', i)

        P = NS()
        P.Wi = sbuf(es, "Wi", [128, KT, NPROJ], BF16)
        P.Wo = sbuf(es, "Wo", [128, KT, D], BF16)
        P.consts = sbuf(es, "consts", [128, 8, 128])
        P.biasT = sbuf(es, "biasT", [128, 2, 8, 128])
        P.cbias = sbuf(es, "cbias", [128, 8])
        P.colp = sbuf(es, "colp", [128, CP_N])
        P.rowp = sbuf(es, "rowp", [128, 24])
        P.Aneg = sbuf(es, "Aneg", [128, 8])
        P.lnp = sbuf(es, "lnp", [128, 4, D])
        P.flg = sbuf(es, "flg", [128, 1])
        P.mhalf = sbuf(es, "mhalf", [128, 1])
        NKR = 5
        P.X1S = sbuf(es, "X1S", [128, 4, D])
        P.X1T = sbuf(es, "X1T", [128, KT, 512], BF16)
        P.XBC = sbuf(es, "XBC", [128, 8, 131])
        P.SMALL = sbuf(es, "SMALL", [128, 16])
        P.STAT = sbuf(es, "STAT", [128, 2, 6])
        P.MV = sbuf(es, "MV", [128, 2])
        P.TT = sbuf(es, "TT", [128, 128])
        P.TAIL = sbuf(es, "TAIL", [128, 96])
        ident = P.consts[:, 0, :]
        cT, cU, cOnes, cT2, cU2 = (P.consts[:, i, :] for i in (1, 2, 3, 4, 5))
        Wi, Wo, colp, rowp, lnp, flg, mhalf, SMALL, STAT, MV = P.Wi, P.Wo, P.colp, P.rowp, P.lnp, P.flg, P.mhalf, P.SMALL, P.STAT, P.MV

        def bc(ap, shape, axis):
            return ap.unsqueeze(axis).to_broadcast(list(shape))

        def mixer_bufs(st):
            M = NS()
            M.X = [sbuf(st, "X%d" % i, [128, D]) for i in range(2)]
            M.XT = [sbuf(st, "XT%d" % i, [128, KT, 128], BF16) for i in range(2)]
            M.ACC = sbuf(st, "ACC", [128, 8, 128])
            M.SIL = sbuf(st, "SIL", [128, 8, 128])
            M.RHSA = sbuf(st, "RHSA", [128, 8, 128])
            M.BCT = sbuf(st, "BCT", [128, 4, 128], BF16)
            M.BK = sbuf(st, "BK", [128, 2, 128], BF16)
            M.QT = sbuf(st, "QT", [128, 4, 128], BF16)
            M.SZ = sbuf(st, "SZ", [128, 512])
            M.PT = [sbuf(st, "PT%d" % i, [128, 5, 128], BF16) for i in range(2)]
            M.SB = [sbuf(st, "SB%d" % i, [128, 2, 128]) for i in range(2)]
            M.DTA = sbuf(st, "DTA", [128, 4, 8])
            M.LM = sbuf(st, "LM", [128, 8, 128])
            M.EE = sbuf(st, "EE", [128, 8])
            M.CD = sbuf(st, "CD", [128, 2, 8])
            M.WDS = sbuf(st, "WDS", [128, 8])
            M.CBM = sbuf(st, "CBM", [128, 2, 128])
            M.MT = sbuf(st, "MT", [128, 8, 128], BF16)
            M.XSK = sbuf(st, "XSK", [128, 512])
            M.XDT = sbuf(st, "XDT", [128, 8, 64], BF16)
            M.XDD = sbuf(st, "XDD", [128, 8, 64], BF16)
            M.T1 = sbuf(st, "T1", [128, 512])
            M.T2 = sbuf(st, "T2", [128, 512])
            M.YY = sbuf(st, "YY", [128, 512])
            M.JB = sbuf(st, "JB", [128, 512], BF16)
            M.AST = sbuf(st, "AST", [128, KT, 128], BF16)
            for i in range(2):
                sch.op('pool', lambda e, i=i: e.memset(M.PT[i][:], 0.0), writes=[('PT', i)])
            return M

        def ffn_bufs(st):
            Fb = NS()
            Fb.Wup = [sbuf(st, "Wup%d" % i, [128, KT, 256], BF16) for i in range(3)]
            Fb.Wdn = [sbuf(st, "Wdn%d" % i, [128, NF, 256], BF16) for i in range(2)]
            Fb.GT = sbuf(st, "GT", [128, NF, 512], BF16)
            Fb.HEAD = [sbuf(st, "HEAD%d" % i, [128, 8]) for i in range(4)]
            Fb.FACC = [sbuf(st, "FACC%d" % i, [128, 512]) for i in range(4)]
            return Fb

        def cast_rows(dst, src, nrows, split, key, nparts, extra_reads=()):
            nb = nrows // 128
            per = (nb + nparts - 1) // nparts
            r = 0
            while r < nb:
                n = min(per, nb - r)
                s_ = src[r * 128:(r + n) * 128, :].rearrange("r (a b) -> r a b", a=split)
                d_ = dst[r * 128:(r + n) * 128, :].rearrange("r (a b) -> r a b", a=split)
                sch.dma('pool', d_, s_, reads=list(extra_reads), writes=[(key, i) for i in range(r, r + n)])
                r += n

        sch.dma('sp', P.consts[:], consts_d[:, :, :], writes=['consts'])
        sch.dma('sp', colp[:], colp_d[:, :], writes=['colp'])
        sch.dma('sp', flg[:], flag[:, :], writes=['flg'])
        sch.dma('sp', rowp[:], bass.AP(tensor=rowp_d.tensor, offset=0, ap=[[0, 128], [1, 24]]), writes=['rowp'])
        sch.dma('sp', P.cbias[:], bass.AP(tensor=cbias_d.tensor, offset=0, ap=[[0, 128], [1, 8]]), writes=['cbias'])
        with ExitStack() as st0:
            stg = [sbuf(st0, "WSTG%d" % i, [128, NPROJ]) for i in range(3)]
            for kt in range(KT):
                si = kt % 3
                sch.dma('sp', stg[si][:], w_in[kt * 128:(kt + 1) * 128, :], writes=[('WSTG', si)])
                if kt % 2 == 0:
                    sch.op('act', lambda e, kt=kt, si=si: e.activation(out=Wi[:, kt, :], in_=stg[si][:], func=AF.Copy), reads=[('WSTG', si)], writes=[('Wi', kt)])
                else:
                    sch.op('dve', lambda e, kt=kt, si=si: e.tensor_copy(out=Wi[:, kt, :], in_=stg[si][:]), reads=[('WSTG', si)], writes=[('Wi', kt)])
            sch.dma('sp', lnp[:].rearrange("p a b -> p (a b)"),
                    bass.AP(tensor=lnp_d.tensor, offset=0, ap=[[0, 128], [1, 4 * D]]), writes=['lnp'])
            sch.dma('sp', P.biasT[:], biasT_d[:, :, :, :], writes=['biasT'])
            for k4 in range(0, KT, 4):
                sch.dma('pool', Wo[:, k4:k4 + 4, :], w_out[k4 * 128:(k4 + 4) * 128, :].rearrange("(k p) n -> p k n", p=128),
                        reads=[('WSTG', (KT - 1) % 3)], writes=[('Wo', k4 + i_) for i_ in range(4)])
            cast_rows(wb_up, w_up, D, 6, 'wb_up', 4, extra_reads=[('WSTG', (KT - 1) % 3)])
            cast_rows(wb_down, w_down, DFF, 1, 'wb_down', 3, extra_reads=[('WSTG', (KT - 1) % 3)])
            sch.barrier(skip_pool_dma=True)
        WUPK = [('wb_up', r) for r in range(8)]
        WDNK = [('wb_down', r) for r in range(NF)]

        sch.op('act', lambda e: e.activation(out=P.Aneg[:], in_=rowp[:, 8:16], func=AF.Exp), reads=['rowp'], writes=['Aneg'])
        sch.op('dve', lambda e: e.tensor_scalar(out=P.Aneg[:], in0=P.Aneg[:], scalar1=-1.0, scalar2=None, op0=ALU.mult),
               reads=['Aneg'], writes=['Aneg'])
        sch.op('dve', lambda e: e.memset(mhalf[:], -0.5), writes=['mhalf'])
        sch.op('dve', lambda e: e.memset(P.XBC[:], 0.0), writes=['XBC'])

        if stage == 'prologue':
            sch.finish()
            return nc

        def transpose_to(dst_fn, src_fn, nparts, ntiles, reads, writes, post=None, engs=('act', 'dve')):
            j = 0
            gi = 0
            while j < ntiles:
                n = min(4, ntiles - j)
                bk, bkk = bank()
                for q in range(n):
                    sch.op('pe', lambda e, q=q, j=j, bk=bk: e.transpose(out=bk[:, q * nparts:(q + 1) * nparts],
                                                                        in_=src_fn(j + q), identity=ident[0:nparts, 0:nparts]),
                           reads=list(reads) + ['consts'], writes=[bkk], signal=(q == n - 1))
                src = bk[:, 0:n * nparts].rearrange("p (a b) -> p a b", a=n)
                if post is not None:
                    post(j, n, src, bkk)
                elif engs[gi % 2] == 'act':
                    sch.op('act', lambda e, j=j, n=n, src=src: e.activation(out=dst_fn(j, n), in_=src, func=AF.Copy),
                           reads=[bkk], writes=writes)
                else:
                    sch.op('dve', lambda e, j=j, n=n, src=src: e.tensor_copy(out=dst_fn(j, n), in_=src),
                           reads=[bkk], writes=writes)
                j += n
                gi += 1

        def rstd_from_sumsq(ss_ap, n_feat, npart, key):
            sch.op('pool', lambda e: e.tensor_scalar(out=ss_ap, in0=ss_ap, scalar1=1.0 / n_feat, scalar2=EPS,
                                                     op0=ALU.mult, op1=ALU.add), reads=[key], writes=[key])
            sch.op('pool', lambda e: e.tensor_tensor(out=ss_ap, in0=ss_ap, in1=mhalf[0:npart, :], op=ALU.pow),
                   reads=[key, 'mhalf'], writes=[key])

        def layer_norm(src_ap, dst_ap, NT, gi, src_key, dst_key):
            for c in range(2):
                sch.op('dve', lambda e, c=c: e.bn_stats(out=STAT[0:NT, c, :], in_=src_ap[:, c * 512:(c + 1) * 512]),
                       reads=[src_key], writes=['STAT'])
            sch.op('dve', lambda e: e.bn_aggr(out=MV[0:NT, :], in_=STAT[0:NT, :, :].rearrange("p a b -> p (a b)")),
                   reads=['STAT'], writes=['MV'])
            sch.op('pool', lambda e: e.tensor_scalar(out=MV[0:NT, 1:2], in0=MV[0:NT, 1:2], scalar1=1.0, scalar2=EPS,
                                                     op0=ALU.mult, op1=ALU.add), reads=['MV'], writes=['MV'])
            sch.op('pool', lambda e: e.tensor_tensor(out=MV[0:NT, 1:2], in0=MV[0:NT, 1:2], in1=mhalf[0:NT, :], op=ALU.pow),
                   reads=['MV', 'mhalf'], writes=['MV'])
            sch.op('dve', lambda e: e.tensor_scalar(out=dst_ap, in0=src_ap, scalar1=MV[0:NT, 0:1], scalar2=MV[0:NT, 1:2],
                                                    op0=ALU.subtract, op1=ALU.mult), reads=[src_key, 'MV'], writes=[dst_key])
            for hf_, eng in ((0, 'dve'), (1, 'pool')):
                cs_ = slice(hf_ * 512, (hf_ + 1) * 512)
                sch.op(eng, lambda e, cs_=cs_: e.tensor_tensor(out=dst_ap[:, cs_], in0=dst_ap[:, cs_], in1=lnp[0:NT, gi, cs_], op=ALU.mult),
                       reads=[dst_key, 'lnp'], writes=[(dst_key, hf_)])
                sch.op(eng, lambda e, cs_=cs_: e.tensor_tensor(out=dst_ap[:, cs_], in0=dst_ap[:, cs_], in1=lnp[0:NT, gi + 1, cs_], op=ALU.add),
                       reads=[(dst_key, hf_), 'lnp'], writes=[(dst_key, hf_)])

        deferred = []

        def flush_deferred():
            while deferred:
                deferred.pop(0)()

        sch.pre_barrier = flush_deferred

        def mixer_block(M, kind, NT, nseg, L, x_src, xi, xbc_t, xbc_key, HFs, attn_tiles, outs, kv_dst, j_in_super):
            bstate['nb'] = 6
            full = kind == 'full'
            Xb = M.X[xi]
            XTb = M.XT[xi]
            xk, xtk = ('X', xi), ('XT', xi)
            Tm = cT if nseg == 1 else cT2
            Um = cU if nseg == 1 else cU2
            DTA, LM, SIL, ACC, BCT = M.DTA, M.LM, M.SIL, M.ACC, M.BCT
            sch.dma('sp', Xb[0:NT, :], x_src, writes=[xk])
            transpose_to(lambda j, n: XTb[:, j:j + n, 0:NT], lambda j: Xb[0:NT, j * 128:(j + 1) * 128], NT, KT,
                         reads=[xk], writes=[xtk], engs=('act', 'act'))

            def proj_fm(c0, ntiles, evac):
                j = 0
                while j < ntiles:
                    n = min(4, ntiles - j)
                    bk, bkk = bank()
                    for q in range(n):
                        cc = c0 + (j + q) * 128
                        for kt in range(KT):
                            sch.op('pe', lambda e, q=q, kt=kt, cc=cc, bk=bk: e.matmul(bk[:, q * NT:(q + 1) * NT], lhsT=Wi[:, kt, cc:cc + 128],
                                                                                       rhs=XTb[:, kt, 0:NT], start=(kt == 0), stop=(kt == KT - 1)),
                                   reads=[xtk, ('Wi', kt)], writes=[bkk], signal=(q == n - 1 and kt == KT - 1))
                    evac(bk, bkk, j, n)
                    j += n

            def proj_tm(c0, ncols, bk, bkk):
                for kt in range(KT):
                    sch.op('pe', lambda e, kt=kt: e.matmul(bk[0:NT, 0:ncols], lhsT=XTb[:, kt, 0:NT], rhs=Wi[:, kt, c0:c0 + ncols],
                                                           start=(kt == 0), stop=(kt == KT - 1)),
                           reads=[xtk, ('Wi', kt)], writes=[bkk], signal=(kt == KT - 1))

            bkd, bkdk = bank()
            proj_tm(C_DT, 8, bkd, bkdk)
            u_, au, dt_, a_ = (DTA[0:NT, i, :] for i in range(4))
            sch.op('dve', lambda e: e.tensor_tensor(out=u_, in0=bkd[0:NT, 0:8], in1=rowp[0:NT, 0:8], op=ALU.add),
                   reads=[bkdk, 'rowp'], writes=['DTA'])
            sch.op('act', lambda e: e.activation(out=au, in_=u_, func=AF.Abs), reads=['DTA'], writes=['DTA'])
            sch.op('act', lambda e: e.activation(out=au, in_=au, func=AF.Exp, scale=-1.0), reads=['DTA'], writes=['DTA'])
            sch.op('act', lambda e: e.activation(out=au, in_=au, func=AF.Ln, bias=1.0, scale=1.0), reads=['DTA'], writes=['DTA'])
            sch.op('dve', lambda e: e.scalar_tensor_tensor(out=dt_, in0=u_, scalar=0.0, in1=au, op0=ALU.max, op1=ALU.add),
                   reads=['DTA'], writes=['DTA'])
            sch.op('dve', lambda e: e.tensor_tensor(out=a_, in0=dt_, in1=P.Aneg[0:NT, :], op=ALU.mult), reads=['DTA', 'Aneg'], writes=['DTA'])

            RHSA = M.RHSA
            for h in range(8):
                sch.op('pool', lambda e, h=h: e.tensor_scalar(out=RHSA[0:NT, h, 0:NT], in0=Tm[0:NT, 0:NT], scalar1=DTA[0:NT, 3, h:h + 1],
                                                              scalar2=0.0, op0=ALU.mult, op1=ALU.add), reads=['DTA', 'consts'], writes=[('RHSA', h)])
            def evac_xbc(bk, bkk, j, n):
                src = bk[:, 0:n * NT].rearrange("p (a s l) -> p a s l", a=n, s=nseg)
                sch.op('act', lambda e: e.activation(out=xbc_t[:, j:j + n, :, 3:3 + L], in_=src, func=AF.Copy),
                       reads=[bkk], writes=[xbc_key])
            proj_fm(C_XBC, 8, evac_xbc)

            if outs.get('sconv') is not None:
                nr = 3 * nseg
                tl = P.TAIL[:, 0:8 * nr].rearrange("p (t s r) -> p t s r", t=8, s=nseg)
                sch.op('pool', lambda e: e.tensor_copy(out=tl, in_=xbc_t[:, :, :, L:L + 3]), reads=[xbc_key], writes=['TAIL'])
                bkt, bktk = bank()
                sch.op('pe', lambda e: e.transpose(out=bkt[0:8 * nr, 0:128], in_=P.TAIL[:, 0:8 * nr], identity=ident),
                       reads=['TAIL', 'consts'], writes=[bktk])
                sch.op('act', lambda e: e.activation(out=P.TT[0:8 * nr, :], in_=bkt[0:8 * nr, 0:128], func=AF.Copy),
                       reads=[bktk], writes=['TT'])
                for t in range(8):
                    sch.dma('sp', outs['sconv'][:, t * 128:(t + 1) * 128], P.TT[t * nr:(t + 1) * nr, :], reads=['TT'])

            for jj in (3, 0, 1, 2):
                for t in range(8):
                    cw = CP_CW + 4 * t
                    acc = ACC[:, t, 0:NT].rearrange("p (s l) -> p s l", s=nseg)
                    if jj == 3:
                        sch.op('dve', lambda e, t=t, cw=cw, acc=acc: e.tensor_scalar(out=acc, in0=xbc_t[:, t, :, 3:3 + L], scalar1=colp[:, cw + 3:cw + 4],
                                                                                     scalar2=colp[:, CP_CB + t:CP_CB + t + 1], op0=ALU.mult, op1=ALU.add),
                               reads=[xbc_key, 'colp'], writes=[('ACC', t)])
                    else:
                        sch.op('dve', lambda e, t=t, cw=cw, acc=acc, jj=jj: e.scalar_tensor_tensor(out=acc, in0=xbc_t[:, t, :, jj:jj + L],
                                                                                                  scalar=colp[:, cw + jj:cw + jj + 1], in1=acc,
                                                                                                  op0=ALU.mult, op1=ALU.add),
                               reads=[xbc_key, 'colp', ('ACC', t)], writes=[('ACC', t)])

            if kind != 'ssm':
                kt_dst, kt_key, va_dst, va_key = kv_dst

                def evac_k(bk, bkk, j, n):
                    src = bk[:, 0:n * NT].rearrange("p (a b) -> p a b", a=n)
                    sch.op('dve', lambda e: e.tensor_copy(out=kt_dst(j, n), in_=src), reads=[bkk], writes=[kt_key])
                proj_fm(C_K, 4, evac_k)
                bkv, bkvk = bank()
                proj_tm(C_V, 512, bkv, bkvk)
                sch.op('act', lambda e: e.activation(out=va_dst[:, :, 0:64], in_=bkv[0:NT, :].rearrange("p (h d) -> p h d", h=8),
                                                     func=AF.Copy), reads=[bkvk], writes=[va_key])
                if kind == 'kv':
                    sch.op('pool', lambda e: e.tensor_copy(out=va_dst[:, :, 64], in_=flg[0:NT, 0:1].to_broadcast([NT, 8])),
                           reads=['flg'], writes=[va_key])
                else:
                    sch.op('pool', lambda e: e.memset(va_dst[:, :, 64], 1.0), writes=[va_key])
                if outs.get('nv') is not None:
                    sch.op('act', lambda e: e.activation(out=M.T1[0:NT, :], in_=bkv[0:NT, :], func=AF.Copy), reads=[bkvk], writes=['T1'])
                    sch.dma('sp', outs['nv'], M.T1[0:NT, :], reads=['T1'])
                if outs.get('nk') is not None:
                    bkk_, bkkk = bank()
                    proj_tm(C_K, 512, bkk_, bkkk)
                    sch.op('act', lambda e: e.activation(out=M.T2[0:NT, :], in_=bkk_[0:NT, :], func=AF.Copy), reads=[bkkk], writes=['T2'])
                    sch.dma('sp', outs['nk'], M.T2[0:NT, :], reads=['T2'])

            if full:
                def evac_q(bk, bkk, j, n):
                    src = bk[:, 0:n * NT].rearrange("p (a b) -> p a b", a=n)
                    sch.op('dve', lambda e: e.tensor_copy(out=M.QT[:, j:j + n, 0:NT], in_=src), reads=[bkk], writes=['QT'])
                proj_fm(C_Q, 4, evac_q)
                bkz, bkzk = bank()
                proj_tm(C_Z, 512, bkz, bkzk)
                sch.op('act', lambda e: e.activation(out=M.SZ[0:NT, :], in_=bkz[0:NT, :], func=AF.Silu), reads=[bkzk], writes=['SZ'])

            step_attn = lambda n: None
            if full:
                ob = [(banks[6], ('bk', 6)), (banks[7], ('bk', 7))]
                units = [(h, s_) for h in range(8) for s_ in range(nseg)]

                def attn_front(ui):
                    h, s = units[ui]
                    hp, po = h // 2, (h % 2) * 64
                    pti = ui % 2
                    PTb, ptk, SBb, sbk = M.PT[pti], ('PT', pti), M.SB[pti], ('SB', pti)
                    tiles = attn_tiles(s, h)
                    ctiles = [t_ for t_ in tiles if t_['bias'] is None]
                    btiles = [t_ for t_ in tiles if t_['bias'] is not None]
                    nc_ = len(ctiles)
                    bkA, bkAk = bank()
                    bkB, bkBk = bank()
                    for lst, bk, bkk in ((ctiles, bkA, bkAk), (btiles, bkB, bkBk)):
                        for ii, t_ in enumerate(lst):
                            nk, pb = t_['nk'], t_['pb']
                            sch.op('pe', lambda e, t_=t_, ii=ii, bk=bk, nk=nk, pb=pb, s=s: e.matmul(
                                bk[pb:pb + nk, ii * L:(ii + 1) * L], lhsT=t_['kt'], rhs=M.QT[po:po + 64, hp, s * L:(s + 1) * L], start=True, stop=True),
                                reads=[t_['ktkey'], 'QT'], writes=[bkk], signal=(ii == len(lst) - 1))
                    sch.op('act', lambda e, bkA=bkA, PTb=PTb, h=h: e.activation(
                        out=PTb[:, 0:nc_, 0:L], in_=bkA[:, 0:nc_ * L].rearrange("p (a l) -> p a l", a=nc_), func=AF.Exp,
                        bias=P.cbias[:, h:h + 1], scale=0.125), reads=[bkAk, 'cbias'], writes=[ptk])
                    for ii, t_ in enumerate(btiles):
                        nk, pb = t_['nk'], t_['pb']
                        sch.op('dve', lambda e, t_=t_, ii=ii, nk=nk, pb=pb, bkB=bkB, SBb=SBb: e.scalar_tensor_tensor(
                            out=SBb[pb:pb + nk, ii, 0:L], in0=bkB[pb:pb + nk, ii * L:(ii + 1) * L], scalar=0.125, in1=t_['bias'],
                            op0=ALU.mult, op1=ALU.add), reads=[bkBk, 'biasT', 'biasN'], writes=[sbk])
                        sch.op('act', lambda e, ii=ii, nk=nk, pb=pb, SBb=SBb, PTb=PTb: e.activation(
                            out=PTb[pb:pb + nk, nc_ + ii, 0:L], in_=SBb[pb:pb + nk, ii, 0:L], func=AF.Exp), reads=[sbk], writes=[ptk])
                    alltiles = ctiles + btiles
                    for ii, t_ in enumerate(alltiles):
                        for (p0, p1, q0, q1) in t_.get('zero', ()):
                            sch.op('pool', lambda e, ii=ii, p0=p0, p1=p1, q0=q0, q1=q1, PTb=PTb: e.memset(PTb[p0:p1, ii, q0:q1], 0.0),
                                   reads=[ptk], writes=[ptk])
                    return alltiles

                def attn_back(ui, alltiles):
                    h, s = units[ui]
                    bo, bok = ob[h // 4]
                    pti = ui % 2
                    PTb, ptk = M.PT[pti], ('PT', pti)
                    for ii, t_ in enumerate(alltiles):
                        nk, pb = t_['nk'], t_['pb']
                        sch.op('pe', lambda e, t_=t_, ii=ii, nk=nk, pb=pb, PTb=PTb, bo=bo, s=s, h=h: e.matmul(
                            bo[s * L:(s + 1) * L, (h % 4) * 65:(h % 4) * 65 + 65], lhsT=PTb[pb:pb + nk, ii, 0:L], rhs=t_['va'],
                            start=(ii == 0), stop=(ii == len(alltiles) - 1)),
                            reads=[ptk, t_['vakey']], writes=[bok], signal=(ii == len(alltiles) - 1))

                def attn_gen():
                    pend = attn_front(0)
                    for ui in range(len(units)):
                        nxt = attn_front(ui + 1) if ui + 1 < len(units) else None
                        attn_back(ui, pend)
                        pend = nxt
                        yield
                agen = [attn_gen()]

                def step_attn(n):
                    for _ in range(n * nseg):
                        try:
                            next(agen[0])
                        except StopIteration:
                            return

            while deferred:
                deferred.pop(0)()
            hpb = 512 // NT
            for g0 in range(0, 8, hpb):
                bk, bkk = bank()
                sch.op('pe', lambda e, bk=bk, g0=g0: e.matmul(bk[0:NT, 0:hpb * NT], lhsT=Um[0:NT, 0:NT],
                                                              rhs=RHSA[0:NT, g0:g0 + hpb, 0:NT], start=True, stop=True),
                       reads=[('RHSA', t_) for t_ in range(8)] + ['consts'], writes=[bkk])
                sch.op('act', lambda e, bk=bk, g0=g0: e.activation(out=LM[0:NT, g0:g0 + hpb, 0:NT],
                                                                  in_=bk[0:NT, 0:hpb * NT].rearrange("p (h l) -> p h l", h=hpb), func=AF.Exp),
                       reads=[bkk], writes=['LM'])
            bke, bkek = bank()
            sch.op('pe', lambda e: e.matmul(bke[0:NT, 0:8], lhsT=Tm[0:NT, 0:NT], rhs=DTA[0:NT, 3, :], start=True, stop=True),
                   reads=['DTA', 'consts'], writes=[bkek], signal=False)
            for s in range(nseg):
                osm = cOnes if nseg == 1 else P.consts[:, 6 + s, :]
                sch.op('pe', lambda e, s=s, osm=osm: e.matmul(bke[:, 8 + 8 * s:16 + 8 * s], lhsT=osm[0:NT, :], rhs=DTA[0:NT, 3, :], start=True, stop=True),
                       reads=['DTA', 'consts'], writes=[bkek], signal=(s == nseg - 1))
            sch.op('act', lambda e: e.activation(out=M.EE[0:NT, :], in_=bke[0:NT, 0:8], func=AF.Exp), reads=[bkek], writes=['EE'])
            sch.op('act', lambda e: e.activation(out=M.CD[:, 0:nseg, :], in_=bke[:, 8:8 + 8 * nseg].rearrange("p (s h) -> p s h", s=nseg), func=AF.Exp),
                   reads=[bkek], writes=['CD'])
            for s in range(nseg):
                sch.op('dve', lambda e, s=s: e.tensor_tensor(out=M.WDS[s * L:(s + 1) * L, :], in0=DTA[s * L:(s + 1) * L, 2, :],
                                                             in1=LM[s * L:(s + 1) * L, :, (s + 1) * L - 1], op=ALU.mult),
                       reads=['DTA', 'LM'], writes=['WDS'])

            sch.op('act', lambda e: e.activation(out=SIL[:, :, 0:NT], in_=ACC[:, :, 0:NT], func=AF.Silu), reads=[('ACC', t_) for t_ in range(8)], writes=['SIL'])
            sch.op('act', lambda e: e.activation(out=BCT[:, :, 0:NT], in_=ACC[:, 4:8, 0:NT], func=AF.Silu), reads=[('ACC', t_) for t_ in range(4, 8)], writes=['BCT'])
            step_attn(2)

            bkx, bkxk = bank()
            for q in range(4):
                sch.op('pe', lambda e, q=q: e.transpose(out=bkx[0:NT, q * 128:(q + 1) * 128], in_=SIL[:, q, 0:NT], identity=ident),
                       reads=['SIL', 'consts'], writes=[bkxk], signal=(q == 3))
            xsv = bkx[0:NT, :].rearrange("p (h d) -> p h d", h=8)
            if full:
                sch.op('act', lambda e: e.activation(out=M.XSK[0:NT, :], in_=bkx[0:NT, :], func=AF.Copy), reads=[bkxk], writes=['XSK'])
                sch.op('dve', lambda e: e.tensor_tensor(out=M.XDT[0:NT, :, :], in0=xsv, in1=bc(DTA[0:NT, 2, :], [NT, 8, 64], 2), op=ALU.mult),
                       reads=[bkxk, 'DTA'], writes=['XDT'])
            sch.op('dve', lambda e: e.tensor_tensor(out=M.XDD[0:NT, :, :], in0=xsv, in1=bc(M.WDS[0:NT, :], [NT, 8, 64], 2), op=ALU.mult),
                   reads=[bkxk, 'WDS'], writes=['XDD'])
            step_attn(1)
            bkb, bkbk = bank()
            for g in range(2):
                sch.op('pe', lambda e, g=g: e.transpose(out=bkb[0:NT, g * 128:(g + 1) * 128], in_=SIL[:, 4 + g, 0:NT], identity=ident),
                       reads=['SIL', 'consts'], writes=[bkbk], signal=(g == 1))
            sch.op('act', lambda e: e.activation(out=M.BK[0:NT, :, :], in_=bkb[0:NT, 0:256].rearrange("p (g n) -> p g n", g=2), func=AF.Copy),
                   reads=[bkbk], writes=['BK'])
            step_attn(1)

            if full:
                bkc, bkck = bank()
                for g in range(2):
                    sch.op('pe', lambda e, g=g: e.matmul(bkc[0:NT, g * NT:(g + 1) * NT], lhsT=BCT[:, g, 0:NT], rhs=BCT[:, 2 + g, 0:NT],
                                                         start=True, stop=True), reads=['BCT'], writes=[bkck], signal=(g == 1))
                sch.op('dve', lambda e: e.tensor_tensor(out=M.CBM[0:NT, :, 0:NT], in0=bkc[0:NT, 0:2 * NT].rearrange("p (g l) -> p g l", g=2),
                                                        in1=bc(Tm[0:NT, 0:NT], [NT, 2, NT], 1), op=ALU.mult),
                       reads=[bkck, 'consts'], writes=['CBM'])
                for g in range(2):
                    sch.op(('dve', 'pool')[g], lambda e, g=g: e.tensor_tensor(out=M.MT[0:NT, 4 * g:4 * g + 4, 0:NT], in0=LM[0:NT, 4 * g:4 * g + 4, 0:NT],
                                                                  in1=bc(M.CBM[0:NT, g, 0:NT], [NT, 4, NT], 1), op=ALU.mult),
                           reads=['LM', 'CBM'], writes=[('MT', g)])
                bko, bkok = bank()
                for s in range(nseg):
                    _, HBs, hk = HFs[s]
                    for g in range(2):
                        sch.op('pe', lambda e, s=s, g=g, HBs=HBs: e.matmul(bko[s * L:(s + 1) * L, g * 256:(g + 1) * 256], lhsT=BCT[:, 2 + g, s * L:(s + 1) * L],
                                                                           rhs=HBs[:, g * 256:(g + 1) * 256], start=True, stop=True),
                               reads=['BCT', ('HB', hk)], writes=[bkok], signal=(s == nseg - 1 and g == 1))
                bky, bkyk = bank()
                for h in range(8):
                    sch.op('pe', lambda e, h=h: e.matmul(bky[0:NT, h * 64:(h + 1) * 64], lhsT=M.MT[0:NT, h, 0:NT], rhs=M.XDT[0:NT, h, :],
                                                         start=True, stop=True), reads=[('MT', h // 4), 'XDT'], writes=[bkyk], signal=(h == 7))

            for s in range(nseg):
                HFt, HBt, hk = HFs[s]
                bks, bksk = bank()
                for g in range(2):
                    sch.op('pe', lambda e, s=s, g=g, bks=bks: e.matmul(bks[:, g * 256:(g + 1) * 256], lhsT=M.BK[s * L:(s + 1) * L, g, :],
                                                                       rhs=M.XDD[s * L:(s + 1) * L, 4 * g:4 * g + 4, :], start=True, stop=True),
                           reads=['BK', 'XDD'], writes=[bksk], signal=(g == 1))
                hv = HFt[:].rearrange("p (h d) -> p h d", h=8)
                sch.op('pool', lambda e, s=s, hv=hv: e.tensor_tensor(out=hv, in0=hv, in1=bc(M.CD[:, s, :], [128, 8, 64], 2), op=ALU.mult),
                       reads=[('HF', hk), 'CD', ('HB', hk)], writes=[('HF', hk)])
                sch.op('dve', lambda e, HFt=HFt, bks=bks: e.tensor_tensor(out=HFt[:], in0=HFt[:], in1=bks[:, 0:512], op=ALU.add),
                       reads=[('HF', hk), bksk], writes=[('HF', hk)])
                if outs.get('flag_state'):
                    sch.op('dve', lambda e, HFt=HFt: e.tensor_scalar(out=HFt[:], in0=HFt[:], scalar1=flg[:, 0:1], scalar2=None, op0=ALU.mult),
                           reads=[('HF', hk), 'flg'], writes=[('HF', hk)])
                sch.op('act', lambda e, HFt=HFt, HBt=HBt: e.activation(out=HBt[:], in_=HFt[:], func=AF.Copy), reads=[('HF', hk)], writes=[('HB', hk)])

            if not full:
                return

            T1, T2, YY, AST = M.T1, M.T2, M.YY, M.AST
            v8 = lambda ap: ap.rearrange("p (h d) -> p h d", h=8)
            sch.op('dve', lambda e: e.tensor_tensor(out=v8(T1[0:NT, :]), in0=v8(bko[0:NT, :]), in1=bc(M.EE[0:NT, :], [NT, 8, 64], 2), op=ALU.mult),
                   reads=[bkok, 'EE'], writes=['T1'])
            sch.op('pool', lambda e: e.tensor_tensor(out=v8(T2[0:NT, :]), in0=v8(M.XSK[0:NT, :]), in1=bc(rowp[0:NT, 16:24], [NT, 8, 64], 2), op=ALU.mult),
                   reads=['XSK', 'rowp'], writes=['T2'])
            sch.op('dve', lambda e: e.tensor_tensor(out=T2[0:NT, :], in0=T2[0:NT, :], in1=T1[0:NT, :], op=ALU.add), reads=['T1', 'T2'], writes=['T2'])
            sch.op('dve', lambda e: e.tensor_tensor(out=YY[0:NT, :], in0=bky[0:NT, :], in1=T2[0:NT, :], op=ALU.add), reads=[bkyk, 'T2'], writes=['YY'])
            dump(0, YY[0:NT, :], NT, 512, 'YY')
            dump(1, M.SZ[0:NT, :], NT, 512, 'SZ')
            dump(5, T1[0:NT, :], NT, 512, 'T1')
            dump(6, M.XSK[0:NT, :], NT, 512, 'XSK')
            dump(7, M.EE[0:NT, :], NT, 8, 'EE')
            dump(8, M.DTA[0:NT, :, :].rearrange("p a b -> p (a b)"), NT, 32, 'DTA')
            sch.op('dve', lambda e: e.tensor_tensor(out=YY[0:NT, :], in0=YY[0:NT, :], in1=M.SZ[0:NT, :], op=ALU.mult), reads=['YY', 'SZ'], writes=['YY'])
            step_attn(2)
            sch.op('act', lambda e: e.activation(out=M.JB[0:NT, :], in_=YY[0:NT, :], func=AF.Square, accum_out=SMALL[0:NT, 0:1]),
                   reads=['YY'], writes=['JB', ('SM', 0)])
            rstd_from_sumsq(SMALL[0:NT, 0:1], 512.0, NT, ('SM', 0))
            sch.op('dve', lambda e: e.tensor_scalar(out=YY[0:NT, :], in0=YY[0:NT, :], scalar1=SMALL[0:NT, 0:1], scalar2=None, op0=ALU.mult),
                   reads=['YY', ('SM', 0)], writes=['YY'])

            def post_s(j, n, src, bkk):
                sch.op('dve', lambda e: e.tensor_tensor(out=AST[:, 4:8, 0:NT], in0=src, in1=bc(colp[:, CP_SG:CP_SG + 4], [128, 4, NT], 2), op=ALU.mult),
                       reads=[bkk, 'colp'], writes=['AST'])
            transpose_to(None, lambda j: YY[0:NT, j * 128:(j + 1) * 128], NT, 4, reads=['YY'], writes=['AST'], post=post_s)

            step_attn(1000)
            AA = T1
            AA3 = AA[0:NT, :].rearrange("p (h d) -> p h d", h=8)
            if dbg:
                sch.op('act', lambda e: e.activation(out=T2[0:NT, 0:260], in_=ob[0][0][0:NT, 0:260], func=AF.Copy), reads=[ob[0][1]], writes=['T2'])
                dump(9, T2[0:NT, 0:260], NT, 260, 'T2')
                sch.op('act', lambda e: e.activation(out=T2[:, 0:320].rearrange("p (a b) -> p a b", a=5), in_=M.PT[1][:, :, 0:64], func=AF.Copy), reads=[('PT', 1)], writes=['T2'])
                dump(10, T2[:, 0:320], 128, 320, 'T2')
            for hh in range(2):
                bo, bok = ob[hh]
                ov = bo[0:NT, 0:260].rearrange("p (h d) -> p h d", h=4)
                sch.op('dve', lambda e, ov=ov, hh=hh: e.reciprocal(out=SMALL[0:NT, 8 + 4 * hh:12 + 4 * hh], in_=ov[:, :, 64]),
                       reads=[bok], writes=[('SM', 1 + hh)])
                sch.op('dve', lambda e, ov=ov, hh=hh: e.tensor_tensor(out=AA3[:, 4 * hh:4 * hh + 4, :], in0=ov[:, :, 0:64],
                                                                      in1=bc(SMALL[0:NT, 8 + 4 * hh:12 + 4 * hh], [NT, 4, 64], 2), op=ALU.mult),
                       reads=[bok, ('SM', 1 + hh)], writes=['T1'])
            AAf = AA[0:NT, :]
            dump(2, AAf, NT, 512, 'T1')
            sch.op('act', lambda e: e.activation(out=M.JB[0:NT, :], in_=AAf, func=AF.Square, accum_out=SMALL[0:NT, 1:2]),
                   reads=['T1'], writes=['JB', ('SM', 3)])
            rstd_from_sumsq(SMALL[0:NT, 1:2], 512.0, NT, ('SM', 3))
            sch.op('dve', lambda e: e.tensor_scalar(out=AAf, in0=AAf, scalar1=SMALL[0:NT, 1:2], scalar2=None, op0=ALU.mult),
                   reads=['T1', ('SM', 3)], writes=['T1'])

            def post_a(j, n, src, bkk):
                sch.op('dve', lambda e: e.tensor_tensor(out=AST[:, 0:4, 0:NT], in0=src, in1=bc(colp[:, CP_AG:CP_AG + 4], [128, 4, NT], 2), op=ALU.mult),
                       reads=[bkk, 'colp'], writes=['AST'])
            transpose_to(None, lambda j: AAf[:, j * 128:(j + 1) * 128], NT, 4, reads=['T1'], writes=['AST'], post=post_a)

            for c in range(2):
                bk, bkk = bank()
                for kt in range(KT):
                    sch.op('pe', lambda e, kt=kt, c=c, bk=bk: e.matmul(bk[0:NT, :], lhsT=AST[:, kt, 0:NT], rhs=Wo[:, kt, c * 512:(c + 1) * 512],
                                                                       start=(kt == 0), stop=(kt == KT - 1)),
                           reads=['AST', ('Wo', kt)], writes=[bkk], signal=(kt == KT - 1))
                sch.op('dve', lambda e, c=c, bk=bk: e.scalar_tensor_tensor(out=Xb[0:NT, c * 512:(c + 1) * 512], in0=Xb[0:NT, c * 512:(c + 1) * 512],
                                                                           scalar=ALPHA, in1=bk[0:NT, :], op0=ALU.mult, op1=ALU.add),
                       reads=[xk, bkk], writes=[xk])
            x1 = P.X1S[0:NT, j_in_super, :]
            dump(3, Xb[0:NT, :], NT, D, xk)
            layer_norm(Xb[0:NT, :], x1, NT, 0, xk, ('X1S', j_in_super))
            dump(4, x1, NT, D, ('X1S', j_in_super))
            t0 = j_in_super * 128
            def x1T_later():
                transpose_to(lambda j, n: P.X1T[:, j:j + n, t0:t0 + NT], lambda j: x1[:, j * 128:(j + 1) * 128], NT, KT,
                             reads=[('X1S', j_in_super), (('X1S', j_in_super), 0), (('X1S', j_in_super), 1)], writes=[('X1T', j_in_super)])
            deferred.append(x1T_later)

        def ffn(Fb, NTs, nseg, L, jlist, ft_t, ft_key, y_dsts, fconv_out, tails_only, defer_ln2=False):
            bstate['nb'] = 8
            x1keys = [('X1T', j) for j in range(4)]
            pend_gate = [None]
            for f in range(NF):
                ui = f % 3
                sch.dma('sp', Fb.Wup[ui][:, :, :], wb_up[:, f * 256:(f + 1) * 256].rearrange("(kt p) n -> p kt n", p=128),
                        reads=WUPK, writes=[('Wup', ui)])
                hbs = []
                for part in range(2):
                    fp = part * NF + f
                    bk, bkk = bank()
                    for kt in range(KT):
                        sch.op('pe', lambda e, kt=kt, bk=bk, ui=ui, part=part: e.matmul(bk[:, 0:NTs], lhsT=Fb.Wup[ui][:, kt, part * 128:(part + 1) * 128],
                                                                                        rhs=P.X1T[:, kt, 0:NTs], start=(kt == 0), stop=(kt == KT - 1)),
                               reads=x1keys + [('Wup', ui)], writes=[bkk], signal=(kt == KT - 1))
                    hi_ = part * 2 + (f % 2)
                    ps3 = bk[:, 0:NTs].rearrange("p (s l) -> p s l", s=nseg)
                    hd = Fb.HEAD[hi_][:, 0:nseg * 4].rearrange("p (s c) -> p s c", s=nseg)
                    hdk = ('HEAD', hi_)
                    if not tails_only:
                        sch.op('pool', lambda e, hd=hd, fp=fp: e.tensor_copy(out=hd[:, :, 0:2], in_=ft_t[:, fp, :, :]), reads=[(ft_key, fp)], writes=[hdk])
                        sch.op('act', lambda e, hd=hd, ps3=ps3: e.activation(out=hd[:, :, 2:4], in_=ps3[:, :, 0:2], func=AF.Copy), reads=[bkk], writes=[hdk])
                    sch.op('act', lambda e, ps3=ps3, fp=fp: e.activation(out=ft_t[:, fp, :, :], in_=ps3[:, :, L - 2:L], func=AF.Copy),
                           reads=[bkk], writes=[(ft_key, fp)])
                    if not tails_only:
                        cw = CP_FW + 3 * fp
                        sch.op('act', lambda e, bk=bk, hi_=hi_, cw=cw, fp=fp: e.activation(
                            out=Fb.FACC[hi_][:, 0:NTs], in_=bk[:, 0:NTs], func=AF.Identity, scale=colp[:, cw + 2:cw + 3],
                            bias=colp[:, CP_FB + fp:CP_FB + fp + 1]), reads=[bkk, 'colp'], writes=[('FACC', hi_), ('FACCh', hi_)])
                    hbs.append((ps3, bkk, hd, hdk, fp, hi_))
                if tails_only:
                    continue
                for jj in (1, 0):
                    for (ps3, bkk, hd, hdk, fp, hi_) in hbs:
                        fa = Fb.FACC[hi_][:, 0:NTs].rearrange("p (s l) -> p s l", s=nseg)
                        cw = CP_FW + 3 * fp
                        sch.op('dve', lambda e, ps3=ps3, fa=fa, cw=cw, jj=jj: e.scalar_tensor_tensor(out=fa[:, :, 2:L], in0=ps3[:, :, jj:jj + L - 2],
                                                                                                scalar=colp[:, cw + jj:cw + jj + 1], in1=fa[:, :, 2:L],
                                                                                                op0=ALU.mult, op1=ALU.add),
                               reads=[bkk, 'colp', ('FACC', hi_)], writes=[('FACC', hi_)])
                for jj in (1, 0):
                    for (ps3, bkk, hd, hdk, fp, hi_) in hbs:
                        fa = Fb.FACC[hi_][:, 0:NTs].rearrange("p (s l) -> p s l", s=nseg)
                        cw = CP_FW + 3 * fp
                        sch.op('dve', lambda e, hd=hd, fa=fa, cw=cw, jj=jj: e.scalar_tensor_tensor(out=fa[:, :, 0:2], in0=hd[:, :, jj:jj + 2],
                                                                                              scalar=colp[:, cw + jj:cw + jj + 1], in1=fa[:, :, 0:2],
                                                                                              op0=ALU.mult, op1=ALU.add),
                               reads=[hdk, 'colp', ('FACCh', hi_)], writes=[('FACCh', hi_)])
                def gate(f=f):
                    fv, fg = Fb.FACC[f % 2], Fb.FACC[2 + f % 2]
                    kv_, kg_ = ('FACC', f % 2), ('FACC', 2 + f % 2)
                    kvh, kgh = ('FACCh', f % 2), ('FACCh', 2 + f % 2)
                    sch.op('act', lambda e: e.activation(out=fg[:, 0:NTs], in_=fg[:, 0:NTs], func=AF.Silu), reads=[kg_, kgh], writes=[kg_, kgh])
                    sch.op('pool', lambda e: e.tensor_tensor(out=Fb.GT[:, f, 0:NTs], in0=fv[:, 0:NTs], in1=fg[:, 0:NTs], op=ALU.mult),
                           reads=[kv_, kg_, kvh, kgh], writes=[('GT', f)])
                if pend_gate[0] is not None:
                    pend_gate[0]()
                pend_gate[0] = gate
            if pend_gate[0] is not None:
                pend_gate[0]()
                pend_gate[0] = None
            if fconv_out is not None:
                nr = 2 * nseg
                per = 128 // nr // 2 * 2
                per = min(per, 42)
                f0 = 0
                while f0 < 42:
                    nf_ = min(per, 42 - f0)
                    ftf = ft_t[:, f0:f0 + nf_, :, :].rearrange("p f s r -> p (f s r)")
                    bkt, bktk = bank()
                    sch.op('pe', lambda e, ftf=ftf, bkt=bkt, nf_=nf_: e.transpose(out=bkt[0:nf_ * nr, 0:128], in_=ftf, identity=ident),
                           reads=[(ft_key, q_) for q_ in range(42)] + ['consts'], writes=[bktk])
                    sch.op('act', lambda e, bkt=bkt, nf_=nf_: e.activation(out=P.TT[0:nf_ * nr, :], in_=bkt[0:nf_ * nr, 0:128], func=AF.Copy),
                           reads=[bktk], writes=['TT'])
                    for q in range(nf_):
                        fp = f0 + q
                        sch.dma('sp', fconv_out[:, fp * 128:(fp + 1) * 128], P.TT[q * nr:(q + 1) * nr, :], reads=['TT'])
                    f0 += nf_
            if tails_only:
                return
            gkeys = [('GT', f) for f in range(NF)]
            for cb in range(4):
                di = cb % 2
                sch.dma('sp', Fb.Wdn[di][:, :, :], wb_down[:, cb * 256:(cb + 1) * 256].rearrange("(f p) n -> p f n", p=128),
                        reads=WDNK, writes=[('Wdn', di)])
                for (j, t0, nt) in jlist:
                    bk, bkk = bank()
                    for f in range(NF):
                        sch.op('pe', lambda e, f=f, bk=bk, di=di, t0=t0, nt=nt: e.matmul(bk[0:nt, 0:256], lhsT=Fb.GT[:, f, t0:t0 + nt], rhs=Fb.Wdn[di][:, f, :],
                                                                                         start=(f == 0), stop=(f == NF - 1)),
                               reads=gkeys + [('Wdn', di)], writes=[bkk], signal=(f == NF - 1))
                    xs_ = P.X1S[0:nt, j, cb * 256:(cb + 1) * 256]
                    sch.op('dve', lambda e, xs_=xs_, bk=bk, nt=nt: e.scalar_tensor_tensor(out=xs_, in0=xs_, scalar=ALPHA, in1=bk[0:nt, 0:256],
                                                                                         op0=ALU.mult, op1=ALU.add),
                           reads=[('X1S', j), (('X1S', j), 0), (('X1S', j), 1), bkk], writes=[('X1S', j)])
            def ln2_later():
                for n_, (j, t0, nt) in enumerate(jlist):
                    layer_norm(P.X1S[0:nt, j, :], P.X1S[0:nt, j, :], nt, 2, ('X1S', j), ('X1S', j))
                    sch.dma('sp', y_dsts[n_], P.X1S[0:nt, j, :], reads=[('X1S', j), (('X1S', j), 0), (('X1S', j), 1)])
            if defer_ln2:
                return ln2_later
            ln2_later()

        def final_state(HFt, hk, dst, CK):
            def post_h(j, n, src, bkk):
                sch.op('act', lambda e: e.activation(out=CK[:, 0:4, 0:128], in_=src, func=AF.Copy), reads=[bkk], writes=['CKo'])
            transpose_to(None, lambda j: HFt[:, j * 128:(j + 1) * 128], 128, 4, reads=[('HF', hk)], writes=['CKo'], post=post_h)
            sch.dma('sp', dst.rearrange("(a p) n -> p a n", p=128), CK[:, 0:4, 0:128], reads=['CKo'])


        def light_bufs(st):
            Lb = NS()
            Lb.XBCL = sbuf(st, "XBCL", [128, 8, 515])
            Lb.ACCL = sbuf(st, "ACCL", [128, 8, 512])
            Lb.SILL = sbuf(st, "SILL", [128, 8, 512])
            Lb.DTAL = sbuf(st, "DTAL", [128, 4, 4, 8])
            Lb.WDSL = sbuf(st, "WDSL", [128, 4, 8])
            Lb.CDL = sbuf(st, "CDL", [128, 8])
            Lb.XDDL = sbuf(st, "XDDL", [128, 4, 8, 64], BF16)
            Lb.BKL = sbuf(st, "BKL", [128, 4, 2, 128], BF16)
            sch.op('dve', lambda e: e.memset(Lb.XBCL[:, :, 0:3], 0.0), writes=[('XBCL', t_) for t_ in range(8)])
            return Lb

        def light_tile(Lb, b0, nb, kv, last):
            bstate['nb'] = 8
            NT = nb * 128
            XL = P.X1S
            XTL = P.X1T
            XBCL, ACCL, SILL, DTAL = Lb.XBCL, Lb.ACCL, Lb.SILL, Lb.DTAL
            sch.dma('sp', XL[:, 0:nb, :], xs[b0 * 128:(b0 + nb) * 128, :].rearrange("(c p) d -> p c d", p=128), writes=['XL'])
            for c in range(nb):
                transpose_to(lambda j, n, c=c: XTL[:, j:j + n, c * 128:(c + 1) * 128], lambda j, c=c: XL[:, c, j * 128:(j + 1) * 128], 128, KT,
                             reads=['XL'], writes=[('XTL', c)])
            xtk = [('XTL', c) for c in range(nb)]
            for t in range(8):
                bk, bkk = bank()
                cc = C_XBC + t * 128
                for kt in range(KT):
                    sch.op('pe', lambda e, kt=kt, cc=cc, bk=bk: e.matmul(bk[:, 0:NT], lhsT=Wi[:, kt, cc:cc + 128], rhs=XTL[:, kt, 0:NT],
                                                                         start=(kt == 0), stop=(kt == KT - 1)),
                           reads=xtk + [('Wi', kt)], writes=[bkk], signal=(kt == KT - 1))
                sch.op('act', lambda e, t=t, bk=bk: e.activation(out=XBCL[:, t, 3:3 + NT], in_=bk[:, 0:NT], func=AF.Copy),
                       reads=[bkk], writes=[('XBCL', t)])
                cw_ = CP_CW + 4 * t
                sch.op('act', lambda e, t=t, bk=bk, cw_=cw_: e.activation(out=ACCL[:, t, 0:NT], in_=bk[:, 0:NT], func=AF.Identity,
                                                                          scale=colp[:, cw_ + 3:cw_ + 4], bias=colp[:, CP_CB + t:CP_CB + t + 1]),
                       reads=[bkk, 'colp'], writes=[('ACCL', t)])
            bkd, bkdk = bank()
            for c in range(nb):
                for kt in range(KT):
                    sch.op('pe', lambda e, kt=kt, c=c: e.matmul(bkd[:, c * 8:(c + 1) * 8], lhsT=XTL[:, kt, c * 128:(c + 1) * 128], rhs=Wi[:, kt, C_DT:C_DT + 8],
                                                                start=(kt == 0), stop=(kt == KT - 1)),
                           reads=xtk + [('Wi', kt)], writes=[bkdk], signal=(c == nb - 1 and kt == KT - 1))
            u_, au, dt_, a_ = (DTAL[:, i, 0:nb, :] for i in range(4))
            sch.op('dve', lambda e: e.tensor_tensor(out=u_, in0=bkd[:, 0:nb * 8].rearrange("p (c h) -> p c h", c=nb),
                                                    in1=bc(rowp[:, 0:8], [128, nb, 8], 1), op=ALU.add), reads=[bkdk, 'rowp'], writes=['DTAL'])
            sch.op('act', lambda e: e.activation(out=au, in_=u_, func=AF.Abs), reads=['DTAL'], writes=['DTAL'])
            sch.op('act', lambda e: e.activation(out=au, in_=au, func=AF.Exp, scale=-1.0), reads=['DTAL'], writes=['DTAL'])
            sch.op('act', lambda e: e.activation(out=au, in_=au, func=AF.Ln, bias=1.0, scale=1.0), reads=['DTAL'], writes=['DTAL'])
            sch.op('dve', lambda e: e.scalar_tensor_tensor(out=dt_, in0=u_, scalar=0.0, in1=au, op0=ALU.max, op1=ALU.add),
                   reads=['DTAL'], writes=['DTAL'])
            sch.op('dve', lambda e: e.tensor_tensor(out=a_, in0=dt_, in1=bc(P.Aneg[:, :], [128, nb, 8], 1), op=ALU.mult),
                   reads=['DTAL', 'Aneg'], writes=['DTAL'])
            if kv:
                for hp in range(4):
                    bk, bkk = bank()
                    cc = C_K + hp * 128
                    for kt in range(KT):
                        sch.op('pe', lambda e, kt=kt, cc=cc, bk=bk: e.matmul(bk[:, 0:NT], lhsT=Wi[:, kt, cc:cc + 128], rhs=XTL[:, kt, 0:NT],
                                                                             start=(kt == 0), stop=(kt == KT - 1)),
                               reads=xtk + [('Wi', kt)], writes=[bkk], signal=(kt == KT - 1))
                    for c in range(nb):
                        sl = (b0 + c) % NKR
                        sch.op('dve', lambda e, c=c, sl=sl, hp=hp, bk=bk: e.tensor_copy(out=P.KTr[:, sl, hp, :], in_=bk[:, c * 128:(c + 1) * 128]),
                               reads=[bkk], writes=[('KTr', sl)])
                for c in range(nb):
                    sl = (b0 + c) % NKR
                    bkv, bkvk = bank()
                    for kt in range(KT):
                        sch.op('pe', lambda e, kt=kt, c=c, bkv=bkv: e.matmul(bkv[:, :], lhsT=XTL[:, kt, c * 128:(c + 1) * 128], rhs=Wi[:, kt, C_V:C_V + 512],
                                                                             start=(kt == 0), stop=(kt == KT - 1)),
                               reads=xtk + [('Wi', kt)], writes=[bkvk], signal=(kt == KT - 1))
                    sch.op('act', lambda e, sl=sl, bkv=bkv: e.activation(out=P.VAr[:, sl, :, 0:64], in_=bkv[:, :].rearrange("p (h d) -> p h d", h=8),
                                                                         func=AF.Copy), reads=[bkvk], writes=[('VAr', sl)])
                    sch.op('dve', lambda e, sl=sl: e.tensor_copy(out=P.VAr[:, sl, :, 64], in_=flg[:, 0:1].to_broadcast([128, 8])),
                           reads=['flg'], writes=[('VAr', sl)])
            for jj in (0, 1, 2):
                for t in range(8):
                    cw = CP_CW + 4 * t
                    acc = ACCL[:, t, 0:NT]
                    if jj == 3:
                        sch.op('dve', lambda e, t=t, cw=cw, acc=acc: e.tensor_scalar(out=acc, in0=XBCL[:, t, 3:3 + NT], scalar1=colp[:, cw + 3:cw + 4],
                                                                                     scalar2=colp[:, CP_CB + t:CP_CB + t + 1], op0=ALU.mult, op1=ALU.add),
                               reads=[('XBCL', t), 'colp'], writes=[('ACCL', t)])
                    else:
                        sch.op('dve', lambda e, t=t, cw=cw, acc=acc, jj=jj: e.scalar_tensor_tensor(out=acc, in0=XBCL[:, t, jj:jj + NT],
                                                                                                  scalar=colp[:, cw + jj:cw + jj + 1], in1=acc,
                                                                                                  op0=ALU.mult, op1=ALU.add),
                               reads=[('XBCL', t), 'colp', ('ACCL', t)], writes=[('ACCL', t)])
            for hf_ in range(2):
                sch.op('act', lambda e, hf_=hf_: e.activation(out=SILL[:, 4 * hf_:4 * hf_ + 4, 0:NT], in_=ACCL[:, 4 * hf_:4 * hf_ + 4, 0:NT], func=AF.Silu),
                       reads=[('ACCL', t_) for t_ in range(4 * hf_, 4 * hf_ + 4)], writes=[('SILL', hf_)])
            if last:
                sch.op('dve', lambda e: e.tensor_copy(out=P.XBC[:, :, 128:131], in_=XBCL[:, :, NT:NT + 3]),
                       reads=[('XBCL', t_) for t_ in range(8)], writes=['XBC'])
            else:
                sch.op('dve', lambda e: e.tensor_copy(out=XBCL[:, :, 0:3], in_=XBCL[:, :, NT:NT + 3]),
                       reads=[('XBCL', t_) for t_ in range(8)] + [('ACCL', t_) for t_ in range(8)], writes=[('XBCL', t_) for t_ in range(8)])
            bke, bkek = bank()
            for c in range(nb):
                ops_ = [(cU, c)] + [(cOnes, c2) for c2 in range(c + 1, nb)]
                for ii, (lm, c2) in enumerate(ops_):
                    sch.op('pe', lambda e, c=c, lm=lm, c2=c2, ii=ii, n_=len(ops_): e.matmul(bke[:, c * 8:(c + 1) * 8], lhsT=lm, rhs=DTAL[:, 3, c2, :],
                                                                                          start=(ii == 0), stop=(ii == n_ - 1)),
                           reads=['DTAL', 'consts'], writes=[bkek], signal=False)
            for c in range(nb):
                sch.op('pe', lambda e, c=c: e.matmul(bke[:, 32:40], lhsT=cOnes, rhs=DTAL[:, 3, c, :], start=(c == 0), stop=(c == nb - 1)),
                       reads=['DTAL', 'consts'], writes=[bkek], signal=(c == nb - 1))
            sch.op('act', lambda e: e.activation(out=Lb.WDSL[:, 0:nb, :], in_=bke[:, 0:nb * 8].rearrange("p (c h) -> p c h", c=nb), func=AF.Exp),
                   reads=[bkek], writes=['WDSL'])
            sch.op('act', lambda e: e.activation(out=Lb.CDL[:, :], in_=bke[:, 32:40], func=AF.Exp), reads=[bkek], writes=['CDL'])
            sch.op('dve', lambda e: e.tensor_tensor(out=Lb.WDSL[:, 0:nb, :], in0=Lb.WDSL[:, 0:nb, :], in1=dt_, op=ALU.mult),
                   reads=['WDSL', 'DTAL'], writes=['WDSL'])
            for c in range(nb):
                bkx, bkxk = bank()
                for q in range(4):
                    sch.op('pe', lambda e, q=q, c=c, bkx=bkx: e.transpose(out=bkx[:, q * 128:(q + 1) * 128], in_=SILL[:, q, c * 128:(c + 1) * 128], identity=ident),
                           reads=[('SILL', 0), 'consts'], writes=[bkxk], signal=(q == 3))
                sch.op('dve', lambda e, c=c, bkx=bkx: e.tensor_tensor(out=Lb.XDDL[:, c, :, :], in0=bkx[:, :].rearrange("p (h d) -> p h d", h=8),
                                                                      in1=bc(Lb.WDSL[:, c, :], [128, 8, 64], 2), op=ALU.mult),
                       reads=[bkxk, 'WDSL'], writes=[('XDDL', c)])
            for c0 in range(0, nb, 2):
                n2 = min(2, nb - c0)
                bkb, bkbk = bank()
                for cc_ in range(n2):
                    for g in range(2):
                        sch.op('pe', lambda e, cc_=cc_, g=g, c0=c0, bkb=bkb: e.transpose(out=bkb[:, (cc_ * 2 + g) * 128:(cc_ * 2 + g + 1) * 128],
                                                                                         in_=SILL[:, 4 + g, (c0 + cc_) * 128:(c0 + cc_ + 1) * 128], identity=ident),
                               reads=[('SILL', 1), 'consts'], writes=[bkbk], signal=(cc_ == n2 - 1 and g == 1))
                sch.op('act', lambda e, c0=c0, n2=n2, bkb=bkb: e.activation(out=Lb.BKL[:, c0:c0 + n2, :, :],
                                                                           in_=bkb[:, 0:n2 * 256].rearrange("p (c g n) -> p c g n", c=n2, g=2), func=AF.Copy),
                       reads=[bkbk], writes=[('BKL', c0 // 2)])
            bks, bksk = bank()
            for g in range(2):
                for c in range(nb):
                    sch.op('pe', lambda e, g=g, c=c: e.matmul(bks[:, g * 256:(g + 1) * 256], lhsT=Lb.BKL[:, c, g, :], rhs=Lb.XDDL[:, c, 4 * g:4 * g + 4, :],
                                                              start=(c == 0), stop=(c == nb - 1)),
                           reads=[('BKL', c // 2), ('XDDL', c)], writes=[bksk], signal=(g == 1 and c == nb - 1))
            hv = P.HF[:].rearrange("p (h d) -> p h d", h=8)
            sch.op('dve', lambda e: e.tensor_tensor(out=hv, in0=hv, in1=bc(Lb.CDL[:, :], [128, 8, 64], 2), op=ALU.mult),
                   reads=[('HF', 0), 'CDL'], writes=[('HF', 0)])
            sch.op('dve', lambda e: e.tensor_tensor(out=P.HF[:], in0=P.HF[:], in1=bks[:, 0:512], op=ALU.add),
                   reads=[('HF', 0), bksk], writes=[('HF', 0)])
            if last:
                sch.op('act', lambda e: e.activation(out=P.HB[:], in_=P.HF[:], func=AF.Copy), reads=[('HF', 0)], writes=[('HB', 0)])

        def tiles_p(b):
            def f(s, h):
                hp, po = h // 2, (h % 2) * 64
                tl = []
                for i in (2, 3, 4, 0, 1):
                    sl = (b - i) % NKR
                    t_ = dict(kt=P.KTr[po:po + 64, sl, hp, :], ktkey=('KTr', sl), va=P.VAr[:, sl, h, :], vakey=('VAr', sl), nk=128, pb=0,
                              bias=None if i >= 2 else P.biasT[:, i, h, :])
                    if i == 4:
                        t_['zero'] = [(0, 64, 64, 128)]
                    if i == 0:
                        t_['zero'] = [(64, 128, 0, 64)]
                    tl.append(t_)
                return tl
            return f

        def run_blocks(M, b_lo, b_hi):
            for b in range(b_lo, b_hi):
                kind = 'ssm' if b < B_KV else ('kv' if b < B_FULL else 'full')
                sch.op('pool', lambda e: e.tensor_copy(out=P.XBC[:, :, 0:3], in_=P.XBC[:, :, 128:131]), reads=['XBC'], writes=['XBC'])
                outs = {}
                if b >= NBLK - 4:
                    r0 = (b - (NBLK - 4)) * 128
                    outs['nk'] = nk_o[r0:r0 + 128, :]
                    outs['nv'] = nv_o[r0:r0 + 128, :]
                if b == NBLK - 1:
                    outs['sconv'] = sc_o
                if b == B_FULL:
                    outs['flag_state'] = True
                j_in_super = 0 if b <= B_FULL else (b - B_MAIN) % 4
                sl = b % NKR
                kvd = (lambda j, n, sl=sl: P.KTr[:, sl, j:j + n, :], ('KTr', sl), P.VAr[:, sl, :, :], ('VAr', sl))
                mixer_block(M, kind, 128, 1, 128, xs[b * 128:(b + 1) * 128, :], b % 2,
                            P.XBC[:].rearrange("p t (s l) -> p t s l", s=1), 'XBC', [(P.HF, P.HB, 0)], tiles_p(b), outs, kvd, j_in_super)

        if do_prompt:
            sP = ExitStack()
            P.KTr = sbuf(sP, "KTr", [128, NKR, 4, 128], BF16)
            P.VAr = sbuf(sP, "VAr", [128, NKR, 8, 65], BF16)
            P.HF = sbuf(sP, "HF0", [128, 512])
            P.HB = sbuf(sP, "HB0", [128, 512], BF16)
            P.FT = sbuf(sP, "FT", [128, 42, 2])
            sch.op('dve', lambda e: e.memset(P.HF[:], 0.0), writes=[('HF', 0)])
            sch.op('dve', lambda e: e.memset(P.HB[:], 0.0), writes=[('HB', 0)])
            sch.op('dve', lambda e: e.memset(P.FT[:], 0.0), writes=[('FT', q_) for q_ in range(42)])
            FT4 = P.FT[:].rearrange("p f (s r) -> p f s r", s=1)
            sch.barrier(skip_pool_dma=True)
            if nblk == NBLK:
                with ExitStack() as st:
                    Lb = light_bufs(st)
                    tiles_ = [(0, 3)] + [(3 + 4 * i, 4) for i in range(7)]
                    for (b0_, nb_) in tiles_:
                        light_tile(Lb, b0_, nb_, b0_ + nb_ == B_FULL, b0_ + nb_ == B_FULL)
                    sch.barrier(skip_pool_dma=True)
            with ExitStack() as st:
                M = mixer_bufs(st)
                run_blocks(M, B_FULL if nblk == NBLK else NBLK - nblk, B_FULL + 1)
                sl = B_FULL % NKR
                sch.op('pool', lambda e: e.tensor_copy(out=P.VAr[:, sl, :, 64], in_=flg[:, 0:1].to_broadcast([128, 8])),
                       reads=['flg', ('VAr', sl)], writes=[('VAr', sl)])
                sch.barrier()
            with ExitStack() as st:
                Fb = ffn_bufs(st)
                ffn(Fb, 128, 1, 128, [(0, 0, 128)], FT4, 'FT', [None], None, True)
                sch.op('dve', lambda e: e.tensor_scalar(out=P.FT[:], in0=P.FT[:], scalar1=flg[:, 0:1], scalar2=None, op0=ALU.mult),
                       reads=[('FT', q_) for q_ in range(42)] + ['flg'], writes=[('FT', q_) for q_ in range(42)])
                sch.barrier()
            pend_ln2 = None
            for b0 in range(B_MAIN, NBLK, 4):
                with ExitStack() as st:
                    M = mixer_bufs(st)
                    if pend_ln2 is not None:
                        pend_ln2()
                        pend_ln2 = None
                    run_blocks(M, b0, b0 + 4)
                    if b0 + 4 == NBLK:
                        final_state(P.HF, 0, ssm_o, M.LM[:, 0:4, :])
                    sch.barrier()
                with ExitStack() as st:
                    Fb = ffn_bufs(st)
                    jl = [(j, j * 128, 128) for j in range(4)]
                    yd = [y_o[(b0 - B_MAIN + j) * 128:(b0 - B_MAIN + j + 1) * 128, :] for j in range(4)]
                    pend_ln2 = ffn(Fb, 512, 1, 512, jl, FT4, 'FT', yd, fc_o if b0 + 4 == NBLK else None, False, defer_ln2=(b0 + 4 < NBLK))
                    sch.barrier()
            sP.close()
            sch.barrier()
        if do_sample:
            sA = ExitStack()
            S = NS()
            S.FTs = sbuf(sA, "FTs", [128, 42, 2, 2])
            with ExitStack() as st:
                M = mixer_bufs(st)
                S.biasN = sbuf(st, "biasN", [64, 8, 32])
                S.XBCs = sbuf(st, "XBCs", [128, 8, 2, 35])
                S.KTc = [sbuf(st, "KTc%d" % i, [128, 4, 512], BF16) for i in range(2)]
                S.VAc = [sbuf(st, "VAc%d" % i, [128, 4, 8, 65], BF16) for i in range(2)]
                S.CKf = sbuf(st, "CKf", [128, 512])
                S.CKg = sbuf(st, "CKg", [128, 512])
                CK3 = S.CKf[:].rearrange("p (a b) -> p a b", a=4)
                S.HF = [sbuf(st, "HFs%d" % i, [128, 512]) for i in range(2)]
                S.HB = [sbuf(st, "HBs%d" % i, [128, 512], BF16) for i in range(2)]
                S.KTn = sbuf(st, "KTn", [128, 4, 64], BF16)
                S.VAn = sbuf(st, "VAn", [64, 8, 65], BF16)
                L = 32
                if stage == 's1':
                    sch.finish()
                    st.close()
                    sA.close()
                    return nc
                sch.dma('sp', S.biasN[:], biasN_d[:, :, :], writes=['biasN'])
                Xs = M.X[1]
                sch.dma('sp', Xs[0:6, :], cs0[:, :], writes=[('X', 1)])

                def post_cs(j, n, src, bkk):
                    sch.op('act', lambda e: e.activation(out=S.XBCs[:, j:j + n, :, 0:3], in_=src.rearrange("p a (s r) -> p a s r", s=2), func=AF.Copy),
                           reads=[bkk], writes=['XBCs'])
                transpose_to(None, lambda j: Xs[0:6, j * 128:(j + 1) * 128], 6, 8, reads=[('X', 1)], writes=['XBCs'], post=post_cs)
                if stage == 's2':
                    sch.finish()
                    st.close()
                    sA.close()
                    return nc
                for c5 in range(6):
                    w = min(D, 2 * DFF - c5 * D)
                    sch.dma('sp', Xs[0:4, 0:w], fs0[:, c5 * D:c5 * D + w], writes=[('X', 1)])

                    def post_fs(j, n, src, bkk, c5=c5):
                        sch.op('act', lambda e: e.activation(out=S.FTs[:, c5 * 8 + j:c5 * 8 + j + n, :, :], in_=src.rearrange("p a (s r) -> p a s r", s=2),
                                                             func=AF.Copy), reads=[bkk], writes=[('FTs', q_) for q_ in range(42)])
                    transpose_to(None, lambda j: Xs[0:4, j * 128:(j + 1) * 128], 4, w // 128, reads=[('X', 1)], writes=['FTs'], post=post_fs)
                if stage == 's3':
                    sch.finish()
                    st.close()
                    sA.close()
                    return nc
                for s in range(2):
                    sch.dma('sp', CK3, hs0[s].rearrange("(a p) n -> p a n", p=128), writes=['CKf'])
                    if stage == 's3a':
                        sch.finish()
                        st.close()
                        sA.close()
                        return nc

                    def post_h0(j, n, src, bkk, s=s):
                        if stage == 's3c':
                            return
                        if stage == 's4v1':
                            sch.op('dve', lambda e: e.tensor_copy(out=M.T2[:], in_=src.rearrange("p a b -> p (a b)")), reads=[bkk], writes=['T2'])
                            return
                        if stage == 's4v3':
                            sch.op('dve', lambda e: e.tensor_scalar(out=M.T2[:], in0=src.rearrange("p a b -> p (a b)"), scalar1=1.0, scalar2=None, op0=ALU.mult), reads=[bkk], writes=['T2'])
                            return
                        sch.op('act', lambda e: e.activation(out=S.HF[s][:], in_=src.rearrange("p a b -> p (a b)"), func=AF.Copy),
                               reads=[bkk], writes=[('HF', 1 + s)])
                        if stage == 's3b':
                            return
                        if stage == 's4y':
                            sch.op('dve', lambda e: e.memset(M.T2[:], 0.0), reads=[bkk], writes=['T2'])
                            return
                        if stage == 's4z':
                            sch.op('dve', lambda e: e.memset(M.T2[:], 0.0), reads=[], writes=['T2'])
                            return
                        if stage == 's4x':
                            sch.op('dve', lambda e: e.tensor_copy(out=M.T2[:], in_=src.rearrange("p a b -> p (a b)")), reads=[bkk], writes=['T2'])
                            return
                        sch.op('dve', lambda e: e.tensor_copy(out=S.HB[s][:], in_=src.rearrange("p a b -> p (a b)")), reads=[bkk], writes=[('HB', 1 + s)])
                    transpose_to(None, lambda j: CK3[:, j, :], 128, 4, reads=['CKf'], writes=[], post=post_h0)
                if stage in ('s4', 's3b', 's3c', 's4x', 's4y', 's4z', 's4v1', 's4v3'):
                    sch.finish()
                    st.close()
                    sA.close()
                    return nc
                for s in range(2):
                    for m in range(4):
                        CKb, ckk = ((S.CKf, 'CKf'), (S.CKg, 'CKg'))[m % 2]
                        sch.dma('sp', CKb[:], ck[s][m * 128:(m + 1) * 128, :], writes=[ckk])
                        transpose_to(lambda j, n, m=m, s=s: S.KTc[s][:, j:j + n, m * 128:(m + 1) * 128], lambda j, CKb=CKb: CKb[:, j * 128:(j + 1) * 128], 128, 4,
                                     reads=[ckk], writes=[('KTc', s)])
                    for m in range(4):
                        CKb, ckk = ((S.CKf, 'CKf'), (S.CKg, 'CKg'))[m % 2]
                        sch.dma('sp', CKb[:], cv[s][m * 128:(m + 1) * 128, :], writes=[ckk])
                        sch.op('dve', lambda e, s=s, m=m, CKb=CKb: e.tensor_copy(out=S.VAc[s][:, m, :, 0:64], in_=CKb[:].rearrange("p (h d) -> p h d", h=8)),
                               reads=[ckk], writes=[('VAc', s)])
                    sch.op('pool', lambda e, s=s: e.memset(S.VAc[s][:, :, :, 64], 1.0), writes=[('VAc', s)])

                def tiles_s(s, h):
                    hp, po = h // 2, (h % 2) * 64
                    tl = []
                    for m in range(3):
                        tl.append(dict(kt=S.KTc[s][po:po + 64, hp, m * 128:(m + 1) * 128], ktkey=('KTc', s), va=S.VAc[s][:, m, h, :], vakey=('VAc', s),
                                       nk=128, pb=0, bias=None))
                    tl.append(dict(kt=S.KTc[s][po:po + 64, hp, 384:512], ktkey=('KTc', s), va=S.VAc[s][:, 3, h, :], vakey=('VAc', s),
                                   nk=128, pb=0, bias=P.biasT[:, 1, h, 0:32]))
                    tl.append(dict(kt=S.KTn[po:po + 64, hp, s * 32:(s + 1) * 32], ktkey='KTn', va=S.VAn[s * 32:(s + 1) * 32, h, :], vakey='VAn',
                                   nk=32, pb=s * 32, bias=S.biasN[s * 32:(s + 1) * 32, h, :]))
                    return tl

                kvd = (lambda j, n: S.KTn[:, j:j + n, 0:64], 'KTn', S.VAn[0:64, :, :], 'VAn')
                if stage == 'sample_setup':
                    sch.finish()
                    st.close()
                    sA.close()
                    return nc
                mixer_block(M, 'full', 64, 2, 32, xq[:, :], 0, S.XBCs[:], 'XBCs',
                            [(S.HF[0], S.HB[0], 1), (S.HF[1], S.HB[1], 2)], tiles_s,
                            dict(nk=nks_o[:, :], nv=nvs_o[:, :], sconv=scs_o), kvd, 0)
                if stage == 'sample_mixer':
                    sch.finish()
                    st.close()
                    sA.close()
                    return nc
                for s in range(2):
                    final_state(S.HF[s], 1 + s, ssms_o[s], CK3)
                sch.barrier()
            with ExitStack() as st2:
                Fb = ffn_bufs(st2)
                ffn(Fb, 64, 2, 32, [(0, 0, 64)], S.FTs[:], 'FTs', [ys_o[:, :]], fcs_o, False)
                sch.barrier()
            sA.close()

        sch.finish()
    return nc


_CACHE = {}


def _get_nc():
    if 'nc' not in _CACHE:
        _CACHE['nc'] = build_program()
    return _CACHE['nc']


def _host_consts(rel_bias):
    k = np.arange(128)[:, None]
    q = np.arange(128)[None, :]
    rb = rel_bias
    biasT = np.empty((128, 2, 8, 128), np.float32)
    for i in range(2):
        idx = np.clip(128 * i + q - k, -128, 128) + 128
        biasT[:, i] = np.transpose(rb[:, idx], (1, 0, 2))
    kk = np.arange(32)[:, None]
    qq = np.arange(32)[None, :]
    idx = np.clip(qq - kk, -128, 128) + 128
    bn = np.transpose(rb[:, idx], (1, 0, 2))
    biasN = np.concatenate([bn, bn], axis=0).astype(np.float32)
    cbias = rb[:, 256].reshape(1, 8).astype(np.float32)
    consts = np.zeros((128, 8, 128), np.float32)
    consts[:, 0] = np.eye(128)
    consts[:, 1] = (k <= q)
    consts[:, 2] = (k > q)
    consts[:, 3] = 1.0
    seg = np.arange(128) // 32
    same = seg[:, None] == seg[None, :]
    consts[:, 4] = (k <= q) & same
    consts[:, 5] = (k > q) & same
    consts[:, 6] = (seg == 0)[:, None]
    consts[:, 7] = (seg == 1)[:, None]
    return biasT, biasN, cbias, consts


def kernel(x_prompt, x_sample, cache_attn_k, cache_attn_v, state_ssm, state_ssm_conv, state_ffn_conv,
           w_in, rel_bias, attn_norm_g, ssm_conv_w, ssm_conv_b, ssm_dt_bias, ssm_A_log, ssm_D,
           ssm_norm_g, w_out, ln1_g, ln1_b, w_up, ffn_conv_w, ffn_conv_b, w_down, ln2_g, ln2_b):
    f32 = np.float32
    A = lambda a: np.ascontiguousarray(np.asarray(a, dtype=f32))
    x_prompt, x_sample = A(x_prompt), A(x_sample)
    ck_, cv_ = A(cache_attn_k)[0], A(cache_attn_v)[0]
    hs_, cs_, fs_ = A(state_ssm)[0], A(state_ssm_conv)[0], A(state_ffn_conv)[0]
    biasT, biasN, cbias, consts = _host_consts(A(rel_bias)[0])
    colp = np.zeros((128, CP_N), f32)
    colp[:, CP_AG:CP_AG + 4] = A(attn_norm_g)[0].reshape(4, 128).T
    colp[:, CP_SG:CP_SG + 4] = A(ssm_norm_g)[0].reshape(4, 128).T
    cw = A(ssm_conv_w)[0]
    colp[:, CP_CW:CP_CW + 32] = cw.reshape(4, 8, 128).transpose(2, 1, 0).reshape(128, 32)
    colp[:, CP_CB:CP_CB + 8] = A(ssm_conv_b)[0].reshape(8, 128).T
    fw = A(ffn_conv_w)[0]
    colp[:, CP_FW:CP_FW + 126] = fw.reshape(3, 42, 128).transpose(2, 1, 0).reshape(128, 126)
    colp[:, CP_FB:CP_FB + 42] = A(ffn_conv_b)[0].reshape(42, 128).T
    rowp = np.concatenate([A(ssm_dt_bias)[0], A(ssm_A_log)[0], A(ssm_D)[0]]).reshape(1, 24)
    lnp = np.concatenate([A(ln1_g)[0], A(ln1_b)[0], A(ln2_g)[0], A(ln2_b)[0]]).reshape(1, 4 * D)
    wu = A(w_up)[0]
    wu_perm = np.ascontiguousarray(wu.reshape(D, 2, NF, 128).transpose(0, 2, 1, 3).reshape(D, 2 * DFF))
    shared = dict(w_in=A(w_in)[0], w_out=A(w_out)[0], w_up=wu_perm, w_down=A(w_down)[0],
                  biasT=biasT, biasN=biasN, cbias=cbias, colp=colp, rowp=rowp, lnp=lnp, consts=consts)
    in_maps = []
    for c in range(8):
        b, hf = c // 2, c % 2
        if hf == 0:
            xs = np.concatenate([np.zeros((4096, D), f32), x_prompt[b, :4096]], axis=0)
        else:
            xs = x_prompt[b]
        m = dict(shared)
        m.update(xs=np.ascontiguousarray(xs), flag=np.full((128, 1), float(hf), f32),
                 xq=np.ascontiguousarray(x_sample[2 * c:2 * c + 2].reshape(64, D)),
                 ck=np.ascontiguousarray(ck_[2 * c:2 * c + 2].reshape(2, 512, 512)),
                 cv=np.ascontiguousarray(cv_[2 * c:2 * c + 2].reshape(2, 512, 512)),
                 hs0=np.ascontiguousarray(hs_[2 * c:2 * c + 2].reshape(2, 512, 128)),
                 cs0=np.ascontiguousarray(cs_[2 * c:2 * c + 2].reshape(6, D)),
                 fs0=np.ascontiguousarray(fs_[2 * c:2 * c + 2].reshape(4, 2 * DFF)))
        in_maps.append(m)
    nc = _get_nc()
    res = run_bass_kernel_spmd(nc, in_maps, core_ids=list(range(8)))
    R = res.results
    y_prompt = np.stack([np.concatenate([R[2 * b]["y_o"], R[2 * b + 1]["y_o"]], axis=0) for b in range(4)]).astype(f32)
    y_sample = np.concatenate([R[c]["ys_o"].reshape(2, 32, D) for c in range(8)], axis=0).astype(f32)
    nkp = np.stack([R[2 * b + 1]["nk_o"].reshape(512, 8, 64) for b in range(4)])[None].astype(f32)
    nvp = np.stack([R[2 * b + 1]["nv_o"].reshape(512, 8, 64) for b in range(4)])[None].astype(f32)
    nks = np.concatenate([R[c]["nks_o"].reshape(2, 32, 8, 64) for c in range(8)], axis=0)[None].astype(f32)
    nvs = np.concatenate([R[c]["nvs_o"].reshape(2, 32, 8, 64) for c in range(8)], axis=0)[None].astype(f32)
    ssmp = np.stack([R[2 * b + 1]["ssm_o"].reshape(8, 64, 128) for b in range(4)])[None].astype(f32)
    ssms = np.concatenate([R[c]["ssms_o"].reshape(2, 8, 64, 128) for c in range(8)], axis=0)[None].astype(f32)
    scp = np.stack([R[2 * b + 1]["sc_o"] for b in range(4)])[None].astype(f32)
    scs = np.concatenate([R[c]["scs_o"].reshape(2, 3, D) for c in range(8)], axis=0)[None].astype(f32)
    fcp = np.stack([R[2 * b + 1]["fc_o"] for b in range(4)])[None].astype(f32)
    fcs = np.concatenate([R[c]["fcs_o"].reshape(2, 2, 2 * DFF) for c in range(8)], axis=0)[None].astype(f32)
    return (y_prompt, y_sample, nkp, nvp, nks, nvs, ssmp, ssms, scp, scs, fcp, fcs)
```

```python
import numpy as np
from contextlib import ExitStack
import concourse.bass as bass
import concourse.mybir as mybir
from concourse.bass_utils import run_bass_kernel_spmd

F32 = mybir.dt.float32
BF16 = mybir.dt.bfloat16
AF = mybir.ActivationFunctionType
ALU = mybir.AluOpType

D = 1024
KT = 8
NPROJ = 3080
DFF = 2688
NF = 21
C_Q, C_K, C_V, C_Z, C_XBC, C_DT = 0, 512, 1024, 1536, 2048, 3072
ALPHA = 2.0 ** 0.25
EPS = 1e-5
NBLK = 64
B_KV = 27
B_FULL = 31
B_MAIN = 32
CP_AG, CP_SG, CP_CW, CP_CB, CP_FW, CP_FB, CP_N = 0, 4, 8, 40, 48, 174, 216


class Sched:
    def __init__(self, nc, es, ndma=64):
        self.nc = nc
        self.engs = {'pe': nc.tensor, 'act': nc.scalar, 'dve': nc.vector, 'pool': nc.gpsimd, 'sp': nc.sync}
        self.sem = {k: es.enter_context(nc.semaphore('s_' + k)) for k in self.engs}
        self.cnt = {k: 0 for k in self.engs}
        self.seen = {k: {} for k in self.engs}
        self.dsem = [es.enter_context(nc.semaphore('d%d' % i)) for i in range(ndma)]
        self.dval = [0] * ndma
        self.dnext = 0
        self.npool = 16
        self.pool_used = 0
        self.lastw = {}
        self.readers = {}
        self.nwait = 0
        self.pre_barrier = None

    def _wait(self, e, tok):
        kind, src, val = tok
        if kind == 'e' and src == e and e in ('pe', 'sp'):
            return
        key = (kind, src)
        if self.seen[e].get(key, 0) >= val:
            return
        sem = self.sem[src] if kind == 'e' else self.dsem[src]
        self.engs[e].wait_ge(sem, val)
        self.seen[e][key] = val
        self.nwait += 1

    def _deps(self, e, reads, writes):
        for k in reads:
            t = self.lastw.get(k)
            if t is not None:
                self._wait(e, t)
            if isinstance(k, tuple) and k[0] == 'bk':
                for (kind, src), val in self.readers.get(k, {}).items():
                    if src != e:
                        self._wait(e, (kind, src, val))
        for k in writes:
            t = self.lastw.get(k)
            if t is not None:
                self._wait(e, t)
            for (kind, src), val in self.readers.get(k, {}).items():
                self._wait(e, (kind, src, val))

    def _commit(self, tok, reads, writes):
        kind, src, val = tok
        for k in reads:
            d = self.readers.setdefault(k, {})
            if d.get((kind, src), 0) < val:
                d[(kind, src)] = val
        for k in writes:
            self.lastw[k] = tok
            self.readers[k] = {}

    def op(self, e, fn, reads=(), writes=(), signal=True):
        self._deps(e, reads, writes)
        ins = fn(self.engs[e])
        if signal:
            self.cnt[e] += 1
            ins.then_inc(self.sem[e], 1)
            tok = ('e', e, self.cnt[e])
        else:
            tok = ('e', e, self.cnt[e] + 1)
        self._commit(tok, reads, writes)

    def dma(self, q, out, in_, reads=(), writes=(), **kw):
        if q == 'pool':
            i = len(self.dsem) - 1 - self.pool_used
            self.pool_used += 1
            assert self.pool_used <= self.npool and self.dval[i] == 0
        else:
            i = self.dnext
            self.dnext = (self.dnext + 1) % (len(self.dsem) - self.npool)
        if self.dval[i] > 0:
            self._wait(q, ('d', i, self.dval[i]))
        self._deps(q, reads, writes)
        self.dval[i] += 16
        self.engs[q].dma_start(out=out, in_=in_, **kw).then_inc(self.dsem[i], 16)
        self._commit(('d', i, self.dval[i]), reads, writes)

    def barrier(self, skip_pool_dma=False):
        if self.pre_barrier is not None:
            self.pre_barrier()
        nhw = len(self.dsem) - self.npool
        for i, v in enumerate(self.dval):
            if v > 0 and not (skip_pool_dma and i >= nhw):
                self._wait('sp', ('d', i, v))
        self.cnt['sp'] += 1
        self.engs['sp'].sem_inc(self.sem['sp'], 1)
        ce = ('pe', 'act', 'dve', 'pool')
        for e in ce:
            for f in ce + ('sp',):
                if self.cnt[f] > 0:
                    self._wait(e, ('e', f, self.cnt[f]))
        for f in ce:
            if self.cnt[f] > 0:
                self._wait('sp', ('e', f, self.cnt[f]))
        keep = {}
        if skip_pool_dma:
            keep = {k: t for k, t in self.lastw.items() if t[0] == 'd' and t[1] >= nhw}
        self.lastw.clear()
        self.lastw.update(keep)
        self.readers.clear()

    def finish(self):
        for i, v in enumerate(self.dval):
            if v > 0:
                self._wait('sp', ('d', i, v))


class NS:
    pass


def build_program(do_sample=True, nblk=NBLK, do_prompt=True, dbg=False, stage=None):
    nc = bass.Bass("TRN2", target_bir_lowering=False)

    def din(name, shape, dt=F32):
        return nc.dram_tensor(name, list(shape), dt, kind="ExternalInput").ap()

    def dout(name, shape, dt=F32):
        return nc.dram_tensor(name, list(shape), dt, kind="ExternalOutput").ap()

    def dint(name, shape, dt):
        return nc.dram_tensor(name, list(shape), dt, kind="Internal").ap()

    xs = din("xs", [NBLK * 128, D])
    flag = din("flag", [128, 1])
    xq = din("xq", [64, D])
    ck = din("ck", [2, 512, 512])
    cv = din("cv", [2, 512, 512])
    hs0 = din("hs0", [2, 512, 128])
    cs0 = din("cs0", [6, D])
    fs0 = din("fs0", [4, 2 * DFF])
    w_in = din("w_in", [D, NPROJ])
    w_out = din("w_out", [D, D])
    w_up = din("w_up", [D, 2 * DFF])
    w_down = din("w_down", [DFF, D])
    biasT_d = din("biasT", [128, 2, 8, 128])
    biasN_d = din("biasN", [64, 8, 32])
    cbias_d = din("cbias", [1, 8])
    colp_d = din("colp", [128, CP_N])
    rowp_d = din("rowp", [1, 24])
    lnp_d = din("lnp", [1, 4 * D])
    consts_d = din("consts", [128, 8, 128])

    y_o = dout("y_o", [4096, D])
    ys_o = dout("ys_o", [64, D])
    nk_o = dout("nk_o", [512, 512])
    nv_o = dout("nv_o", [512, 512])
    nks_o = dout("nks_o", [64, 512])
    nvs_o = dout("nvs_o", [64, 512])
    ssm_o = dout("ssm_o", [512, 128])
    ssms_o = dout("ssms_o", [2, 512, 128])
    sc_o = dout("sc_o", [3, D])
    scs_o = dout("scs_o", [6, D])
    fc_o = dout("fc_o", [2, 2 * DFF])
    fcs_o = dout("fcs_o", [4, 2 * DFF])

    dbg_o = dout("dbg_o", [128, 16, D]) if dbg else None

    def dump(slot, ap, npart, w, key):
        if dbg:
            sch.dma('sp', dbg_o[0:npart, slot, 0:w], ap, reads=[key])

    wb_in = dint("wb_in", [D, NPROJ], BF16)
    wb_out = dint("wb_out", [D, D], BF16)
    wb_up = dint("wb_up", [D, 2 * DFF], BF16)
    wb_down = dint("wb_down", [DFF, D], BF16)

    es = ExitStack()
    with es:
        sch = Sched(nc, es)

        uid = {'n': 0}

        def sbuf(stack, name, shape, dt=F32):
            uid['n'] += 1
            return stack.enter_context(nc.sbuf_tensor("s%d_%s" % (uid['n'], name), list(shape), dt))

        banks = [es.enter_context(nc.psum_tensor("bk%d" % i, [128, 512], F32)) for i in range(8)]
        bstate = {'n': 0}

        def bank():
            nb_ = bstate.get('nb', 6)
            i = bstate['n'] % nb_
            bstate['n'] = (i + 1) % nb_
            return banks[i], ('bk', i)

        P = NS()
        P.Wi = sbuf(es, "Wi", [128, KT, NPROJ], BF16)
        P.Wo = sbuf(es, "Wo", [128, KT, D], BF16)
        P.consts = sbuf(es, "consts", [128, 8, 128])
        P.biasT = sbuf(es, "biasT", [128, 2, 8, 128])
        P.cbias = sbuf(es, "cbias", [128, 8])
        P.colp = sbuf(es, "colp", [128, CP_N])
        P.rowp = sbuf(es, "rowp", [128, 24])
        P.Aneg = sbuf(es, "Aneg", [128, 8])
        P.lnp = sbuf(es, "lnp", [128, 4, D])
        P.flg = sbuf(es, "flg", [128, 1])
        P.mhalf = sbuf(es, "mhalf", [128, 1])
        NKR = 5
        P.X1S = sbuf(es, "X1S", [128, 4, D])
        P.X1T = sbuf(es, "X1T", [128, KT, 512], BF16)
        P.XBC = sbuf(es, "XBC", [128, 8, 131])
        P.SMALL = sbuf(es, "SMALL", [128, 16])
        P.STAT = sbuf(es, "STAT", [128, 2, 6])
        P.MV = sbuf(es, "MV", [128, 2])
        P.TT = sbuf(es, "TT", [128, 128])
        P.TAIL = sbuf(es, "TAIL", [128, 96])
        ident = P.consts[:, 0, :]
        cT, cU, cOnes, cT2, cU2 = (P.consts[:, i, :] for i in (1, 2, 3, 4, 5))
        Wi, Wo, colp, rowp, lnp, flg, mhalf, SMALL, STAT, MV = P.Wi, P.Wo, P.colp, P.rowp, P.lnp, P.flg, P.mhalf, P.SMALL, P.STAT, P.MV

        def bc(ap, shape, axis):
            return ap.unsqueeze(axis).to_broadcast(list(shape))

        def mixer_bufs(st):
            M = NS()
            M.X = [sbuf(st, "X%d" % i, [128, D]) for i in range(2)]
            M.XT = [sbuf(st, "XT%d" % i, [128, KT, 128], BF16) for i in range(2)]
            M.ACC = sbuf(st, "ACC", [128, 8, 128])
            M.SIL = sbuf(st, "SIL", [128, 8, 128])
            M.RHSA = sbuf(st, "RHSA", [128, 8, 128])
            M.BCT = sbuf(st, "BCT", [128, 4, 128], BF16)
            M.BK = sbuf(st, "BK", [128, 2, 128], BF16)
            M.QT = sbuf(st, "QT", [128, 4, 128], BF16)
            M.SZ = sbuf(st, "SZ", [128, 512])
            M.PT = [sbuf(st, "PT%d" % i, [128, 5, 128], BF16) for i in range(2)]
            M.SB = [sbuf(st, "SB%d" % i, [128, 2, 128]) for i in range(2)]
            M.DTA = sbuf(st, "DTA", [128, 4, 8])
            M.LM = sbuf(st, "LM", [128, 8, 128])
            M.EE = sbuf(st, "EE", [128, 8])
            M.CD = sbuf(st, "CD", [128, 2, 8])
            M.WDS = sbuf(st, "WDS", [128, 8])
            M.CBM = sbuf(st, "CBM", [128, 2, 128])
            M.MT = sbuf(st, "MT", [128, 8, 128], BF16)
            M.XSK = sbuf(st, "XSK", [128, 512])
            M.XDT = sbuf(st, "XDT", [128, 8, 64], BF16)
            M.XDD = sbuf(st, "XDD", [128, 8, 64], BF16)
            M.T1 = sbuf(st, "T1", [128, 512])
            M.T2 = sbuf(st, "T2", [128, 512])
            M.YY = sbuf(st, "YY", [128, 512])
            M.JB = sbuf(st, "JB", [128, 512], BF16)
            M.AST = sbuf(st, "AST", [128, KT, 128], BF16)
            for i in range(2):
                sch.op('pool', lambda e, i=i: e.memset(M.PT[i][:], 0.0), writes=[('PT', i)])
            return M

        def ffn_bufs(st):
            Fb = NS()
            Fb.Wup = [sbuf(st, "Wup%d" % i, [128, KT, 256], BF16) for i in range(3)]
            Fb.Wdn = [sbuf(st, "Wdn%d" % i, [128, NF, 256], BF16) for i in range(2)]
            Fb.GT = sbuf(st, "GT", [128, NF, 512], BF16)
            Fb.HEAD = [sbuf(st, "HEAD%d" % i, [128, 8]) for i in range(4)]
            Fb.FACC = [sbuf(st, "FACC%d" % i, [128, 512]) for i in range(4)]
            return Fb

        def cast_rows(dst, src, nrows, split, key, nparts, extra_reads=()):
            nb = nrows // 128
            per = (nb + nparts - 1) // nparts
            r = 0
            while r < nb:
                n = min(per, nb - r)
                s_ = src[r * 128:(r + n) * 128, :].rearrange("r (a b) -> r a b", a=split)
                d_ = dst[r * 128:(r + n) * 128, :].rearrange("r (a b) -> r a b", a=split)
                sch.dma('pool', d_, s_, reads=list(extra_reads), writes=[(key, i) for i in range(r, r + n)])
                r += n

        sch.dma('sp', P.consts[:], consts_d[:, :, :], writes=['consts'])
        sch.dma('sp', colp[:], colp_d[:, :], writes=['colp'])
        sch.dma('sp', flg[:], flag[:, :], writes=['flg'])
        sch.dma('sp', rowp[:], bass.AP(tensor=rowp_d.tensor, offset=0, ap=[[0, 128], [1, 24]]), writes=['rowp'])
        sch.dma('sp', P.cbias[:], bass.AP(tensor=cbias_d.tensor, offset=0, ap=[[0, 128], [1, 8]]), writes=['cbias'])
        with ExitStack() as st0:
            stg = [sbuf(st0, "WSTG%d" % i, [128, NPROJ]) for i in range(3)]
            for kt in range(KT):
                si = kt % 3
                sch.dma('sp', stg[si][:], w_in[kt * 128:(kt + 1) * 128, :], writes=[('WSTG', si)])
                if kt % 2 == 0:
                    sch.op('act', lambda e, kt=kt, si=si: e.activation(out=Wi[:, kt, :], in_=stg[si][:], func=AF.Copy), reads=[('WSTG', si)], writes=[('Wi', kt)])
                else:
                    sch.op('dve', lambda e, kt=kt, si=si: e.tensor_copy(out=Wi[:, kt, :], in_=stg[si][:]), reads=[('WSTG', si)], writes=[('Wi', kt)])
            sch.dma('sp', lnp[:].rearrange("p a b -> p (a b)"),
                    bass.AP(tensor=lnp_d.tensor, offset=0, ap=[[0, 128], [1, 4 * D]]), writes=['lnp'])
            sch.dma('sp', P.biasT[:], biasT_d[:, :, :, :], writes=['biasT'])
            for k4 in range(0, KT, 4):
                sch.dma('pool', Wo[:, k4:k4 + 4, :], w_out[k4 * 128:(k4 + 4) * 128, :].rearrange("(k p) n -> p k n", p=128),
                        reads=[('WSTG', (KT - 1) % 3)], writes=[('Wo', k4 + i_) for i_ in range(4)])
            cast_rows(wb_up, w_up, D, 6, 'wb_up', 4, extra_reads=[('WSTG', (KT - 1) % 3)])
            cast_rows(wb_down, w_down, DFF, 1, 'wb_down', 3, extra_reads=[('WSTG', (KT - 1) % 3)])
            sch.barrier(skip_pool_dma=True)
        WUPK = [('wb_up', r) for r in range(8)]
        WDNK = [('wb_down', r) for r in range(NF)]

        sch.op('act', lambda e: e.activation(out=P.Aneg[:], in_=rowp[:, 8:16], func=AF.Exp), reads=['rowp'], writes=['Aneg'])
        sch.op('dve', lambda e: e.tensor_scalar(out=P.Aneg[:], in0=P.Aneg[:], scalar1=-1.0, scalar2=None, op0=ALU.mult),
               reads=['Aneg'], writes=['Aneg'])
        sch.op('dve', lambda e: e.memset(mhalf[:], -0.5), writes=['mhalf'])
        sch.op('dve', lambda e: e.memset(P.XBC[:], 0.0), writes=['XBC'])

        if stage == 'prologue':
            sch.finish()
            return nc

        def transpose_to(dst_fn, src_fn, nparts, ntiles, reads, writes, post=None, engs=('act', 'dve')):
            j = 0
            gi = 0
            while j < ntiles:
                n = min(4, ntiles - j)
                bk, bkk = bank()
                for q in range(n):
                    sch.op('pe', lambda e, q=q, j=j, bk=bk: e.transpose(out=bk[:, q * nparts:(q + 1) * nparts],
                                                                        in_=src_fn(j + q), identity=ident[0:nparts, 0:nparts]),
                           reads=list(reads) + ['consts'], writes=[bkk], signal=(q == n - 1))
                src = bk[:, 0:n * nparts].rearrange("p (a b) -> p a b", a=n)
                if post is not None:
                    post(j, n, src, bkk)
                elif engs[gi % 2] == 'act':
                    sch.op('act', lambda e, j=j, n=n, src=src: e.activation(out=dst_fn(j, n), in_=src, func=AF.Copy),
                           reads=[bkk], writes=writes)
                else:
                    sch.op('dve', lambda e, j=j, n=n, src=src: e.tensor_copy(out=dst_fn(j, n), in_=src),
                           reads=[bkk], writes=writes)
                j += n
                gi += 1

        def rstd_from_sumsq(ss_ap, n_feat, npart, key):
            sch.op('pool', lambda e: e.tensor_scalar(out=ss_ap, in0=ss_ap, scalar1=1.0 / n_feat, scalar2=EPS,
                                                     op0=ALU.mult, op1=ALU.add), reads=[key], writes=[key])
            sch.op('pool', lambda e: e.tensor_tensor(out=ss_ap, in0=ss_ap, in1=mhalf[0:npart, :], op=ALU.pow),
                   reads=[key, 'mhalf'], writes=[key])

        def layer_norm(src_ap, dst_ap, NT, gi, src_key, dst_key):
            for c in range(2):
                sch.op('dve', lambda e, c=c: e.bn_stats(out=STAT[0:NT, c, :], in_=src_ap[:, c * 512:(c + 1) * 512]),
                       reads=[src_key], writes=['STAT'])
            sch.op('dve', lambda e: e.bn_aggr(out=MV[0:NT, :], in_=STAT[0:NT, :, :].rearrange("p a b -> p (a b)")),
                   reads=['STAT'], writes=['MV'])
            sch.op('pool', lambda e: e.tensor_scalar(out=MV[0:NT, 1:2], in0=MV[0:NT, 1:2], scalar1=1.0, scalar2=EPS,
                                                     op0=ALU.mult, op1=ALU.add), reads=['MV'], writes=['MV'])
            sch.op('pool', lambda e: e.tensor_tensor(out=MV[0:NT, 1:2], in0=MV[0:NT, 1:2], in1=mhalf[0:NT, :], op=ALU.pow),
                   reads=['MV', 'mhalf'], writes=['MV'])
            sch.op('dve', lambda e: e.tensor_scalar(out=dst_ap, in0=src_ap, scalar1=MV[0:NT, 0:1], scalar2=MV[0:NT, 1:2],
                                                    op0=ALU.subtract, op1=ALU.mult), reads=[src_key, 'MV'], writes=[dst_key])
            for hf_, eng in ((0, 'dve'), (1, 'pool')):
                cs_ = slice(hf_ * 512, (hf_ + 1) * 512)
                sch.op(eng, lambda e, cs_=cs_: e.tensor_tensor(out=dst_ap[:, cs_], in0=dst_ap[:, cs_], in1=lnp[0:NT, gi, cs_], op=ALU.mult),
                       reads=[dst_key, 'lnp'], writes=[(dst_key, hf_)])
                sch.op(eng, lambda e, cs_=cs_: e.tensor_tensor(out=dst_ap[:, cs_], in0=dst_ap[:, cs_], in1=lnp[0:NT, gi + 1, cs_], op=ALU.add),
                       reads=[(dst_key, hf_), 'lnp'], writes=[(dst_key, hf_)])

        deferred = []

        def flush_deferred():
            while deferred:
                deferred.pop(0)()

        sch.pre_barrier = flush_deferred

        def mixer_block(M, kind, NT, nseg, L, x_src, xi, xbc_t, xbc_key, HFs, attn_tiles, outs, kv_dst, j_in_super):
            bstate['nb'] = 6
            full = kind == 'full'
            Xb = M.X[xi]
            XTb = M.XT[xi]
            xk, xtk = ('X', xi), ('XT', xi)
            Tm = cT if nseg == 1 else cT2
            Um = cU if nseg == 1 else cU2
            DTA, LM, SIL, ACC, BCT = M.DTA, M.LM, M.SIL, M.ACC, M.BCT
            sch.dma('sp', Xb[0:NT, :], x_src, writes=[xk])
            transpose_to(lambda j, n: XTb[:, j:j + n, 0:NT], lambda j: Xb[0:NT, j * 128:(j + 1) * 128], NT, KT,
                         reads=[xk], writes=[xtk], engs=('act', 'act'))

            def proj_fm(c0, ntiles, evac):
                j = 0
                while j < ntiles:
                    n = min(4, ntiles - j)
                    bk, bkk = bank()
                    for q in range(n):
                        cc = c0 + (j + q) * 128
                        for kt in range(KT):
                            sch.op('pe', lambda e, q=q, kt=kt, cc=cc, bk=bk: e.matmul(bk[:, q * NT:(q + 1) * NT], lhsT=Wi[:, kt, cc:cc + 128],
                                                                                       rhs=XTb[:, kt, 0:NT], start=(kt == 0), stop=(kt == KT - 1)),
                                   reads=[xtk, ('Wi', kt)], writes=[bkk], signal=(q == n - 1 and kt == KT - 1))
                    evac(bk, bkk, j, n)
                    j += n

            def proj_tm(c0, ncols, bk, bkk):
                for kt in range(KT):
                    sch.op('pe', lambda e, kt=kt: e.matmul(bk[0:NT, 0:ncols], lhsT=XTb[:, kt, 0:NT], rhs=Wi[:, kt, c0:c0 + ncols],
                                                           start=(kt == 0), stop=(kt == KT - 1)),
                           reads=[xtk, ('Wi', kt)], writes=[bkk], signal=(kt == KT - 1))

            bkd, bkdk = bank()
            proj_tm(C_DT, 8, bkd, bkdk)
            u_, au, dt_, a_ = (DTA[0:NT, i, :] for i in range(4))
            sch.op('dve', lambda e: e.tensor_tensor(out=u_, in0=bkd[0:NT, 0:8], in1=rowp[0:NT, 0:8], op=ALU.add),
                   reads=[bkdk, 'rowp'], writes=['DTA'])
            sch.op('act', lambda e: e.activation(out=au, in_=u_, func=AF.Abs), reads=['DTA'], writes=['DTA'])
            sch.op('act', lambda e: e.activation(out=au, in_=au, func=AF.Exp, scale=-1.0), reads=['DTA'], writes=['DTA'])
            sch.op('act', lambda e: e.activation(out=au, in_=au, func=AF.Ln, bias=1.0, scale=1.0), reads=['DTA'], writes=['DTA'])
            sch.op('dve', lambda e: e.scalar_tensor_tensor(out=dt_, in0=u_, scalar=0.0, in1=au, op0=ALU.max, op1=ALU.add),
                   reads=['DTA'], writes=['DTA'])
            sch.op('dve', lambda e: e.tensor_tensor(out=a_, in0=dt_, in1=P.Aneg[0:NT, :], op=ALU.mult), reads=['DTA', 'Aneg'], writes=['DTA'])

            RHSA = M.RHSA
            for h in range(8):
                sch.op('pool', lambda e, h=h: e.tensor_scalar(out=RHSA[0:NT, h, 0:NT], in0=Tm[0:NT, 0:NT], scalar1=DTA[0:NT, 3, h:h + 1],
                                                              scalar2=0.0, op0=ALU.mult, op1=ALU.add), reads=['DTA', 'consts'], writes=[('RHSA', h)])
            def evac_xbc(bk, bkk, j, n):
                src = bk[:, 0:n * NT].rearrange("p (a s l) -> p a s l", a=n, s=nseg)
                sch.op('act', lambda e: e.activation(out=xbc_t[:, j:j + n, :, 3:3 + L], in_=src, func=AF.Copy),
                       reads=[bkk], writes=[xbc_key])
            proj_fm(C_XBC, 8, evac_xbc)

            if outs.get('sconv') is not None:
                nr = 3 * nseg
                tl = P.TAIL[:, 0:8 * nr].rearrange("p (t s r) -> p t s r", t=8, s=nseg)
                sch.op('pool', lambda e: e.tensor_copy(out=tl, in_=xbc_t[:, :, :, L:L + 3]), reads=[xbc_key], writes=['TAIL'])
                bkt, bktk = bank()
                sch.op('pe', lambda e: e.transpose(out=bkt[0:8 * nr, 0:128], in_=P.TAIL[:, 0:8 * nr], identity=ident),
                       reads=['TAIL', 'consts'], writes=[bktk])
                sch.op('act', lambda e: e.activation(out=P.TT[0:8 * nr, :], in_=bkt[0:8 * nr, 0:128], func=AF.Copy),
                       reads=[bktk], writes=['TT'])
                for t in range(8):
                    sch.dma('sp', outs['sconv'][:, t * 128:(t + 1) * 128], P.TT[t * nr:(t + 1) * nr, :], reads=['TT'])

            for jj in (3, 0, 1, 2):
                for t in range(8):
                    cw = CP_CW + 4 * t
                    acc = ACC[:, t, 0:NT].rearrange("p (s l) -> p s l", s=nseg)
                    if jj == 3:
                        sch.op('dve', lambda e, t=t, cw=cw, acc=acc: e.tensor_scalar(out=acc, in0=xbc_t[:, t, :, 3:3 + L], scalar1=colp[:, cw + 3:cw + 4],
                                                                                     scalar2=colp[:, CP_CB + t:CP_CB + t + 1], op0=ALU.mult, op1=ALU.add),
                               reads=[xbc_key, 'colp'], writes=[('ACC', t)])
                    else:
                        sch.op('dve', lambda e, t=t, cw=cw, acc=acc, jj=jj: e.scalar_tensor_tensor(out=acc, in0=xbc_t[:, t, :, jj:jj + L],
                                                                                                  scalar=colp[:, cw + jj:cw + jj + 1], in1=acc,
                                                                                                  op0=ALU.mult, op1=ALU.add),
                               reads=[xbc_key, 'colp', ('ACC', t)], writes=[('ACC', t)])

            if kind != 'ssm':
                kt_dst, kt_key, va_dst, va_key = kv_dst

                def evac_k(bk, bkk, j, n):
                    src = bk[:, 0:n * NT].rearrange("p (a b) -> p a b", a=n)
                    sch.op('act', lambda e: e.activation(out=kt_dst(j, n), in_=src, func=AF.Copy), reads=[bkk], writes=[kt_key])
                proj_fm(C_K, 4, evac_k)
                bkv, bkvk = bank()
                proj_tm(C_V, 512, bkv, bkvk)
                sch.op('act', lambda e: e.activation(out=va_dst[:, :, 0:64], in_=bkv[0:NT, :].rearrange("p (h d) -> p h d", h=8),
                                                     func=AF.Copy), reads=[bkvk], writes=[va_key])
                if kind == 'kv':
                    sch.op('pool', lambda e: e.tensor_copy(out=va_dst[:, :, 64], in_=flg[0:NT, 0:1].to_broadcast([NT, 8])),
                           reads=['flg'], writes=[va_key])
                else:
                    sch.op('pool', lambda e: e.memset(va_dst[:, :, 64], 1.0), writes=[va_key])
                if outs.get('nv') is not None:
                    sch.op('act', lambda e: e.activation(out=M.T1[0:NT, :], in_=bkv[0:NT, :], func=AF.Copy), reads=[bkvk], writes=['T1'])
                    sch.dma('sp', outs['nv'], M.T1[0:NT, :], reads=['T1'])
                if outs.get('nk') is not None:
                    bkk_, bkkk = bank()
                    proj_tm(C_K, 512, bkk_, bkkk)
                    sch.op('act', lambda e: e.activation(out=M.T2[0:NT, :], in_=bkk_[0:NT, :], func=AF.Copy), reads=[bkkk], writes=['T2'])
                    sch.dma('sp', outs['nk'], M.T2[0:NT, :], reads=['T2'])

            if full:
                def evac_q(bk, bkk, j, n):
                    src = bk[:, 0:n * NT].rearrange("p (a b) -> p a b", a=n)
                    sch.op('act', lambda e: e.activation(out=M.QT[:, j:j + n, 0:NT], in_=src, func=AF.Copy), reads=[bkk], writes=['QT'])
                proj_fm(C_Q, 4, evac_q)
                bkz, bkzk = bank()
                proj_tm(C_Z, 512, bkz, bkzk)
                sch.op('act', lambda e: e.activation(out=M.SZ[0:NT, :], in_=bkz[0:NT, :], func=AF.Silu), reads=[bkzk], writes=['SZ'])

            step_attn = lambda n: None
            if full:
                ob = [(banks[6], ('bk', 6)), (banks[7], ('bk', 7))]
                units = [(h, s_) for h in range(8) for s_ in range(nseg)]

                def attn_front(ui):
                    h, s = units[ui]
                    hp, po = h // 2, (h % 2) * 64
                    pti = ui % 2
                    PTb, ptk, SBb, sbk = M.PT[pti], ('PT', pti), M.SB[pti], ('SB', pti)
                    tiles = attn_tiles(s, h)
                    ctiles = [t_ for t_ in tiles if t_['bias'] is None]
                    btiles = [t_ for t_ in tiles if t_['bias'] is not None]
                    nc_ = len(ctiles)
                    bkA, bkAk = bank()
                    bkB, bkBk = bank()
                    for lst, bk, bkk in ((ctiles, bkA, bkAk), (btiles, bkB, bkBk)):
                        for ii, t_ in enumerate(lst):
                            nk, pb = t_['nk'], t_['pb']
                            sch.op('pe', lambda e, t_=t_, ii=ii, bk=bk, nk=nk, pb=pb, s=s: e.matmul(
                                bk[pb:pb + nk, ii * L:(ii + 1) * L], lhsT=t_['kt'], rhs=M.QT[po:po + 64, hp, s * L:(s + 1) * L], start=True, stop=True),
                                reads=[t_['ktkey'], 'QT'], writes=[bkk], signal=(ii == len(lst) - 1))
                    sch.op('act', lambda e, bkA=bkA, PTb=PTb, h=h: e.activation(
                        out=PTb[:, 0:nc_, 0:L], in_=bkA[:, 0:nc_ * L].rearrange("p (a l) -> p a l", a=nc_), func=AF.Exp,
                        bias=P.cbias[:, h:h + 1], scale=0.125), reads=[bkAk, 'cbias'], writes=[ptk])
                    for ii, t_ in enumerate(btiles):
                        nk, pb = t_['nk'], t_['pb']
                        sch.op('dve', lambda e, t_=t_, ii=ii, nk=nk, pb=pb, bkB=bkB, SBb=SBb: e.scalar_tensor_tensor(
                            out=SBb[pb:pb + nk, ii, 0:L], in0=bkB[pb:pb + nk, ii * L:(ii + 1) * L], scalar=0.125, in1=t_['bias'],
                            op0=ALU.mult, op1=ALU.add), reads=[bkBk, 'biasT', 'biasN'], writes=[sbk])
                        sch.op('act', lambda e, ii=ii, nk=nk, pb=pb, SBb=SBb, PTb=PTb: e.activation(
                            out=PTb[pb:pb + nk, nc_ + ii, 0:L], in_=SBb[pb:pb + nk, ii, 0:L], func=AF.Exp), reads=[sbk], writes=[ptk])
                    alltiles = ctiles + btiles
                    for ii, t_ in enumerate(alltiles):
                        for (p0, p1, q0, q1) in t_.get('zero', ()):
                            sch.op('pool', lambda e, ii=ii, p0=p0, p1=p1, q0=q0, q1=q1, PTb=PTb: e.memset(PTb[p0:p1, ii, q0:q1], 0.0),
                                   reads=[ptk], writes=[ptk])
                    return alltiles

                def attn_back(ui, alltiles):
                    h, s = units[ui]
                    bo, bok = ob[h // 4]
                    pti = ui % 2
                    PTb, ptk = M.PT[pti], ('PT', pti)
                    for ii, t_ in enumerate(alltiles):
                        nk, pb = t_['nk'], t_['pb']
                        sch.op('pe', lambda e, t_=t_, ii=ii, nk=nk, pb=pb, PTb=PTb, bo=bo, s=s, h=h: e.matmul(
                            bo[s * L:(s + 1) * L, (h % 4) * 65:(h % 4) * 65 + 65], lhsT=PTb[pb:pb + nk, ii, 0:L], rhs=t_['va'],
                            start=(ii == 0), stop=(ii == len(alltiles) - 1)),
                            reads=[ptk, t_['vakey']], writes=[bok], signal=(ii == len(alltiles) - 1))

                def attn_gen():
                    pend = attn_front(0)
                    for ui in range(len(units)):
                        nxt = attn_front(ui + 1) if ui + 1 < len(units) else None
                        attn_back(ui, pend)
                        pend = nxt
                        yield
                agen = [attn_gen()]

                def step_attn(n):
                    for _ in range(n * nseg):
                        try:
                            next(agen[0])
                        except StopIteration:
                            return

            while deferred:
                deferred.pop(0)()
            hpb = 512 // NT
            for g0 in range(0, 8, hpb):
                bk, bkk = bank()
                sch.op('pe', lambda e, bk=bk, g0=g0: e.matmul(bk[0:NT, 0:hpb * NT], lhsT=Um[0:NT, 0:NT],
                                                              rhs=RHSA[0:NT, g0:g0 + hpb, 0:NT], start=True, stop=True),
                       reads=[('RHSA', t_) for t_ in range(8)] + ['consts'], writes=[bkk])
                sch.op('act', lambda e, bk=bk, g0=g0: e.activation(out=LM[0:NT, g0:g0 + hpb, 0:NT],
                                                                  in_=bk[0:NT, 0:hpb * NT].rearrange("p (h l) -> p h l", h=hpb), func=AF.Exp),
                       reads=[bkk], writes=['LM'])
            bke, bkek = bank()
            sch.op('pe', lambda e: e.matmul(bke[0:NT, 0:8], lhsT=Tm[0:NT, 0:NT], rhs=DTA[0:NT, 3, :], start=True, stop=True),
                   reads=['DTA', 'consts'], writes=[bkek], signal=False)
            for s in range(nseg):
                osm = cOnes if nseg == 1 else P.consts[:, 6 + s, :]
                sch.op('pe', lambda e, s=s, osm=osm: e.matmul(bke[:, 8 + 8 * s:16 + 8 * s], lhsT=osm[0:NT, :], rhs=DTA[0:NT, 3, :], start=True, stop=True),
                       reads=['DTA', 'consts'], writes=[bkek], signal=(s == nseg - 1))
            sch.op('act', lambda e: e.activation(out=M.EE[0:NT, :], in_=bke[0:NT, 0:8], func=AF.Exp), reads=[bkek], writes=['EE'])
            sch.op('act', lambda e: e.activation(out=M.CD[:, 0:nseg, :], in_=bke[:, 8:8 + 8 * nseg].rearrange("p (s h) -> p s h", s=nseg), func=AF.Exp),
                   reads=[bkek], writes=['CD'])
            for s in range(nseg):
                sch.op('dve', lambda e, s=s: e.tensor_tensor(out=M.WDS[s * L:(s + 1) * L, :], in0=DTA[s * L:(s + 1) * L, 2, :],
                                                             in1=LM[s * L:(s + 1) * L, :, (s + 1) * L - 1], op=ALU.mult),
                       reads=['DTA', 'LM'], writes=['WDS'])

            sch.op('act', lambda e: e.activation(out=SIL[:, :, 0:NT], in_=ACC[:, :, 0:NT], func=AF.Silu), reads=[('ACC', t_) for t_ in range(8)], writes=['SIL'])
            sch.op('act', lambda e: e.activation(out=BCT[:, :, 0:NT], in_=ACC[:, 4:8, 0:NT], func=AF.Silu), reads=[('ACC', t_) for t_ in range(4, 8)], writes=['BCT'])
            step_attn(2)

            bkx, bkxk = bank()
            for q in range(4):
                sch.op('pe', lambda e, q=q: e.transpose(out=bkx[0:NT, q * 128:(q + 1) * 128], in_=SIL[:, q, 0:NT], identity=ident),
                       reads=['SIL', 'consts'], writes=[bkxk], signal=(q == 3))
            xsv = bkx[0:NT, :].rearrange("p (h d) -> p h d", h=8)
            if full:
                sch.op('act', lambda e: e.activation(out=M.XSK[0:NT, :], in_=bkx[0:NT, :], func=AF.Copy), reads=[bkxk], writes=['XSK'])
                sch.op('dve', lambda e: e.tensor_tensor(out=M.XDT[0:NT, :, :], in0=xsv, in1=bc(DTA[0:NT, 2, :], [NT, 8, 64], 2), op=ALU.mult),
                       reads=[bkxk, 'DTA'], writes=['XDT'])
            sch.op('dve', lambda e: e.tensor_tensor(out=M.XDD[0:NT, :, :], in0=xsv, in1=bc(M.WDS[0:NT, :], [NT, 8, 64], 2), op=ALU.mult),
                   reads=[bkxk, 'WDS'], writes=['XDD'])
            step_attn(1)
            bkb, bkbk = bank()
            for g in range(2):
                sch.op('pe', lambda e, g=g: e.transpose(out=bkb[0:NT, g * 128:(g + 1) * 128], in_=SIL[:, 4 + g, 0:NT], identity=ident),
                       reads=['SIL', 'consts'], writes=[bkbk], signal=(g == 1))
            sch.op('act', lambda e: e.activation(out=M.BK[0:NT, :, :], in_=bkb[0:NT, 0:256].rearrange("p (g n) -> p g n", g=2), func=AF.Copy),
                   reads=[bkbk], writes=['BK'])
            step_attn(1)

            if full:
                bkc, bkck = bank()
                for g in range(2):
                    sch.op('pe', lambda e, g=g: e.matmul(bkc[0:NT, g * NT:(g + 1) * NT], lhsT=BCT[:, g, 0:NT], rhs=BCT[:, 2 + g, 0:NT],
                                                         start=True, stop=True), reads=['BCT'], writes=[bkck], signal=(g == 1))
                sch.op('dve', lambda e: e.tensor_tensor(out=M.CBM[0:NT, :, 0:NT], in0=bkc[0:NT, 0:2 * NT].rearrange("p (g l) -> p g l", g=2),
                                                        in1=bc(Tm[0:NT, 0:NT], [NT, 2, NT], 1), op=ALU.mult),
                       reads=[bkck, 'consts'], writes=['CBM'])
                for g in range(2):
                    sch.op(('dve', 'pool')[g], lambda e, g=g: e.tensor_tensor(out=M.MT[0:NT, 4 * g:4 * g + 4, 0:NT], in0=LM[0:NT, 4 * g:4 * g + 4, 0:NT],
                                                                  in1=bc(M.CBM[0:NT, g, 0:NT], [NT, 4, NT], 1), op=ALU.mult),
                           reads=['LM', 'CBM'], writes=[('MT', g)])
                bko, bkok = bank()
                for s in range(nseg):
                    _, HBs, hk = HFs[s]
                    for g in range(2):
                        sch.op('pe', lambda e, s=s, g=g, HBs=HBs: e.matmul(bko[s * L:(s + 1) * L, g * 256:(g + 1) * 256], lhsT=BCT[:, 2 + g, s * L:(s + 1) * L],
                                                                           rhs=HBs[:, g * 256:(g + 1) * 256], start=True, stop=True),
                               reads=['BCT', ('HB', hk)], writes=[bkok], signal=(s == nseg - 1 and g == 1))
                bky, bkyk = bank()
                for h in range(8):
                    sch.op('pe', lambda e, h=h: e.matmul(bky[0:NT, h * 64:(h + 1) * 64], lhsT=M.MT[0:NT, h, 0:NT], rhs=M.XDT[0:NT, h, :],
                                                         start=True, stop=True), reads=[('MT', h // 4), 'XDT'], writes=[bkyk], signal=(h == 7))

            for s in range(nseg):
                HFt, HBt, hk = HFs[s]
                bks, bksk = bank()
                for g in range(2):
                    sch.op('pe', lambda e, s=s, g=g, bks=bks: e.matmul(bks[:, g * 256:(g + 1) * 256], lhsT=M.BK[s * L:(s + 1) * L, g, :],
                                                                       rhs=M.XDD[s * L:(s + 1) * L, 4 * g:4 * g + 4, :], start=True, stop=True),
                           reads=['BK', 'XDD'], writes=[bksk], signal=(g == 1))
                hv = HFt[:].rearrange("p (h d) -> p h d", h=8)
                sch.op('pool', lambda e, s=s, hv=hv: e.tensor_tensor(out=hv, in0=hv, in1=bc(M.CD[:, s, :], [128, 8, 64], 2), op=ALU.mult),
                       reads=[('HF', hk), 'CD', ('HB', hk)], writes=[('HF', hk)])
                sch.op('dve', lambda e, HFt=HFt, bks=bks: e.tensor_tensor(out=HFt[:], in0=HFt[:], in1=bks[:, 0:512], op=ALU.add),
                       reads=[('HF', hk), bksk], writes=[('HF', hk)])
                if outs.get('flag_state'):
                    sch.op('dve', lambda e, HFt=HFt: e.tensor_scalar(out=HFt[:], in0=HFt[:], scalar1=flg[:, 0:1], scalar2=None, op0=ALU.mult),
                           reads=[('HF', hk), 'flg'], writes=[('HF', hk)])
                sch.op('act', lambda e, HFt=HFt, HBt=HBt: e.activation(out=HBt[:], in_=HFt[:], func=AF.Copy), reads=[('HF', hk)], writes=[('HB', hk)])

            if not full:
                return

            T1, T2, YY, AST = M.T1, M.T2, M.YY, M.AST
            v8 = lambda ap: ap.rearrange("p (h d) -> p h d", h=8)
            sch.op('dve', lambda e: e.tensor_tensor(out=v8(T1[0:NT, :]), in0=v8(bko[0:NT, :]), in1=bc(M.EE[0:NT, :], [NT, 8, 64], 2), op=ALU.mult),
                   reads=[bkok, 'EE'], writes=['T1'])
            sch.op('pool', lambda e: e.tensor_tensor(out=v8(T2[0:NT, :]), in0=v8(M.XSK[0:NT, :]), in1=bc(rowp[0:NT, 16:24], [NT, 8, 64], 2), op=ALU.mult),
                   reads=['XSK', 'rowp'], writes=['T2'])
            sch.op('dve', lambda e: e.tensor_tensor(out=T2[0:NT, :], in0=T2[0:NT, :], in1=T1[0:NT, :], op=ALU.add), reads=['T1', 'T2'], writes=['T2'])
            sch.op('dve', lambda e: e.tensor_tensor(out=YY[0:NT, :], in0=bky[0:NT, :], in1=T2[0:NT, :], op=ALU.add), reads=[bkyk, 'T2'], writes=['YY'])
            dump(0, YY[0:NT, :], NT, 512, 'YY')
            dump(1, M.SZ[0:NT, :], NT, 512, 'SZ')
            dump(5, T1[0:NT, :], NT, 512, 'T1')
            dump(6, M.XSK[0:NT, :], NT, 512, 'XSK')
            dump(7, M.EE[0:NT, :], NT, 8, 'EE')
            dump(8, M.DTA[0:NT, :, :].rearrange("p a b -> p (a b)"), NT, 32, 'DTA')
            sch.op('dve', lambda e: e.tensor_tensor(out=YY[0:NT, :], in0=YY[0:NT, :], in1=M.SZ[0:NT, :], op=ALU.mult), reads=['YY', 'SZ'], writes=['YY'])
            step_attn(2)
            sch.op('act', lambda e: e.activation(out=M.JB[0:NT, :], in_=YY[0:NT, :], func=AF.Square, accum_out=SMALL[0:NT, 0:1]),
                   reads=['YY'], writes=['JB', ('SM', 0)])
            rstd_from_sumsq(SMALL[0:NT, 0:1], 512.0, NT, ('SM', 0))
            sch.op('dve', lambda e: e.tensor_scalar(out=YY[0:NT, :], in0=YY[0:NT, :], scalar1=SMALL[0:NT, 0:1], scalar2=None, op0=ALU.mult),
                   reads=['YY', ('SM', 0)], writes=['YY'])

            def post_s(j, n, src, bkk):
                sch.op('dve', lambda e: e.tensor_tensor(out=AST[:, 4:8, 0:NT], in0=src, in1=bc(colp[:, CP_SG:CP_SG + 4], [128, 4, NT], 2), op=ALU.mult),
                       reads=[bkk, 'colp'], writes=['AST'])
            transpose_to(None, lambda j: YY[0:NT, j * 128:(j + 1) * 128], NT, 4, reads=['YY'], writes=['AST'], post=post_s)

            step_attn(1000)
            AA = T1
            AA3 = AA[0:NT, :].rearrange("p (h d) -> p h d", h=8)
            if dbg:
                sch.op('act', lambda e: e.activation(out=T2[0:NT, 0:260], in_=ob[0][0][0:NT, 0:260], func=AF.Copy), reads=[ob[0][1]], writes=['T2'])
                dump(9, T2[0:NT, 0:260], NT, 260, 'T2')
                sch.op('act', lambda e: e.activation(out=T2[:, 0:320].rearrange("p (a b) -> p a b", a=5), in_=M.PT[1][:, :, 0:64], func=AF.Copy), reads=[('PT', 1)], writes=['T2'])
                dump(10, T2[:, 0:320], 128, 320, 'T2')
            for hh in range(2):
                bo, bok = ob[hh]
                ov = bo[0:NT, 0:260].rearrange("p (h d) -> p h d", h=4)
                sch.op('dve', lambda e, ov=ov, hh=hh: e.reciprocal(out=SMALL[0:NT, 8 + 4 * hh:12 + 4 * hh], in_=ov[:, :, 64]),
                       reads=[bok], writes=[('SM', 1 + hh)])
                sch.op('dve', lambda e, ov=ov, hh=hh: e.tensor_tensor(out=AA3[:, 4 * hh:4 * hh + 4, :], in0=ov[:, :, 0:64],
                                                                      in1=bc(SMALL[0:NT, 8 + 4 * hh:12 + 4 * hh], [NT, 4, 64], 2), op=ALU.mult),
                       reads=[bok, ('SM', 1 + hh)], writes=['T1'])
            AAf = AA[0:NT, :]
            dump(2, AAf, NT, 512, 'T1')
            sch.op('act', lambda e: e.activation(out=M.JB[0:NT, :], in_=AAf, func=AF.Square, accum_out=SMALL[0:NT, 1:2]),
                   reads=['T1'], writes=['JB', ('SM', 3)])
            rstd_from_sumsq(SMALL[0:NT, 1:2], 512.0, NT, ('SM', 3))
            sch.op('dve', lambda e: e.tensor_scalar(out=AAf, in0=AAf, scalar1=SMALL[0:NT, 1:2], scalar2=None, op0=ALU.mult),
                   reads=['T1', ('SM', 3)], writes=['T1'])

            def post_a(j, n, src, bkk):
                sch.op('dve', lambda e: e.tensor_tensor(out=AST[:, 0:4, 0:NT], in0=src, in1=bc(colp[:, CP_AG:CP_AG + 4], [128, 4, NT], 2), op=ALU.mult),
                       reads=[bkk, 'colp'], writes=['AST'])
            transpose_to(None, lambda j: AAf[:, j * 128:(j + 1) * 128], NT, 4, reads=['T1'], writes=['AST'], post=post_a)

            for c in range(2):
                bk, bkk = bank()
                for kt in range(KT):
                    sch.op('pe', lambda e, kt=kt, c=c, bk=bk: e.matmul(bk[0:NT, :], lhsT=AST[:, kt, 0:NT], rhs=Wo[:, kt, c * 512:(c + 1) * 512],
                                                                       start=(kt == 0), stop=(kt == KT - 1)),
                           reads=['AST', ('Wo', kt)], writes=[bkk], signal=(kt == KT - 1))
                sch.op('dve', lambda e, c=c, bk=bk: e.scalar_tensor_tensor(out=Xb[0:NT, c * 512:(c + 1) * 512], in0=Xb[0:NT, c * 512:(c + 1) * 512],
                                                                           scalar=ALPHA, in1=bk[0:NT, :], op0=ALU.mult, op1=ALU.add),
                       reads=[xk, bkk], writes=[xk])
            x1 = P.X1S[0:NT, j_in_super, :]
            dump(3, Xb[0:NT, :], NT, D, xk)
            layer_norm(Xb[0:NT, :], x1, NT, 0, xk, ('X1S', j_in_super))
            dump(4, x1, NT, D, ('X1S', j_in_super))
            t0 = j_in_super * 128
            def x1T_later():
                transpose_to(lambda j, n: P.X1T[:, j:j + n, t0:t0 + NT], lambda j: x1[:, j * 128:(j + 1) * 128], NT, KT,
                             reads=[('X1S', j_in_super), (('X1S', j_in_super), 0), (('X1S', j_in_super), 1)], writes=[('X1T', j_in_super)], engs=('act', 'act'))
            deferred.append(x1T_later)

        def ffn(Fb, NTs, nseg, L, jlist, ft_t, ft_key, y_dsts, fconv_out, tails_only, defer_ln2=False):
            bstate['nb'] = 8
            x1keys = [('X1T', j) for j in range(4)]
            pend_gate = [None]
            for f in range(NF):
                ui = f % 3
                sch.dma('sp', Fb.Wup[ui][:, :, :], wb_up[:, f * 256:(f + 1) * 256].rearrange("(kt p) n -> p kt n", p=128),
                        reads=WUPK, writes=[('Wup', ui)])
                hbs = []
                for part in range(2):
                    fp = part * NF + f
                    bk, bkk = bank()
                    for kt in range(KT):
                        sch.op('pe', lambda e, kt=kt, bk=bk, ui=ui, part=part: e.matmul(bk[:, 0:NTs], lhsT=Fb.Wup[ui][:, kt, part * 128:(part + 1) * 128],
                                                                                        rhs=P.X1T[:, kt, 0:NTs], start=(kt == 0), stop=(kt == KT - 1)),
                               reads=x1keys + [('Wup', ui)], writes=[bkk], signal=(kt == KT - 1))
                    hi_ = part * 2 + (f % 2)
                    ps3 = bk[:, 0:NTs].rearrange("p (s l) -> p s l", s=nseg)
                    hd = Fb.HEAD[hi_][:, 0:nseg * 4].rearrange("p (s c) -> p s c", s=nseg)
                    hdk = ('HEAD', hi_)
                    if not tails_only:
                        sch.op('pool', lambda e, hd=hd, fp=fp: e.tensor_copy(out=hd[:, :, 0:2], in_=ft_t[:, fp, :, :]), reads=[(ft_key, fp)], writes=[hdk])
                        sch.op('act', lambda e, hd=hd, ps3=ps3: e.activation(out=hd[:, :, 2:4], in_=ps3[:, :, 0:2], func=AF.Copy), reads=[bkk], writes=[hdk])
                    sch.op('act', lambda e, ps3=ps3, fp=fp: e.activation(out=ft_t[:, fp, :, :], in_=ps3[:, :, L - 2:L], func=AF.Copy),
                           reads=[bkk], writes=[(ft_key, fp)])
                    if not tails_only:
                        cw = CP_FW + 3 * fp
                        sch.op('act', lambda e, bk=bk, hi_=hi_, cw=cw, fp=fp: e.activation(
                            out=Fb.FACC[hi_][:, 0:NTs], in_=bk[:, 0:NTs], func=AF.Identity, scale=colp[:, cw + 2:cw + 3],
                            bias=colp[:, CP_FB + fp:CP_FB + fp + 1]), reads=[bkk, 'colp'], writes=[('FACC', hi_), ('FACCh', hi_)])
                    hbs.append((ps3, bkk, hd, hdk, fp, hi_))
                if tails_only:
                    continue
                for jj in (1, 0):
                    for (ps3, bkk, hd, hdk, fp, hi_) in hbs:
                        fa = Fb.FACC[hi_][:, 0:NTs].rearrange("p (s l) -> p s l", s=nseg)
                        cw = CP_FW + 3 * fp
                        sch.op('dve', lambda e, ps3=ps3, fa=fa, cw=cw, jj=jj: e.scalar_tensor_tensor(out=fa[:, :, 2:L], in0=ps3[:, :, jj:jj + L - 2],
                                                                                                scalar=colp[:, cw + jj:cw + jj + 1], in1=fa[:, :, 2:L],
                                                                                                op0=ALU.mult, op1=ALU.add),
                               reads=[bkk, 'colp', ('FACC', hi_)], writes=[('FACC', hi_)])
                for jj in (1, 0):
                    for (ps3, bkk, hd, hdk, fp, hi_) in hbs:
                        fa = Fb.FACC[hi_][:, 0:NTs].rearrange("p (s l) -> p s l", s=nseg)
                        cw = CP_FW + 3 * fp
                        sch.op('dve', lambda e, hd=hd, fa=fa, cw=cw, jj=jj: e.scalar_tensor_tensor(out=fa[:, :, 0:2], in0=hd[:, :, jj:jj + 2],
                                                                                              scalar=colp[:, cw + jj:cw + jj + 1], in1=fa[:, :, 0:2],
                                                                                              op0=ALU.mult, op1=ALU.add),
                               reads=[hdk, 'colp', ('FACCh', hi_)], writes=[('FACCh', hi_)])
                def gate(f=f):
                    fv, fg = Fb.FACC[f % 2], Fb.FACC[2 + f % 2]
                    kv_, kg_ = ('FACC', f % 2), ('FACC', 2 + f % 2)
                    kvh, kgh = ('FACCh', f % 2), ('FACCh', 2 + f % 2)
                    sch.op('act', lambda e: e.activation(out=fg[:, 0:NTs], in_=fg[:, 0:NTs], func=AF.Silu), reads=[kg_, kgh], writes=[kg_, kgh])
                    sch.op('pool', lambda e: e.tensor_tensor(out=Fb.GT[:, f, 0:NTs], in0=fv[:, 0:NTs], in1=fg[:, 0:NTs], op=ALU.mult),
                           reads=[kv_, kg_, kvh, kgh], writes=[('GT', f)])
                if pend_gate[0] is not None:
                    pend_gate[0]()
                pend_gate[0] = gate
            if pend_gate[0] is not None:
                pend_gate[0]()
                pend_gate[0] = None
            if fconv_out is not None:
                nr = 2 * nseg
                per = 128 // nr // 2 * 2
                per = min(per, 42)
                f0 = 0
                while f0 < 42:
                    nf_ = min(per, 42 - f0)
                    ftf = ft_t[:, f0:f0 + nf_, :, :].rearrange("p f s r -> p (f s r)")
                    bkt, bktk = bank()
                    sch.op('pe', lambda e, ftf=ftf, bkt=bkt, nf_=nf_: e.transpose(out=bkt[0:nf_ * nr, 0:128], in_=ftf, identity=ident),
                           reads=[(ft_key, q_) for q_ in range(42)] + ['consts'], writes=[bktk])
                    sch.op('act', lambda e, bkt=bkt, nf_=nf_: e.activation(out=P.TT[0:nf_ * nr, :], in_=bkt[0:nf_ * nr, 0:128], func=AF.Copy),
                           reads=[bktk], writes=['TT'])
                    for q in range(nf_):
                        fp = f0 + q
                        sch.dma('sp', fconv_out[:, fp * 128:(fp + 1) * 128], P.TT[q * nr:(q + 1) * nr, :], reads=['TT'])
                    f0 += nf_
            if tails_only:
                return
            gkeys = [('GT', f) for f in range(NF)]
            for cb in range(4):
                di = cb % 2
                sch.dma('sp', Fb.Wdn[di][:, :, :], wb_down[:, cb * 256:(cb + 1) * 256].rearrange("(f p) n -> p f n", p=128),
                        reads=WDNK, writes=[('Wdn', di)])
                for (j, t0, nt) in jlist:
                    bk, bkk = bank()
                    for f in range(NF):
                        sch.op('pe', lambda e, f=f, bk=bk, di=di, t0=t0, nt=nt: e.matmul(bk[0:nt, 0:256], lhsT=Fb.GT[:, f, t0:t0 + nt], rhs=Fb.Wdn[di][:, f, :],
                                                                                         start=(f == 0), stop=(f == NF - 1)),
                               reads=gkeys + [('Wdn', di)], writes=[bkk], signal=(f == NF - 1))
                    xs_ = P.X1S[0:nt, j, cb * 256:(cb + 1) * 256]
                    sch.op('dve', lambda e, xs_=xs_, bk=bk, nt=nt: e.scalar_tensor_tensor(out=xs_, in0=xs_, scalar=ALPHA, in1=bk[0:nt, 0:256],
                                                                                         op0=ALU.mult, op1=ALU.add),
                           reads=[('X1S', j), (('X1S', j), 0), (('X1S', j), 1), bkk], writes=[('X1S', j)])
            def ln2_later():
                for n_, (j, t0, nt) in enumerate(jlist):
                    layer_norm(P.X1S[0:nt, j, :], P.X1S[0:nt, j, :], nt, 2, ('X1S', j), ('X1S', j))
                    sch.dma('sp', y_dsts[n_], P.X1S[0:nt, j, :], reads=[('X1S', j), (('X1S', j), 0), (('X1S', j), 1)])
            if defer_ln2:
                return ln2_later
            ln2_later()

        def final_state(HFt, hk, dst, CK):
            def post_h(j, n, src, bkk):
                sch.op('act', lambda e: e.activation(out=CK[:, 0:4, 0:128], in_=src, func=AF.Copy), reads=[bkk], writes=['CKo'])
            transpose_to(None, lambda j: HFt[:, j * 128:(j + 1) * 128], 128, 4, reads=[('HF', hk)], writes=['CKo'], post=post_h)
            sch.dma('sp', dst.rearrange("(a p) n -> p a n", p=128), CK[:, 0:4, 0:128], reads=['CKo'])


        def light_bufs(st):
            Lb = NS()
            Lb.XBCL = sbuf(st, "XBCL", [128, 8, 515])
            Lb.ACCL = sbuf(st, "ACCL", [128, 8, 512])
            Lb.SILL = sbuf(st, "SILL", [128, 8, 512])
            Lb.DTAL = sbuf(st, "DTAL", [128, 4, 4, 8])
            Lb.WDSL = sbuf(st, "WDSL", [128, 4, 8])
            Lb.CDL = sbuf(st, "CDL", [128, 8])
            Lb.XDDL = sbuf(st, "XDDL", [128, 4, 8, 64], BF16)
            Lb.BKL = sbuf(st, "BKL", [128, 4, 2, 128], BF16)
            sch.op('dve', lambda e: e.memset(Lb.XBCL[:, :, 0:3], 0.0), writes=[('XBCL', t_) for t_ in range(8)])
            return Lb

        def light_tile(Lb, b0, nb, kv, last):
            bstate['nb'] = 6
            NT = nb * 128
            XL = P.X1S
            XTL = P.X1T
            XBCL, ACCL, SILL, DTAL = Lb.XBCL, Lb.ACCL, Lb.SILL, Lb.DTAL
            sch.dma('sp', XL[:, 0:nb, :], xs[b0 * 128:(b0 + nb) * 128, :].rearrange("(c p) d -> p c d", p=128), writes=['XL'])
            for c in range(nb):
                transpose_to(lambda j, n, c=c: XTL[:, j:j + n, c * 128:(c + 1) * 128], lambda j, c=c: XL[:, c, j * 128:(j + 1) * 128], 128, KT,
                             reads=['XL'], writes=[('XTL', c)])
            xtk = [('XTL', c) for c in range(nb)]
            for t in range(8):
                bk, bkk = bank()
                cc = C_XBC + t * 128
                for kt in range(KT):
                    sch.op('pe', lambda e, kt=kt, cc=cc, bk=bk: e.matmul(bk[:, 0:NT], lhsT=Wi[:, kt, cc:cc + 128], rhs=XTL[:, kt, 0:NT],
                                                                         start=(kt == 0), stop=(kt == KT - 1)),
                           reads=xtk + [('Wi', kt)], writes=[bkk], signal=(kt == KT - 1))
                sch.op('act', lambda e, t=t, bk=bk: e.activation(out=XBCL[:, t, 3:3 + NT], in_=bk[:, 0:NT], func=AF.Copy),
                       reads=[bkk], writes=[('XBCL', t)])
                cw_ = CP_CW + 4 * t
                sch.op('act', lambda e, t=t, bk=bk, cw_=cw_: e.activation(out=ACCL[:, t, 0:NT], in_=bk[:, 0:NT], func=AF.Identity,
                                                                          scale=colp[:, cw_ + 3:cw_ + 4], bias=colp[:, CP_CB + t:CP_CB + t + 1]),
                       reads=[bkk, 'colp'], writes=[('ACCL', t)])
            bkd, bkdk = bank()
            for c in range(nb):
                for kt in range(KT):
                    sch.op('pe', lambda e, kt=kt, c=c: e.matmul(bkd[:, c * 8:(c + 1) * 8], lhsT=XTL[:, kt, c * 128:(c + 1) * 128], rhs=Wi[:, kt, C_DT:C_DT + 8],
                                                                start=(kt == 0), stop=(kt == KT - 1)),
                           reads=xtk + [('Wi', kt)], writes=[bkdk], signal=(c == nb - 1 and kt == KT - 1))
            u_, au, dt_, a_ = (DTAL[:, i, 0:nb, :] for i in range(4))
            sch.op('dve', lambda e: e.tensor_tensor(out=u_, in0=bkd[:, 0:nb * 8].rearrange("p (c h) -> p c h", c=nb),
                                                    in1=bc(rowp[:, 0:8], [128, nb, 8], 1), op=ALU.add), reads=[bkdk, 'rowp'], writes=['DTAL'])
            sch.op('act', lambda e: e.activation(out=au, in_=u_, func=AF.Abs), reads=['DTAL'], writes=['DTAL'])
            sch.op('act', lambda e: e.activation(out=au, in_=au, func=AF.Exp, scale=-1.0), reads=['DTAL'], writes=['DTAL'])
            sch.op('act', lambda e: e.activation(out=au, in_=au, func=AF.Ln, bias=1.0, scale=1.0), reads=['DTAL'], writes=['DTAL'])
            sch.op('dve', lambda e: e.scalar_tensor_tensor(out=dt_, in0=u_, scalar=0.0, in1=au, op0=ALU.max, op1=ALU.add),
                   reads=['DTAL'], writes=['DTAL'])
            sch.op('dve', lambda e: e.tensor_tensor(out=a_, in0=dt_, in1=bc(P.Aneg[:, :], [128, nb, 8], 1), op=ALU.mult),
                   reads=['DTAL', 'Aneg'], writes=['DTAL'])
            if kv:
                for hp in range(4):
                    bk, bkk = bank()
                    cc = C_K + hp * 128
                    for kt in range(KT):
                        sch.op('pe', lambda e, kt=kt, cc=cc, bk=bk: e.matmul(bk[:, 0:NT], lhsT=Wi[:, kt, cc:cc + 128], rhs=XTL[:, kt, 0:NT],
                                                                             start=(kt == 0), stop=(kt == KT - 1)),
                               reads=xtk + [('Wi', kt)], writes=[bkk], signal=(kt == KT - 1))
                    for c in range(nb):
                        sl = (b0 + c) % NKR
                        sch.op('dve', lambda e, c=c, sl=sl, hp=hp, bk=bk: e.tensor_copy(out=P.KTr[:, sl, hp, :], in_=bk[:, c * 128:(c + 1) * 128]),
                               reads=[bkk], writes=[('KTr', sl)])
                for c in range(nb):
                    sl = (b0 + c) % NKR
                    bkv, bkvk = bank()
                    for kt in range(KT):
                        sch.op('pe', lambda e, kt=kt, c=c, bkv=bkv: e.matmul(bkv[:, :], lhsT=XTL[:, kt, c * 128:(c + 1) * 128], rhs=Wi[:, kt, C_V:C_V + 512],
                                                                             start=(kt == 0), stop=(kt == KT - 1)),
                               reads=xtk + [('Wi', kt)], writes=[bkvk], signal=(kt == KT - 1))
                    sch.op('act', lambda e, sl=sl, bkv=bkv: e.activation(out=P.VAr[:, sl, :, 0:64], in_=bkv[:, :].rearrange("p (h d) -> p h d", h=8),
                                                                         func=AF.Copy), reads=[bkvk], writes=[('VAr', sl)])
                    sch.op('dve', lambda e, sl=sl: e.tensor_copy(out=P.VAr[:, sl, :, 64], in_=flg[:, 0:1].to_broadcast([128, 8])),
                           reads=['flg'], writes=[('VAr', sl)])
            for jj in (0, 1, 2):
                for t in range(8):
                    cw = CP_CW + 4 * t
                    acc = ACCL[:, t, 0:NT]
                    if jj == 3:
                        sch.op('dve', lambda e, t=t, cw=cw, acc=acc: e.tensor_scalar(out=acc, in0=XBCL[:, t, 3:3 + NT], scalar1=colp[:, cw + 3:cw + 4],
                                                                                     scalar2=colp[:, CP_CB + t:CP_CB + t + 1], op0=ALU.mult, op1=ALU.add),
                               reads=[('XBCL', t), 'colp'], writes=[('ACCL', t)])
                    else:
                        sch.op('dve', lambda e, t=t, cw=cw, acc=acc, jj=jj: e.scalar_tensor_tensor(out=acc, in0=XBCL[:, t, jj:jj + NT],
                                                                                                  scalar=colp[:, cw + jj:cw + jj + 1], in1=acc,
                                                                                                  op0=ALU.mult, op1=ALU.add),
                               reads=[('XBCL', t), 'colp', ('ACCL', t)], writes=[('ACCL', t)])
            for hf_ in range(2):
                sch.op('act', lambda e, hf_=hf_: e.activation(out=SILL[:, 4 * hf_:4 * hf_ + 4, 0:NT], in_=ACCL[:, 4 * hf_:4 * hf_ + 4, 0:NT], func=AF.Silu),
                       reads=[('ACCL', t_) for t_ in range(4 * hf_, 4 * hf_ + 4)], writes=[('SILL', hf_)])
            if last:
                sch.op('dve', lambda e: e.tensor_copy(out=P.XBC[:, :, 128:131], in_=XBCL[:, :, NT:NT + 3]),
                       reads=[('XBCL', t_) for t_ in range(8)], writes=['XBC'])
            else:
                sch.op('dve', lambda e: e.tensor_copy(out=XBCL[:, :, 0:3], in_=XBCL[:, :, NT:NT + 3]),
                       reads=[('XBCL', t_) for t_ in range(8)] + [('ACCL', t_) for t_ in range(8)], writes=[('XBCL', t_) for t_ in range(8)])
            bke, bkek = bank()
            for c in range(nb):
                ops_ = [(cU, c)] + [(cOnes, c2) for c2 in range(c + 1, nb)]
                for ii, (lm, c2) in enumerate(ops_):
                    sch.op('pe', lambda e, c=c, lm=lm, c2=c2, ii=ii, n_=len(ops_): e.matmul(bke[:, c * 8:(c + 1) * 8], lhsT=lm, rhs=DTAL[:, 3, c2, :],
                                                                                          start=(ii == 0), stop=(ii == n_ - 1)),
                           reads=['DTAL', 'consts'], writes=[bkek], signal=False)
            for c in range(nb):
                sch.op('pe', lambda e, c=c: e.matmul(bke[:, 32:40], lhsT=cOnes, rhs=DTAL[:, 3, c, :], start=(c == 0), stop=(c == nb - 1)),
                       reads=['DTAL', 'consts'], writes=[bkek], signal=(c == nb - 1))
            sch.op('act', lambda e: e.activation(out=Lb.WDSL[:, 0:nb, :], in_=bke[:, 0:nb * 8].rearrange("p (c h) -> p c h", c=nb), func=AF.Exp),
                   reads=[bkek], writes=['WDSL'])
            sch.op('act', lambda e: e.activation(out=Lb.CDL[:, :], in_=bke[:, 32:40], func=AF.Exp), reads=[bkek], writes=['CDL'])
            sch.op('dve', lambda e: e.tensor_tensor(out=Lb.WDSL[:, 0:nb, :], in0=Lb.WDSL[:, 0:nb, :], in1=dt_, op=ALU.mult),
                   reads=['WDSL', 'DTAL'], writes=['WDSL'])
            for c in range(nb):
                bkx, bkxk = bank()
                for q in range(4):
                    sch.op('pe', lambda e, q=q, c=c, bkx=bkx: e.transpose(out=bkx[:, q * 128:(q + 1) * 128], in_=SILL[:, q, c * 128:(c + 1) * 128], identity=ident),
                           reads=[('SILL', 0), 'consts'], writes=[bkxk], signal=(q == 3))
                sch.op('dve', lambda e, c=c, bkx=bkx: e.tensor_tensor(out=Lb.XDDL[:, c, :, :], in0=bkx[:, :].rearrange("p (h d) -> p h d", h=8),
                                                                      in1=bc(Lb.WDSL[:, c, :], [128, 8, 64], 2), op=ALU.mult),
                       reads=[bkxk, 'WDSL'], writes=[('XDDL', c)])
            for c0 in range(0, nb, 2):
                n2 = min(2, nb - c0)
                bkb, bkbk = bank()
                for cc_ in range(n2):
                    for g in range(2):
                        sch.op('pe', lambda e, cc_=cc_, g=g, c0=c0, bkb=bkb: e.transpose(out=bkb[:, (cc_ * 2 + g) * 128:(cc_ * 2 + g + 1) * 128],
                                                                                         in_=SILL[:, 4 + g, (c0 + cc_) * 128:(c0 + cc_ + 1) * 128], identity=ident),
                               reads=[('SILL', 1), 'consts'], writes=[bkbk], signal=(cc_ == n2 - 1 and g == 1))
                sch.op('act', lambda e, c0=c0, n2=n2, bkb=bkb: e.activation(out=Lb.BKL[:, c0:c0 + n2, :, :],
                                                                           in_=bkb[:, 0:n2 * 256].rearrange("p (c g n) -> p c g n", c=n2, g=2), func=AF.Copy),
                       reads=[bkbk], writes=[('BKL', c0 // 2)])
            bks, bksk = bank()
            for g in range(2):
                for c in range(nb):
                    sch.op('pe', lambda e, g=g, c=c: e.matmul(bks[:, g * 256:(g + 1) * 256], lhsT=Lb.BKL[:, c, g, :], rhs=Lb.XDDL[:, c, 4 * g:4 * g + 4, :],
                                                              start=(c == 0), stop=(c == nb - 1)),
                           reads=[('BKL', c // 2), ('XDDL', c)], writes=[bksk], signal=(g == 1 and c == nb - 1))
            hv = P.HF[:].rearrange("p (h d) -> p h d", h=8)
            sch.op('dve', lambda e: e.tensor_tensor(out=hv, in0=hv, in1=bc(Lb.CDL[:, :], [128, 8, 64], 2), op=ALU.mult),
                   reads=[('HF', 0), 'CDL'], writes=[('HF', 0)])
            sch.op('dve', lambda e: e.tensor_tensor(out=P.HF[:], in0=P.HF[:], in1=bks[:, 0:512], op=ALU.add),
                   reads=[('HF', 0), bksk], writes=[('HF', 0)])
            if last:
                sch.op('act', lambda e: e.activation(out=P.HB[:], in_=P.HF[:], func=AF.Copy), reads=[('HF', 0)], writes=[('HB', 0)])

        def tiles_p(b):
            def f(s, h):
                hp, po = h // 2, (h % 2) * 64
                tl = []
                for i in (2, 3, 4, 0, 1):
                    sl = (b - i) % NKR
                    t_ = dict(kt=P.KTr[po:po + 64, sl, hp, :], ktkey=('KTr', sl), va=P.VAr[:, sl, h, :], vakey=('VAr', sl), nk=128, pb=0,
                              bias=None if i >= 2 else P.biasT[:, i, h, :])
                    if i == 4:
                        t_['zero'] = [(0, 64, 64, 128)]
                    if i == 0:
                        t_['zero'] = [(64, 128, 0, 64)]
                    tl.append(t_)
                return tl
            return f

        def run_blocks(M, b_lo, b_hi):
            for b in range(b_lo, b_hi):
                kind = 'ssm' if b < B_KV else ('kv' if b < B_FULL else 'full')
                sch.op('pool', lambda e: e.tensor_copy(out=P.XBC[:, :, 0:3], in_=P.XBC[:, :, 128:131]), reads=['XBC'], writes=['XBC'])
                outs = {}
                if b >= NBLK - 4:
                    r0 = (b - (NBLK - 4)) * 128
                    outs['nk'] = nk_o[r0:r0 + 128, :]
                    outs['nv'] = nv_o[r0:r0 + 128, :]
                if b == NBLK - 1:
                    outs['sconv'] = sc_o
                if b == B_FULL:
                    outs['flag_state'] = True
                j_in_super = 0 if b <= B_FULL else (b - B_MAIN) % 4
                sl = b % NKR
                kvd = (lambda j, n, sl=sl: P.KTr[:, sl, j:j + n, :], ('KTr', sl), P.VAr[:, sl, :, :], ('VAr', sl))
                mixer_block(M, kind, 128, 1, 128, xs[b * 128:(b + 1) * 128, :], b % 2,
                            P.XBC[:].rearrange("p t (s l) -> p t s l", s=1), 'XBC', [(P.HF, P.HB, 0)], tiles_p(b), outs, kvd, j_in_super)

        if do_prompt:
            sP = ExitStack()
            P.KTr = sbuf(sP, "KTr", [128, NKR, 4, 128], BF16)
            P.VAr = sbuf(sP, "VAr", [128, NKR, 8, 65], BF16)
            P.HF = sbuf(sP, "HF0", [128, 512])
            P.HB = sbuf(sP, "HB0", [128, 512], BF16)
            P.FT = sbuf(sP, "FT", [128, 42, 2])
            sch.op('dve', lambda e: e.memset(P.HF[:], 0.0), writes=[('HF', 0)])
            sch.op('dve', lambda e: e.memset(P.HB[:], 0.0), writes=[('HB', 0)])
            sch.op('dve', lambda e: e.memset(P.FT[:], 0.0), writes=[('FT', q_) for q_ in range(42)])
            FT4 = P.FT[:].rearrange("p f (s r) -> p f s r", s=1)
            sch.barrier(skip_pool_dma=True)
            if nblk == NBLK:
                with ExitStack() as st:
                    Lb = light_bufs(st)
                    tiles_ = [(0, 3)] + [(3 + 4 * i, 4) for i in range(7)]
                    for (b0_, nb_) in tiles_:
                        light_tile(Lb, b0_, nb_, b0_ + nb_ == B_FULL, b0_ + nb_ == B_FULL)
                    sch.barrier(skip_pool_dma=True)
            with ExitStack() as st:
                M = mixer_bufs(st)
                run_blocks(M, B_FULL if nblk == NBLK else NBLK - nblk, B_FULL + 1)
                sl = B_FULL % NKR
                sch.op('pool', lambda e: e.tensor_copy(out=P.VAr[:, sl, :, 64], in_=flg[:, 0:1].to_broadcast([128, 8])),
                       reads=['flg', ('VAr', sl)], writes=[('VAr', sl)])
                sch.barrier()
            with ExitStack() as st:
                Fb = ffn_bufs(st)
                ffn(Fb, 128, 1, 128, [(0, 0, 128)], FT4, 'FT', [None], None, True)
                sch.op('dve', lambda e: e.tensor_scalar(out=P.FT[:], in0=P.FT[:], scalar1=flg[:, 0:1], scalar2=None, op0=ALU.mult),
                       reads=[('FT', q_) for q_ in range(42)] + ['flg'], writes=[('FT', q_) for q_ in range(42)])
                sch.barrier()
            pend_ln2 = None
            for b0 in range(B_MAIN, NBLK, 4):
                with ExitStack() as st:
                    M = mixer_bufs(st)
                    if pend_ln2 is not None:
                        pend_ln2()
                        pend_ln2 = None
                    run_blocks(M, b0, b0 + 4)
                    if b0 + 4 == NBLK:
                        final_state(P.HF, 0, ssm_o, M.LM[:, 0:4, :])
                    sch.barrier()
                with ExitStack() as st:
                    Fb = ffn_bufs(st)
                    jl = [(j, j * 128, 128) for j in range(4)]
                    yd = [y_o[(b0 - B_MAIN + j) * 128:(b0 - B_MAIN + j + 1) * 128, :] for j in range(4)]
                    pend_ln2 = ffn(Fb, 512, 1, 512, jl, FT4, 'FT', yd, fc_o if b0 + 4 == NBLK else None, False, defer_ln2=(b0 + 4 < NBLK))
                    sch.barrier()
            sP.close()
            sch.barrier()
        if do_sample:
            sA = ExitStack()
            S = NS()
            S.FTs = sbuf(sA, "FTs", [128, 42, 2, 2])
            with ExitStack() as st:
                M = mixer_bufs(st)
                S.biasN = sbuf(st, "biasN", [64, 8, 32])
                S.XBCs = sbuf(st, "XBCs", [128, 8, 2, 35])
                S.KTc = [sbuf(st, "KTc%d" % i, [128, 4, 512], BF16) for i in range(2)]
                S.VAc = [sbuf(st, "VAc%d" % i, [128, 4, 8, 65], BF16) for i in range(2)]
                S.CKf = sbuf(st, "CKf", [128, 512])
                S.CKg = sbuf(st, "CKg", [128, 512])
                CK3 = S.CKf[:].rearrange("p (a b) -> p a b", a=4)
                S.HF = [sbuf(st, "HFs%d" % i, [128, 512]) for i in range(2)]
                S.HB = [sbuf(st, "HBs%d" % i, [128, 512], BF16) for i in range(2)]
                S.KTn = sbuf(st, "KTn", [128, 4, 64], BF16)
                S.VAn = sbuf(st, "VAn", [64, 8, 65], BF16)
                L = 32
                if stage == 's1':
                    sch.finish()
                    st.close()
                    sA.close()
                    return nc
                sch.dma('sp', S.biasN[:], biasN_d[:, :, :], writes=['biasN'])
                Xs = M.X[1]
                sch.dma('sp', Xs[0:6, :], cs0[:, :], writes=[('X', 1)])

                def post_cs(j, n, src, bkk):
                    sch.op('act', lambda e: e.activation(out=S.XBCs[:, j:j + n, :, 0:3], in_=src.rearrange("p a (s r) -> p a s r", s=2), func=AF.Copy),
                           reads=[bkk], writes=['XBCs'])
                transpose_to(None, lambda j: Xs[0:6, j * 128:(j + 1) * 128], 6, 8, reads=[('X', 1)], writes=['XBCs'], post=post_cs)
                if stage == 's2':
                    sch.finish()
                    st.close()
                    sA.close()
                    return nc
                for c5 in range(6):
                    w = min(D, 2 * DFF - c5 * D)
                    sch.dma('sp', Xs[0:4, 0:w], fs0[:, c5 * D:c5 * D + w], writes=[('X', 1)])

                    def post_fs(j, n, src, bkk, c5=c5):
                        sch.op('act', lambda e: e.activation(out=S.FTs[:, c5 * 8 + j:c5 * 8 + j + n, :, :], in_=src.rearrange("p a (s r) -> p a s r", s=2),
                                                             func=AF.Copy), reads=[bkk], writes=[('FTs', q_) for q_ in range(42)])
                    transpose_to(None, lambda j: Xs[0:4, j * 128:(j + 1) * 128], 4, w // 128, reads=[('X', 1)], writes=['FTs'], post=post_fs)
                if stage == 's3':
                    sch.finish()
                    st.close()
                    sA.close()
                    return nc
                for s in range(2):
                    sch.dma('sp', CK3, hs0[s].rearrange("(a p) n -> p a n", p=128), writes=['CKf'])
                    if stage == 's3a':
                        sch.finish()
                        st.close()
                        sA.close()
                        return nc

                    def post_h0(j, n, src, bkk, s=s):
                        if stage == 's3c':
                            return
                        if stage == 's4v1':
                            sch.op('dve', lambda e: e.tensor_copy(out=M.T2[:], in_=src.rearrange("p a b -> p (a b)")), reads=[bkk], writes=['T2'])
                            return
                        if stage == 's4v3':
                            sch.op('dve', lambda e: e.tensor_scalar(out=M.T2[:], in0=src.rearrange("p a b -> p (a b)"), scalar1=1.0, scalar2=None, op0=ALU.mult), reads=[bkk], writes=['T2'])
                            return
                        sch.op('act', lambda e: e.activation(out=S.HF[s][:], in_=src.rearrange("p a b -> p (a b)"), func=AF.Copy),
                               reads=[bkk], writes=[('HF', 1 + s)])
                        if stage == 's3b':
                            return
                        if stage == 's4y':
                            sch.op('dve', lambda e: e.memset(M.T2[:], 0.0), reads=[bkk], writes=['T2'])
                            return
                        if stage == 's4z':
                            sch.op('dve', lambda e: e.memset(M.T2[:], 0.0), reads=[], writes=['T2'])
                            return
                        if stage == 's4x':
                            sch.op('dve', lambda e: e.tensor_copy(out=M.T2[:], in_=src.rearrange("p a b -> p (a b)")), reads=[bkk], writes=['T2'])
                            return
                        sch.op('dve', lambda e: e.tensor_copy(out=S.HB[s][:], in_=src.rearrange("p a b -> p (a b)")), reads=[bkk], writes=[('HB', 1 + s)])
                    transpose_to(None, lambda j: CK3[:, j, :], 128, 4, reads=['CKf'], writes=[], post=post_h0)
                if stage in ('s4', 's3b', 's3c', 's4x', 's4y', 's4z', 's4v1', 's4v3'):
                    sch.finish()
                    st.close()
                    sA.close()
                    return nc
                for s in range(2):
                    for m in range(4):
                        CKb, ckk = ((S.CKf, 'CKf'), (S.CKg, 'CKg'))[m % 2]
                        sch.dma('sp', CKb[:], ck[s][m * 128:(m + 1) * 128, :], writes=[ckk])
                        transpose_to(lambda j, n, m=m, s=s: S.KTc[s][:, j:j + n, m * 128:(m + 1) * 128], lambda j, CKb=CKb: CKb[:, j * 128:(j + 1) * 128], 128, 4,
                                     reads=[ckk], writes=[('KTc', s)])
                    for m in range(4):
                        CKb, ckk = ((S.CKf, 'CKf'), (S.CKg, 'CKg'))[m % 2]
                        sch.dma('sp', CKb[:], cv[s][m * 128:(m + 1) * 128, :], writes=[ckk])
                        sch.op('dve', lambda e, s=s, m=m, CKb=CKb: e.tensor_copy(out=S.VAc[s][:, m, :, 0:64], in_=CKb[:].rearrange("p (h d) -> p h d", h=8)),
                               reads=[ckk], writes=[('VAc', s)])
                    sch.op('pool', lambda e, s=s: e.memset(S.VAc[s][:, :, :, 64], 1.0), writes=[('VAc', s)])

                def tiles_s(s, h):
                    hp, po = h // 2, (h % 2) * 64
                    tl = []
                    for m in range(3):
                        tl.append(dict(kt=S.KTc[s][po:po + 64, hp, m * 128:(m + 1) * 128], ktkey=('KTc', s), va=S.VAc[s][:, m, h, :], vakey=('VAc', s),
                                       nk=128, pb=0, bias=None))
                    tl.append(dict(kt=S.KTc[s][po:po + 64, hp, 384:512], ktkey=('KTc', s), va=S.VAc[s][:, 3, h, :], vakey=('VAc', s),
                                   nk=128, pb=0, bias=P.biasT[:, 1, h, 0:32]))
                    tl.append(dict(kt=S.KTn[po:po + 64, hp, s * 32:(s + 1) * 32], ktkey='KTn', va=S.VAn[s * 32:(s + 1) * 32, h, :], vakey='VAn',
                                   nk=32, pb=s * 32, bias=S.biasN[s * 32:(s + 1) * 32, h, :]))
                    return tl

                kvd = (lambda j, n: S.KTn[:, j:j + n, 0:64], 'KTn', S.VAn[0:64, :, :], 'VAn')
                if stage == 'sample_setup':
                    sch.finish()
                    st.close()
                    sA.close()
                    return nc
                mixer_block(M, 'full', 64, 2, 32, xq[:, :], 0, S.XBCs[:], 'XBCs',
                            [(S.HF[0], S.HB[0], 1), (S.HF[1], S.HB[1], 2)], tiles_s,
                            dict(nk=nks_o[:, :], nv=nvs_o[:, :], sconv=scs_o), kvd, 0)
                if stage == 'sample_mixer':
                    sch.finish()
                    st.close()
                    sA.close()
                    return nc
                for s in range(2):
                    final_state(S.HF[s], 1 + s, ssms_o[s], CK3)
                sch.barrier()
            with ExitStack() as st2:
                Fb = ffn_bufs(st2)
                ffn(Fb, 64, 2, 32, [(0, 0, 64)], S.FTs[:], 'FTs', [ys_o[:, :]], fcs_o, False)
                sch.barrier()
            sA.close()

        sch.finish()
    return nc


_CACHE = {}


def _get_nc():
    if 'nc' not in _CACHE:
        _CACHE['nc'] = build_program()
    return _CACHE['nc']


def _host_consts(rel_bias):
    k = np.arange(128)[:, None]
    q = np.arange(128)[None, :]
    rb = rel_bias
    biasT = np.empty((128, 2, 8, 128), np.float32)
    for i in range(2):
        idx = np.clip(128 * i + q - k, -128, 128) + 128
        biasT[:, i] = np.transpose(rb[:, idx], (1, 0, 2))
    kk = np.arange(32)[:, None]
    qq = np.arange(32)[None, :]
    idx = np.clip(qq - kk, -128, 128) + 128
    bn = np.transpose(rb[:, idx], (1, 0, 2))
    biasN = np.concatenate([bn, bn], axis=0).astype(np.float32)
    cbias = rb[:, 256].reshape(1, 8).astype(np.float32)
    consts = np.zeros((128, 8, 128), np.float32)
    consts[:, 0] = np.eye(128)
    consts[:, 1] = (k <= q)
    consts[:, 2] = (k > q)
    consts[:, 3] = 1.0
    seg = np.arange(128) // 32
    same = seg[:, None] == seg[None, :]
    consts[:, 4] = (k <= q) & same
    consts[:, 5] = (k > q) & same
    consts[:, 6] = (seg == 0)[:, None]
    consts[:, 7] = (seg == 1)[:, None]
    return biasT, biasN, cbias, consts


def kernel(x_prompt, x_sample, cache_attn_k, cache_attn_v, state_ssm, state_ssm_conv, state_ffn_conv,
           w_in, rel_bias, attn_norm_g, ssm_conv_w, ssm_conv_b, ssm_dt_bias, ssm_A_log, ssm_D,
           ssm_norm_g, w_out, ln1_g, ln1_b, w_up, ffn_conv_w, ffn_conv_b, w_down, ln2_g, ln2_b):
    f32 = np.float32
    A = lambda a: np.ascontiguousarray(np.asarray(a, dtype=f32))
    x_prompt, x_sample = A(x_prompt), A(x_sample)
    ck_, cv_ = A(cache_attn_k)[0], A(cache_attn_v)[0]
    hs_, cs_, fs_ = A(state_ssm)[0], A(state_ssm_conv)[0], A(state_ffn_conv)[0]
    biasT, biasN, cbias, consts = _host_consts(A(rel_bias)[0])
    colp = np.zeros((128, CP_N), f32)
    colp[:, CP_AG:CP_AG + 4] = A(attn_norm_g)[0].reshape(4, 128).T
    colp[:, CP_SG:CP_SG + 4] = A(ssm_norm_g)[0].reshape(4, 128).T
    cw = A(ssm_conv_w)[0]
    colp[:, CP_CW:CP_CW + 32] = cw.reshape(4, 8, 128).transpose(2, 1, 0).reshape(128, 32)
    colp[:, CP_CB:CP_CB + 8] = A(ssm_conv_b)[0].reshape(8, 128).T
    fw = A(ffn_conv_w)[0]
    colp[:, CP_FW:CP_FW + 126] = fw.reshape(3, 42, 128).transpose(2, 1, 0).reshape(128, 126)
    colp[:, CP_FB:CP_FB + 42] = A(ffn_conv_b)[0].reshape(42, 128).T
    rowp = np.concatenate([A(ssm_dt_bias)[0], A(ssm_A_log)[0], A(ssm_D)[0]]).reshape(1, 24)
    lnp = np.concatenate([A(ln1_g)[0], A(ln1_b)[0], A(ln2_g)[0], A(ln2_b)[0]]).reshape(1, 4 * D)
    wu = A(w_up)[0]
    wu_perm = np.ascontiguousarray(wu.reshape(D, 2, NF, 128).transpose(0, 2, 1, 3).reshape(D, 2 * DFF))
    shared = dict(w_in=A(w_in)[0], w_out=A(w_out)[0], w_up=wu_perm, w_down=A(w_down)[0],
                  biasT=biasT, biasN=biasN, cbias=cbias, colp=colp, rowp=rowp, lnp=lnp, consts=consts)
    in_maps = []
    for c in range(8):
        b, hf = c // 2, c % 2
        if hf == 0:
            xs = np.concatenate([np.zeros((4096, D), f32), x_prompt[b, :4096]], axis=0)
        else:
            xs = x_prompt[b]
        m = dict(shared)
        m.update(xs=np.ascontiguousarray(xs), flag=np.full((128, 1), float(hf), f32),
                 xq=np.ascontiguousarray(x_sample[2 * c:2 * c + 2].reshape(64, D)),
                 ck=np.ascontiguousarray(ck_[2 * c:2 * c + 2].reshape(2, 512, 512)),
                 cv=np.ascontiguousarray(cv_[2 * c:2 * c + 2].reshape(2, 512, 512)),
                 hs0=np.ascontiguousarray(hs_[2 * c:2 * c + 2].reshape(2, 512, 128)),
                 cs0=np.ascontiguousarray(cs_[2 * c:2 * c + 2].reshape(6, D)),
                 fs0=np.ascontiguousarray(fs_[2 * c:2 * c + 2].reshape(4, 2 * DFF)))
        in_maps.append(m)
    nc = _get_nc()
    res = run_bass_kernel_spmd(nc, in_maps, core_ids=list(range(8)))
    R = res.results
    y_prompt = np.stack([np.concatenate([R[2 * b]["y_o"], R[2 * b + 1]["y_o"]], axis=0) for b in range(4)]).astype(f32)
    y_sample = np.concatenate([R[c]["ys_o"].reshape(2, 32, D) for c in range(8)], axis=0).astype(f32)
    nkp = np.stack([R[2 * b + 1]["nk_o"].reshape(512, 8, 64) for b in range(4)])[None].astype(f32)
    nvp = np.stack([R[2 * b + 1]["nv_o"].reshape(512, 8, 64) for b in range(4)])[None].astype(f32)
    nks = np.concatenate([R[c]["nks_o"].reshape(2, 32, 8, 64) for c in range(8)], axis=0)[None].astype(f32)
    nvs = np.concatenate([R[c]["nvs_o"].reshape(2, 32, 8, 64) for c in range(8)], axis=0)[None].astype(f32)
    ssmp = np.stack([R[2 * b + 1]["ssm_o"].reshape(8, 64, 128) for b in range(4)])[None].astype(f32)
    ssms = np.concatenate([R[c]["ssms_o"].reshape(2, 8, 64, 128) for c in range(8)], axis=0)[None].astype(f32)
    scp = np.stack([R[2 * b + 1]["sc_o"] for b in range(4)])[None].astype(f32)
    scs = np.concatenate([R[c]["scs_o"].reshape(2, 3, D) for c in range(8)], axis=0)[None].astype(f32)
    fcp = np.stack([R[2 * b + 1]["fc_o"] for b in range(4)])[None].astype(f32)
    fcs = np.concatenate([R[c]["fcs_o"].reshape(2, 2, 2 * DFF) for c in range(8)], axis=0)[None].astype(f32)
    return (y_prompt, y_sample, nkp, nvp, nks, nvs, ssmp, ssms, scp, scs, fcp, fcs)
```

```python
import numpy as np
from contextlib import ExitStack
import concourse.bass as bass
import concourse.mybir as mybir
from concourse.bass_utils import run_bass_kernel_spmd

F32 = mybir.dt.float32
BF16 = mybir.dt.bfloat16
AF = mybir.ActivationFunctionType
ALU = mybir.AluOpType

D = 1024
KT = 8
NPROJ = 3080
DFF = 2688
NF = 21
C_Q, C_K, C_V, C_Z, C_XBC, C_DT = 0, 512, 1024, 1536, 2048, 3072
ALPHA = 2.0 ** 0.25
EPS = 1e-5
NBLK = 64
B_KV = 27
B_FULL = 31
B_MAIN = 32
CP_AG, CP_SG, CP_CW, CP_CB, CP_FW, CP_FB, CP_N = 0, 4, 8, 40, 48, 174, 216


class Sched:
    def __init__(self, nc, es, ndma=64):
        self.nc = nc
        self.engs = {'pe': nc.tensor, 'act': nc.scalar, 'dve': nc.vector, 'pool': nc.gpsimd, 'sp': nc.sync}
        self.sem = {k: es.enter_context(nc.semaphore('s_' + k)) for k in self.engs}
        self.cnt = {k: 0 for k in self.engs}
        self.seen = {k: {} for k in self.engs}
        self.dsem = [es.enter_context(nc.semaphore('d%d' % i)) for i in range(ndma)]
        self.dval = [0] * ndma
        self.dnext = 0
        self.npool = 16
        self.pool_used = 0
        self.lastw = {}
        self.readers = {}
        self.nwait = 0
        self.pre_barrier = None

    def _wait(self, e, tok):
        kind, src, val = tok
        if kind == 'e' and src == e and e in ('pe', 'sp'):
            return
        key = (kind, src)
        if self.seen[e].get(key, 0) >= val:
            return
        sem = self.sem[src] if kind == 'e' else self.dsem[src]
        self.engs[e].wait_ge(sem, val)
        self.seen[e][key] = val
        self.nwait += 1

    def _deps(self, e, reads, writes):
        for k in reads:
            t = self.lastw.get(k)
            if t is not None:
                self._wait(e, t)
            if isinstance(k, tuple) and k[0] == 'bk':
                for (kind, src), val in self.readers.get(k, {}).items():
                    if src != e:
                        self._wait(e, (kind, src, val))
        for k in writes:
            t = self.lastw.get(k)
            if t is not None:
                self._wait(e, t)
            for (kind, src), val in self.readers.get(k, {}).items():
                self._wait(e, (kind, src, val))

    def _commit(self, tok, reads, writes):
        kind, src, val = tok
        for k in reads:
            d = self.readers.setdefault(k, {})
            if d.get((kind, src), 0) < val:
                d[(kind, src)] = val
        for k in writes:
            self.lastw[k] = tok
            self.readers[k] = {}

    def op(self, e, fn, reads=(), writes=(), signal=True):
        self._deps(e, reads, writes)
        ins = fn(self.engs[e])
        if signal:
            self.cnt[e] += 1
            ins.then_inc(self.sem[e], 1)
            tok = ('e', e, self.cnt[e])
        else:
            tok = ('e', e, self.cnt[e] + 1)
        self._commit(tok, reads, writes)

    def dma(self, q, out, in_, reads=(), writes=(), **kw):
        if q == 'pool':
            i = len(self.dsem) - 1 - self.pool_used
            self.pool_used += 1
            assert self.pool_used <= self.npool and self.dval[i] == 0
        else:
            i = self.dnext
            self.dnext = (self.dnext + 1) % (len(self.dsem) - self.npool)
        if self.dval[i] > 0:
            self._wait(q, ('d', i, self.dval[i]))
        self._deps(q, reads, writes)
        self.dval[i] += 16
        self.engs[q].dma_start(out=out, in_=in_, **kw).then_inc(self.dsem[i], 16)
        self._commit(('d', i, self.dval[i]), reads, writes)

    def barrier(self, skip_pool_dma=False):
        if self.pre_barrier is not None:
            self.pre_barrier()
        nhw = len(self.dsem) - self.npool
        for i, v in enumerate(self.dval):
            if v > 0 and not (skip_pool_dma and i >= nhw):
                self._wait('sp', ('d', i, v))
        self.cnt['sp'] += 1
        self.engs['sp'].sem_inc(self.sem['sp'], 1)
        ce = ('pe', 'act', 'dve', 'pool')
        for e in ce:
            for f in ce + ('sp',):
                if self.cnt[f] > 0:
                    self._wait(e, ('e', f, self.cnt[f]))
        for f in ce:
            if self.cnt[f] > 0:
                self._wait('sp', ('e', f, self.cnt[f]))
        keep = {}
        if skip_pool_dma:
            keep = {k: t for k, t in self.lastw.items() if t[0] == 'd' and t[1] >= nhw}
        self.lastw.clear()
        self.lastw.update(keep)
        self.readers.clear()

    def finish(self):
        for i, v in enumerate(self.dval):
            if v > 0:
                self._wait('sp', ('d', i, v))


class NS:
    pass


def build_program(do_sample=True, nblk=NBLK, do_prompt=True, dbg=False, stage=None):
    nc = bass.Bass("TRN2", target_bir_lowering=False)

    def din(name, shape, dt=F32):
        return nc.dram_tensor(name, list(shape), dt, kind="ExternalInput").ap()

    def dout(name, shape, dt=F32):
        return nc.dram_tensor(name, list(shape), dt, kind="ExternalOutput").ap()

    def dint(name, shape, dt):
        return nc.dram_tensor(name, list(shape), dt, kind="Internal").ap()

    xs = din("xs", [NBLK * 128, D])
    flag = din("flag", [128, 1])
    xq = din("xq", [64, D])
    ck = din("ck", [2, 512, 512])
    cv = din("cv", [2, 512, 512])
    hs0 = din("hs0", [2, 512, 128])
    cs0 = din("cs0", [6, D])
    fs0 = din("fs0", [4, 2 * DFF])
    w_in = din("w_in", [D, NPROJ])
    w_out = din("w_out", [D, D])
    w_up = din("w_up", [D, 2 * DFF])
    w_down = din("w_down", [DFF, D])
    biasT_d = din("biasT", [128, 2, 8, 128])
    biasN_d = din("biasN", [64, 8, 32])
    cbias_d = din("cbias", [1, 8])
    colp_d = din("colp", [128, CP_N])
    rowp_d = din("rowp", [1, 24])
    lnp_d = din("lnp", [1, 4 * D])
    consts_d = din("consts", [128, 8, 128])

    y_o = dout("y_o", [4096, D])
    ys_o = dout("ys_o", [64, D])
    nk_o = dout("nk_o", [512, 512])
    nv_o = dout("nv_o", [512, 512])
    nks_o = dout("nks_o", [64, 512])
    nvs_o = dout("nvs_o", [64, 512])
    ssm_o = dout("ssm_o", [512, 128])
    ssms_o = dout("ssms_o", [2, 512, 128])
    sc_o = dout("sc_o", [3, D])
    scs_o = dout("scs_o", [6, D])
    fc_o = dout("fc_o", [2, 2 * DFF])
    fcs_o = dout("fcs_o", [4, 2 * DFF])

    dbg_o = dout("dbg_o", [128, 16, D]) if dbg else None

    def dump(slot, ap, npart, w, key):
        if dbg:
            sch.dma('sp', dbg_o[0:npart, slot, 0:w], ap, reads=[key])

    wb_in = dint("wb_in", [D, NPROJ], BF16)
    wb_out = dint("wb_out", [D, D], BF16)
    wb_up = dint("wb_up", [D, 2 * DFF], BF16)
    wb_down = dint("wb_down", [DFF, D], BF16)

    es = ExitStack()
    with es:
        sch = Sched(nc, es)

        uid = {'n': 0}

        def sbuf(stack, name, shape, dt=F32):
            uid['n'] += 1
            return stack.enter_context(nc.sbuf_tensor("s%d_%s" % (uid['n'], name), list(shape), dt))

        banks = [es.enter_context(nc.psum_tensor("bk%d" % i, [128, 512], F32)) for i in range(8)]
        bstate = {'n': 0}

        def bank():
            nb_ = bstate.get('nb', 6)
            i = bstate['n'] % nb_
            bstate['n'] = (i + 1) % nb_
            return banks[i], ('bk', i)

        P = NS()
        P.Wi = sbuf(es, "Wi", [128, KT, NPROJ], BF16)
        P.Wo = sbuf(es, "Wo", [128, KT, D], BF16)
        P.consts = sbuf(es, "consts", [128, 8, 128])
        P.biasT = sbuf(es, "biasT", [128, 2, 8, 128])
        P.cbias = sbuf(es, "cbias", [128, 8])
        P.colp = sbuf(es, "colp", [128, CP_N])
        P.rowp = sbuf(es, "rowp", [128, 24])
        P.Aneg = sbuf(es, "Aneg", [128, 8])
        P.lnp = sbuf(es, "lnp", [128, 4, D])
        P.flg = sbuf(es, "flg", [128, 1])
        P.mhalf = sbuf(es, "mhalf", [128, 1])
        NKR = 5
        P.X1S = sbuf(es, "X1S", [128, 4, D])
        P.X1T = sbuf(es, "X1T", [128, KT, 512], BF16)
        P.XBC = sbuf(es, "XBC", [128, 8, 131])
        P.SMALL = sbuf(es, "SMALL", [128, 16])
        P.STAT = sbuf(es, "STAT", [128, 2, 6])
        P.MV = sbuf(es, "MV", [128, 2])
        P.TT = sbuf(es, "TT", [128, 128])
        P.TAIL = sbuf(es, "TAIL", [128, 96])
        ident = P.consts[:, 0, :]
        cT, cU, cOnes, cT2, cU2 = (P.consts[:, i, :] for i in (1, 2, 3, 4, 5))
        Wi, Wo, colp, rowp, lnp, flg, mhalf, SMALL, STAT, MV = P.Wi, P.Wo, P.colp, P.rowp, P.lnp, P.flg, P.mhalf, P.SMALL, P.STAT, P.MV

        def bc(ap, shape, axis):
            return ap.unsqueeze(axis).to_broadcast(list(shape))

        def mixer_bufs(st):
            M = NS()
            M.X = [sbuf(st, "X%d" % i, [128, D]) for i in range(2)]
            M.XT = [sbuf(st, "XT%d" % i, [128, KT, 128], BF16) for i in range(2)]
            M.ACC = sbuf(st, "ACC", [128, 8, 128])
            M.SIL = sbuf(st, "SIL", [128, 8, 128])
            M.RHSA = sbuf(st, "RHSA", [128, 8, 128])
            M.BCT = sbuf(st, "BCT", [128, 4, 128], BF16)
            M.BK = sbuf(st, "BK", [128, 2, 128], BF16)
            M.QT = sbuf(st, "QT", [128, 4, 128], BF16)
            M.SZ = sbuf(st, "SZ", [128, 512])
            M.PT = [sbuf(st, "PT%d" % i, [128, 5, 128], BF16) for i in range(2)]
            M.SB = [sbuf(st, "SB%d" % i, [128, 2, 128]) for i in range(2)]
            M.DTA = sbuf(st, "DTA", [128, 4, 8])
            M.LM = sbuf(st, "LM", [128, 8, 128])
            M.EE = sbuf(st, "EE", [128, 8])
            M.CD = sbuf(st, "CD", [128, 2, 8])
            M.WDS = sbuf(st, "WDS", [128, 8])
            M.CBM = sbuf(st, "CBM", [128, 2, 128])
            M.MT = sbuf(st, "MT", [128, 8, 128], BF16)
            M.XSK = sbuf(st, "XSK", [128, 512])
            M.XDT = sbuf(st, "XDT", [128, 8, 64], BF16)
            M.XDD = sbuf(st, "XDD", [128, 8, 64], BF16)
            M.T1 = sbuf(st, "T1", [128, 512])
            M.T2 = sbuf(st, "T2", [128, 512])
            M.YY = sbuf(st, "YY", [128, 512])
            M.JB = sbuf(st, "JB", [128, 512], BF16)
            M.AST = sbuf(st, "AST", [128, KT, 128], BF16)
            for i in range(2):
                sch.op('pool', lambda e, i=i: e.memset(M.PT[i][:], 0.0), writes=[('PT', i)])
            return M

        def ffn_bufs(st):
            Fb = NS()
            Fb.Wup = [sbuf(st, "Wup%d" % i, [128, KT, 256], BF16) for i in range(3)]
            Fb.Wdn = [sbuf(st, "Wdn%d" % i, [128, NF, 256], BF16) for i in range(2)]
            Fb.GT = sbuf(st, "GT", [128, NF, 512], BF16)
            Fb.HEAD = [sbuf(st, "HEAD%d" % i, [128, 8]) for i in range(4)]
            Fb.FACC = [sbuf(st, "FACC%d" % i, [128, 512]) for i in range(4)]
            return Fb

        def cast_rows(dst, src, nrows, split, key, nparts, extra_reads=()):
            nb = nrows // 128
            per = (nb + nparts - 1) // nparts
            r = 0
            while r < nb:
                n = min(per, nb - r)
                s_ = src[r * 128:(r + n) * 128, :].rearrange("r (a b) -> r a b", a=split)
                d_ = dst[r * 128:(r + n) * 128, :].rearrange("r (a b) -> r a b", a=split)
                sch.dma('pool', d_, s_, reads=list(extra_reads), writes=[(key, i) for i in range(r, r + n)])
                r += n

        sch.dma('sp', P.consts[:], consts_d[:, :, :], writes=['consts'])
        sch.dma('sp', colp[:], colp_d[:, :], writes=['colp'])
        sch.dma('sp', flg[:], flag[:, :], writes=['flg'])
        sch.dma('sp', rowp[:], bass.AP(tensor=rowp_d.tensor, offset=0, ap=[[0, 128], [1, 24]]), writes=['rowp'])
        sch.dma('sp', P.cbias[:], bass.AP(tensor=cbias_d.tensor, offset=0, ap=[[0, 128], [1, 8]]), writes=['cbias'])
        with ExitStack() as st0:
            stg = [sbuf(st0, "WSTG%d" % i, [128, NPROJ]) for i in range(3)]
            for kt in range(KT):
                si = kt % 3
                sch.dma('sp', stg[si][:], w_in[kt * 128:(kt + 1) * 128, :], writes=[('WSTG', si)])
                if kt % 2 == 0:
                    sch.op('act', lambda e, kt=kt, si=si: e.activation(out=Wi[:, kt, :], in_=stg[si][:], func=AF.Copy), reads=[('WSTG', si)], writes=[('Wi', kt)])
                else:
                    sch.op('dve', lambda e, kt=kt, si=si: e.tensor_copy(out=Wi[:, kt, :], in_=stg[si][:]), reads=[('WSTG', si)], writes=[('Wi', kt)])
            sch.dma('sp', lnp[:].rearrange("p a b -> p (a b)"),
                    bass.AP(tensor=lnp_d.tensor, offset=0, ap=[[0, 128], [1, 4 * D]]), writes=['lnp'])
            sch.dma('sp', P.biasT[:], biasT_d[:, :, :, :], writes=['biasT'])
            for k4 in range(0, KT, 4):
                sch.dma('pool', Wo[:, k4:k4 + 4, :], w_out[k4 * 128:(k4 + 4) * 128, :].rearrange("(k p) n -> p k n", p=128),
                        reads=[('WSTG', (KT - 1) % 3)], writes=[('Wo', k4 + i_) for i_ in range(4)])
            cast_rows(wb_up, w_up, D, 6, 'wb_up', 4, extra_reads=[('WSTG', (KT - 1) % 3)])
            cast_rows(wb_down, w_down, DFF, 1, 'wb_down', 3, extra_reads=[('WSTG', (KT - 1) % 3)])
            sch.barrier(skip_pool_dma=True)
        WUPK = [('wb_up', r) for r in range(8)]
        WDNK = [('wb_down', r) for r in range(NF)]

        sch.op('act', lambda e: e.activation(out=P.Aneg[:], in_=rowp[:, 8:16], func=AF.Exp), reads=['rowp'], writes=['Aneg'])
        sch.op('dve', lambda e: e.tensor_scalar(out=P.Aneg[:], in0=P.Aneg[:], scalar1=-1.0, scalar2=None, op0=ALU.mult),
               reads=['Aneg'], writes=['Aneg'])
        sch.op('dve', lambda e: e.memset(mhalf[:], -0.5), writes=['mhalf'])
        sch.op('dve', lambda e: e.memset(P.XBC[:], 0.0), writes=['XBC'])

        if stage == 'prologue':
            sch.finish()
            return nc

        def transpose_to(dst_fn, src_fn, nparts, ntiles, reads, writes, post=None, engs=('act', 'dve')):
            j = 0
            gi = 0
            while j < ntiles:
                n = min(4, ntiles - j)
                bk, bkk = bank()
                for q in range(n):
                    sch.op('pe', lambda e, q=q, j=j, bk=bk: e.transpose(out=bk[:, q * nparts:(q + 1) * nparts],
                                                                        in_=src_fn(j + q), identity=ident[0:nparts, 0:nparts]),
                           reads=list(reads) + ['consts'], writes=[bkk], signal=(q == n - 1))
                src = bk[:, 0:n * nparts].rearrange("p (a b) -> p a b", a=n)
                if post is not None:
                    post(j, n, src, bkk)
                elif engs[gi % 2] == 'act':
                    sch.op('act', lambda e, j=j, n=n, src=src: e.activation(out=dst_fn(j, n), in_=src, func=AF.Copy),
                           reads=[bkk], writes=writes)
                else:
                    sch.op('dve', lambda e, j=j, n=n, src=src: e.tensor_copy(out=dst_fn(j, n), in_=src),
                           reads=[bkk], writes=writes)
                j += n
                gi += 1

        def rstd_from_sumsq(ss_ap, n_feat, npart, key):
            sch.op('pool', lambda e: e.tensor_scalar(out=ss_ap, in0=ss_ap, scalar1=1.0 / n_feat, scalar2=EPS,
                                                     op0=ALU.mult, op1=ALU.add), reads=[key], writes=[key])
            sch.op('pool', lambda e: e.tensor_tensor(out=ss_ap, in0=ss_ap, in1=mhalf[0:npart, :], op=ALU.pow),
                   reads=[key, 'mhalf'], writes=[key])

        def layer_norm(src_ap, dst_ap, NT, gi, src_key, dst_key):
            for c in range(2):
                sch.op('dve', lambda e, c=c: e.bn_stats(out=STAT[0:NT, c, :], in_=src_ap[:, c * 512:(c + 1) * 512]),
                       reads=[src_key], writes=['STAT'])
            sch.op('dve', lambda e: e.bn_aggr(out=MV[0:NT, :], in_=STAT[0:NT, :, :].rearrange("p a b -> p (a b)")),
                   reads=['STAT'], writes=['MV'])
            sch.op('pool', lambda e: e.tensor_scalar(out=MV[0:NT, 1:2], in0=MV[0:NT, 1:2], scalar1=1.0, scalar2=EPS,
                                                     op0=ALU.mult, op1=ALU.add), reads=['MV'], writes=['MV'])
            sch.op('pool', lambda e: e.tensor_tensor(out=MV[0:NT, 1:2], in0=MV[0:NT, 1:2], in1=mhalf[0:NT, :], op=ALU.pow),
                   reads=['MV', 'mhalf'], writes=['MV'])
            sch.op('dve', lambda e: e.tensor_scalar(out=dst_ap, in0=src_ap, scalar1=MV[0:NT, 0:1], scalar2=MV[0:NT, 1:2],
                                                    op0=ALU.subtract, op1=ALU.mult), reads=[src_key, 'MV'], writes=[dst_key])
            for hf_, eng in ((0, 'dve'), (1, 'pool')):
                cs_ = slice(hf_ * 512, (hf_ + 1) * 512)
                sch.op(eng, lambda e, cs_=cs_: e.tensor_tensor(out=dst_ap[:, cs_], in0=dst_ap[:, cs_], in1=lnp[0:NT, gi, cs_], op=ALU.mult),
                       reads=[dst_key, 'lnp'], writes=[(dst_key, hf_)])
                sch.op(eng, lambda e, cs_=cs_: e.tensor_tensor(out=dst_ap[:, cs_], in0=dst_ap[:, cs_], in1=lnp[0:NT, gi + 1, cs_], op=ALU.add),
                       reads=[(dst_key, hf_), 'lnp'], writes=[(dst_key, hf_)])

        deferred = []

        def flush_deferred():
            while deferred:
                deferred.pop(0)()

        sch.pre_barrier = flush_deferred

        def mixer_block(M, kind, NT, nseg, L, x_src, xi, xbc_t, xbc_key, HFs, attn_tiles, outs, kv_dst, j_in_super):
            bstate['nb'] = 6
            full = kind == 'full'
            Xb = M.X[xi]
            XTb = M.XT[xi]
            xk, xtk = ('X', xi), ('XT', xi)
            Tm = cT if nseg == 1 else cT2
            Um = cU if nseg == 1 else cU2
            DTA, LM, SIL, ACC, BCT = M.DTA, M.LM, M.SIL, M.ACC, M.BCT
            sch.dma('sp', Xb[0:NT, :], x_src, writes=[xk])
            transpose_to(lambda j, n: XTb[:, j:j + n, 0:NT], lambda j: Xb[0:NT, j * 128:(j + 1) * 128], NT, KT,
                         reads=[xk], writes=[xtk], engs=('act', 'act'))

            def proj_fm(c0, ntiles, evac):
                j = 0
                while j < ntiles:
                    n = min(4, ntiles - j)
                    bk, bkk = bank()
                    for q in range(n):
                        cc = c0 + (j + q) * 128
                        for kt in range(KT):
                            sch.op('pe', lambda e, q=q, kt=kt, cc=cc, bk=bk: e.matmul(bk[:, q * NT:(q + 1) * NT], lhsT=Wi[:, kt, cc:cc + 128],
                                                                                       rhs=XTb[:, kt, 0:NT], start=(kt == 0), stop=(kt == KT - 1)),
                                   reads=[xtk, ('Wi', kt)], writes=[bkk], signal=(q == n - 1 and kt == KT - 1))
                    evac(bk, bkk, j, n)
                    j += n

            def proj_tm(c0, ncols, bk, bkk):
                for kt in range(KT):
                    sch.op('pe', lambda e, kt=kt: e.matmul(bk[0:NT, 0:ncols], lhsT=XTb[:, kt, 0:NT], rhs=Wi[:, kt, c0:c0 + ncols],
                                                           start=(kt == 0), stop=(kt == KT - 1)),
                           reads=[xtk, ('Wi', kt)], writes=[bkk], signal=(kt == KT - 1))

            bkd, bkdk = bank()
            proj_tm(C_DT, 8, bkd, bkdk)
            u_, au, dt_, a_ = (DTA[0:NT, i, :] for i in range(4))
            sch.op('dve', lambda e: e.tensor_tensor(out=u_, in0=bkd[0:NT, 0:8], in1=rowp[0:NT, 0:8], op=ALU.add),
                   reads=[bkdk, 'rowp'], writes=['DTA'])
            sch.op('act', lambda e: e.activation(out=au, in_=u_, func=AF.Abs), reads=['DTA'], writes=['DTA'])
            sch.op('act', lambda e: e.activation(out=au, in_=au, func=AF.Exp, scale=-1.0), reads=['DTA'], writes=['DTA'])
            sch.op('act', lambda e: e.activation(out=au, in_=au, func=AF.Ln, bias=1.0, scale=1.0), reads=['DTA'], writes=['DTA'])
            sch.op('dve', lambda e: e.scalar_tensor_tensor(out=dt_, in0=u_, scalar=0.0, in1=au, op0=ALU.max, op1=ALU.add),
                   reads=['DTA'], writes=['DTA'])
            sch.op('dve', lambda e: e.tensor_tensor(out=a_, in0=dt_, in1=P.Aneg[0:NT, :], op=ALU.mult), reads=['DTA', 'Aneg'], writes=['DTA'])

            RHSA = M.RHSA
            for h in range(8):
                sch.op('pool', lambda e, h=h: e.tensor_scalar(out=RHSA[0:NT, h, 0:NT], in0=Tm[0:NT, 0:NT], scalar1=DTA[0:NT, 3, h:h + 1],
                                                              scalar2=0.0, op0=ALU.mult, op1=ALU.add), reads=['DTA', 'consts'], writes=[('RHSA', h)])
            def evac_xbc(bk, bkk, j, n):
                src = bk[:, 0:n * NT].rearrange("p (a s l) -> p a s l", a=n, s=nseg)
                sch.op('act', lambda e: e.activation(out=xbc_t[:, j:j + n, :, 3:3 + L], in_=src, func=AF.Copy),
                       reads=[bkk], writes=[xbc_key])
            proj_fm(C_XBC, 8, evac_xbc)

            if outs.get('sconv') is not None:
                nr = 3 * nseg
                tl = P.TAIL[:, 0:8 * nr].rearrange("p (t s r) -> p t s r", t=8, s=nseg)
                sch.op('pool', lambda e: e.tensor_copy(out=tl, in_=xbc_t[:, :, :, L:L + 3]), reads=[xbc_key], writes=['TAIL'])
                bkt, bktk = bank()
                sch.op('pe', lambda e: e.transpose(out=bkt[0:8 * nr, 0:128], in_=P.TAIL[:, 0:8 * nr], identity=ident),
                       reads=['TAIL', 'consts'], writes=[bktk])
                sch.op('act', lambda e: e.activation(out=P.TT[0:8 * nr, :], in_=bkt[0:8 * nr, 0:128], func=AF.Copy),
                       reads=[bktk], writes=['TT'])
                for t in range(8):
                    sch.dma('sp', outs['sconv'][:, t * 128:(t + 1) * 128], P.TT[t * nr:(t + 1) * nr, :], reads=['TT'])

            for jj in (3, 0, 1, 2):
                for t in range(8):
                    cw = CP_CW + 4 * t
                    acc = ACC[:, t, 0:NT].rearrange("p (s l) -> p s l", s=nseg)
                    if jj == 3:
                        sch.op('dve', lambda e, t=t, cw=cw, acc=acc: e.tensor_scalar(out=acc, in0=xbc_t[:, t, :, 3:3 + L], scalar1=colp[:, cw + 3:cw + 4],
                                                                                     scalar2=colp[:, CP_CB + t:CP_CB + t + 1], op0=ALU.mult, op1=ALU.add),
                               reads=[xbc_key, 'colp'], writes=[('ACC', t)])
                    else:
                        sch.op('dve', lambda e, t=t, cw=cw, acc=acc, jj=jj: e.scalar_tensor_tensor(out=acc, in0=xbc_t[:, t, :, jj:jj + L],
                                                                                                  scalar=colp[:, cw + jj:cw + jj + 1], in1=acc,
                                                                                                  op0=ALU.mult, op1=ALU.add),
                               reads=[xbc_key, 'colp', ('ACC', t)], writes=[('ACC', t)])

            if kind != 'ssm':
                kt_dst, kt_key, va_dst, va_key = kv_dst

                def evac_k(bk, bkk, j, n):
                    src = bk[:, 0:n * NT].rearrange("p (a b) -> p a b", a=n)
                    sch.op('act', lambda e: e.activation(out=kt_dst(j, n), in_=src, func=AF.Copy), reads=[bkk], writes=[kt_key])
                proj_fm(C_K, 4, evac_k)
                bkv, bkvk = bank()
                proj_tm(C_V, 512, bkv, bkvk)
                sch.op('act', lambda e: e.activation(out=va_dst[:, :, 0:64], in_=bkv[0:NT, :].rearrange("p (h d) -> p h d", h=8),
                                                     func=AF.Copy), reads=[bkvk], writes=[va_key])
                if kind == 'kv':
                    sch.op('pool', lambda e: e.tensor_copy(out=va_dst[:, :, 64], in_=flg[0:NT, 0:1].to_broadcast([NT, 8])),
                           reads=['flg'], writes=[va_key])
                else:
                    sch.op('pool', lambda e: e.memset(va_dst[:, :, 64], 1.0), writes=[va_key])
                if outs.get('nv') is not None:
                    sch.op('act', lambda e: e.activation(out=M.T1[0:NT, :], in_=bkv[0:NT, :], func=AF.Copy), reads=[bkvk], writes=['T1'])
                    sch.dma('sp', outs['nv'], M.T1[0:NT, :], reads=['T1'])
                if outs.get('nk') is not None:
                    bkk_, bkkk = bank()
                    proj_tm(C_K, 512, bkk_, bkkk)
                    sch.op('act', lambda e: e.activation(out=M.T2[0:NT, :], in_=bkk_[0:NT, :], func=AF.Copy), reads=[bkkk], writes=['T2'])
                    sch.dma('sp', outs['nk'], M.T2[0:NT, :], reads=['T2'])

            if full:
                def evac_q(bk, bkk, j, n):
                    src = bk[:, 0:n * NT].rearrange("p (a b) -> p a b", a=n)
                    sch.op('act', lambda e: e.activation(out=M.QT[:, j:j + n, 0:NT], in_=src, func=AF.Copy), reads=[bkk], writes=['QT'])
                proj_fm(C_Q, 4, evac_q)
                bkz, bkzk = bank()
                proj_tm(C_Z, 512, bkz, bkzk)
                sch.op('act', lambda e: e.activation(out=M.SZ[0:NT, :], in_=bkz[0:NT, :], func=AF.Silu), reads=[bkzk], writes=['SZ'])

            step_attn = lambda n: None
            if full:
                ob = [(banks[6], ('bk', 6)), (banks[7], ('bk', 7))]
                units = [(h, s_) for h in range(8) for s_ in range(nseg)]

                def attn_front(ui):
                    h, s = units[ui]
                    hp, po = h // 2, (h % 2) * 64
                    pti = ui % 2
                    PTb, ptk, SBb, sbk = M.PT[pti], ('PT', pti), M.SB[pti], ('SB', pti)
                    tiles = attn_tiles(s, h)
                    ctiles = [t_ for t_ in tiles if t_['bias'] is None]
                    btiles = [t_ for t_ in tiles if t_['bias'] is not None]
                    nc_ = len(ctiles)
                    bkA, bkAk = bank()
                    bkB, bkBk = bank()
                    for lst, bk, bkk in ((ctiles, bkA, bkAk), (btiles, bkB, bkBk)):
                        for ii, t_ in enumerate(lst):
                            nk, pb = t_['nk'], t_['pb']
                            sch.op('pe', lambda e, t_=t_, ii=ii, bk=bk, nk=nk, pb=pb, s=s: e.matmul(
                                bk[pb:pb + nk, ii * L:(ii + 1) * L], lhsT=t_['kt'], rhs=M.QT[po:po + 64, hp, s * L:(s + 1) * L], start=True, stop=True),
                                reads=[t_['ktkey'], 'QT'], writes=[bkk], signal=(ii == len(lst) - 1))
                    sch.op('act', lambda e, bkA=bkA, PTb=PTb, h=h: e.activation(
                        out=PTb[:, 0:nc_, 0:L], in_=bkA[:, 0:nc_ * L].rearrange("p (a l) -> p a l", a=nc_), func=AF.Exp,
                        bias=P.cbias[:, h:h + 1], scale=0.125), reads=[bkAk, 'cbias'], writes=[ptk])
                    for ii, t_ in enumerate(btiles):
                        nk, pb = t_['nk'], t_['pb']
                        sch.op('dve', lambda e, t_=t_, ii=ii, nk=nk, pb=pb, bkB=bkB, SBb=SBb: e.scalar_tensor_tensor(
                            out=SBb[pb:pb + nk, ii, 0:L], in0=bkB[pb:pb + nk, ii * L:(ii + 1) * L], scalar=0.125, in1=t_['bias'],
                            op0=ALU.mult, op1=ALU.add), reads=[bkBk, 'biasT', 'biasN'], writes=[sbk])
                        sch.op('act', lambda e, ii=ii, nk=nk, pb=pb, SBb=SBb, PTb=PTb: e.activation(
                            out=PTb[pb:pb + nk, nc_ + ii, 0:L], in_=SBb[pb:pb + nk, ii, 0:L], func=AF.Exp), reads=[sbk], writes=[ptk])
                    alltiles = ctiles + btiles
                    for ii, t_ in enumerate(alltiles):
                        for (p0, p1, q0, q1) in t_.get('zero', ()):
                            sch.op('pool', lambda e, ii=ii, p0=p0, p1=p1, q0=q0, q1=q1, PTb=PTb: e.memset(PTb[p0:p1, ii, q0:q1], 0.0),
                                   reads=[ptk], writes=[ptk])
                    return alltiles

                def attn_back(ui, alltiles):
                    h, s = units[ui]
                    bo, bok = ob[h // 4]
                    pti = ui % 2
                    PTb, ptk = M.PT[pti], ('PT', pti)
                    for ii, t_ in enumerate(alltiles):
                        nk, pb = t_['nk'], t_['pb']
                        sch.op('pe', lambda e, t_=t_, ii=ii, nk=nk, pb=pb, PTb=PTb, bo=bo, s=s, h=h: e.matmul(
                            bo[s * L:(s + 1) * L, (h % 4) * 65:(h % 4) * 65 + 65], lhsT=PTb[pb:pb + nk, ii, 0:L], rhs=t_['va'],
                            start=(ii == 0), stop=(ii == len(alltiles) - 1)),
                            reads=[ptk, t_['vakey']], writes=[bok], signal=(ii == len(alltiles) - 1))

                def attn_gen():
                    pend = attn_front(0)
                    for ui in range(len(units)):
                        nxt = attn_front(ui + 1) if ui + 1 < len(units) else None
                        attn_back(ui, pend)
                        pend = nxt
                        yield
                agen = [attn_gen()]

                def step_attn(n):
                    for _ in range(n * nseg):
                        try:
                            next(agen[0])
                        except StopIteration:
                            return

            while deferred:
                deferred.pop(0)()
            hpb = 512 // NT
            for g0 in range(0, 8, hpb):
                bk, bkk = bank()
                sch.op('pe', lambda e, bk=bk, g0=g0: e.matmul(bk[0:NT, 0:hpb * NT], lhsT=Um[0:NT, 0:NT],
                                                              rhs=RHSA[0:NT, g0:g0 + hpb, 0:NT], start=True, stop=True),
                       reads=[('RHSA', t_) for t_ in range(8)] + ['consts'], writes=[bkk])
                sch.op('act', lambda e, bk=bk, g0=g0: e.activation(out=LM[0:NT, g0:g0 + hpb, 0:NT],
                                                                  in_=bk[0:NT, 0:hpb * NT].rearrange("p (h l) -> p h l", h=hpb), func=AF.Exp),
                       reads=[bkk], writes=['LM'])
            bke, bkek = bank()
            sch.op('pe', lambda e: e.matmul(bke[0:NT, 0:8], lhsT=Tm[0:NT, 0:NT], rhs=DTA[0:NT, 3, :], start=True, stop=True),
                   reads=['DTA', 'consts'], writes=[bkek], signal=False)
            for s in range(nseg):
                osm = cOnes if nseg == 1 else P.consts[:, 6 + s, :]
                sch.op('pe', lambda e, s=s, osm=osm: e.matmul(bke[:, 8 + 8 * s:16 + 8 * s], lhsT=osm[0:NT, :], rhs=DTA[0:NT, 3, :], start=True, stop=True),
                       reads=['DTA', 'consts'], writes=[bkek], signal=(s == nseg - 1))
            sch.op('act', lambda e: e.activation(out=M.EE[0:NT, :], in_=bke[0:NT, 0:8], func=AF.Exp), reads=[bkek], writes=['EE'])
            sch.op('act', lambda e: e.activation(out=M.CD[:, 0:nseg, :], in_=bke[:, 8:8 + 8 * nseg].rearrange("p (s h) -> p s h", s=nseg), func=AF.Exp),
                   reads=[bkek], writes=['CD'])
            for s in range(nseg):
                sch.op('dve', lambda e, s=s: e.tensor_tensor(out=M.WDS[s * L:(s + 1) * L, :], in0=DTA[s * L:(s + 1) * L, 2, :],
                                                             in1=LM[s * L:(s + 1) * L, :, (s + 1) * L - 1], op=ALU.mult),
                       reads=['DTA', 'LM'], writes=['WDS'])

            sch.op('act', lambda e: e.activation(out=SIL[:, :, 0:NT], in_=ACC[:, :, 0:NT], func=AF.Silu), reads=[('ACC', t_) for t_ in range(8)], writes=['SIL'])
            sch.op('act', lambda e: e.activation(out=BCT[:, :, 0:NT], in_=ACC[:, 4:8, 0:NT], func=AF.Silu), reads=[('ACC', t_) for t_ in range(4, 8)], writes=['BCT'])
            step_attn(2)

            bkx, bkxk = bank()
            for q in range(4):
                sch.op('pe', lambda e, q=q: e.transpose(out=bkx[0:NT, q * 128:(q + 1) * 128], in_=SIL[:, q, 0:NT], identity=ident),
                       reads=['SIL', 'consts'], writes=[bkxk], signal=(q == 3))
            xsv = bkx[0:NT, :].rearrange("p (h d) -> p h d", h=8)
            if full:
                sch.op('act', lambda e: e.activation(out=M.XSK[0:NT, :], in_=bkx[0:NT, :], func=AF.Copy), reads=[bkxk], writes=['XSK'])
                sch.op('dve', lambda e: e.tensor_tensor(out=M.XDT[0:NT, :, :], in0=xsv, in1=bc(DTA[0:NT, 2, :], [NT, 8, 64], 2), op=ALU.mult),
                       reads=[bkxk, 'DTA'], writes=['XDT'])
            sch.op('dve', lambda e: e.tensor_tensor(out=M.XDD[0:NT, :, :], in0=xsv, in1=bc(M.WDS[0:NT, :], [NT, 8, 64], 2), op=ALU.mult),
                   reads=[bkxk, 'WDS'], writes=['XDD'])
            step_attn(1)
            bkb, bkbk = bank()
            for g in range(2):
                sch.op('pe', lambda e, g=g: e.transpose(out=bkb[0:NT, g * 128:(g + 1) * 128], in_=SIL[:, 4 + g, 0:NT], identity=ident),
                       reads=['SIL', 'consts'], writes=[bkbk], signal=(g == 1))
            sch.op('act', lambda e: e.activation(out=M.BK[0:NT, :, :], in_=bkb[0:NT, 0:256].rearrange("p (g n) -> p g n", g=2), func=AF.Copy),
                   reads=[bkbk], writes=['BK'])
            step_attn(1)

            if full:
                bkc, bkck = bank()
                for g in range(2):
                    sch.op('pe', lambda e, g=g: e.matmul(bkc[0:NT, g * NT:(g + 1) * NT], lhsT=BCT[:, g, 0:NT], rhs=BCT[:, 2 + g, 0:NT],
                                                         start=True, stop=True), reads=['BCT'], writes=[bkck], signal=(g == 1))
                sch.op('dve', lambda e: e.tensor_tensor(out=M.CBM[0:NT, :, 0:NT], in0=bkc[0:NT, 0:2 * NT].rearrange("p (g l) -> p g l", g=2),
                                                        in1=bc(Tm[0:NT, 0:NT], [NT, 2, NT], 1), op=ALU.mult),
                       reads=[bkck, 'consts'], writes=['CBM'])
                for g in range(2):
                    sch.op(('dve', 'pool')[g], lambda e, g=g: e.tensor_tensor(out=M.MT[0:NT, 4 * g:4 * g + 4, 0:NT], in0=LM[0:NT, 4 * g:4 * g + 4, 0:NT],
                                                                  in1=bc(M.CBM[0:NT, g, 0:NT], [NT, 4, NT], 1), op=ALU.mult),
                           reads=['LM', 'CBM'], writes=[('MT', g)])
                bko, bkok = bank()
                for s in range(nseg):
                    _, HBs, hk = HFs[s]
                    for g in range(2):
                        sch.op('pe', lambda e, s=s, g=g, HBs=HBs: e.matmul(bko[s * L:(s + 1) * L, g * 256:(g + 1) * 256], lhsT=BCT[:, 2 + g, s * L:(s + 1) * L],
                                                                           rhs=HBs[:, g * 256:(g + 1) * 256], start=True, stop=True),
                               reads=['BCT', ('HB', hk)], writes=[bkok], signal=(s == nseg - 1 and g == 1))
                bky, bkyk = bank()
                for h in range(8):
                    sch.op('pe', lambda e, h=h: e.matmul(bky[0:NT, h * 64:(h + 1) * 64], lhsT=M.MT[0:NT, h, 0:NT], rhs=M.XDT[0:NT, h, :],
                                                         start=True, stop=True), reads=[('MT', h // 4), 'XDT'], writes=[bkyk], signal=(h == 7))

            for s in range(nseg):
                HFt, HBt, hk = HFs[s]
                bks, bksk = bank()
                for g in range(2):
                    sch.op('pe', lambda e, s=s, g=g, bks=bks: e.matmul(bks[:, g * 256:(g + 1) * 256], lhsT=M.BK[s * L:(s + 1) * L, g, :],
                                                                       rhs=M.XDD[s * L:(s + 1) * L, 4 * g:4 * g + 4, :], start=True, stop=True),
                           reads=['BK', 'XDD'], writes=[bksk], signal=(g == 1))
                hv = HFt[:].rearrange("p (h d) -> p h d", h=8)
                sch.op('pool', lambda e, s=s, hv=hv: e.tensor_tensor(out=hv, in0=hv, in1=bc(M.CD[:, s, :], [128, 8, 64], 2), op=ALU.mult),
                       reads=[('HF', hk), 'CD', ('HB', hk)], writes=[('HF', hk)])
                sch.op('dve', lambda e, HFt=HFt, bks=bks: e.tensor_tensor(out=HFt[:], in0=HFt[:], in1=bks[:, 0:512], op=ALU.add),
                       reads=[('HF', hk), bksk], writes=[('HF', hk)])
                if outs.get('flag_state'):
                    sch.op('dve', lambda e, HFt=HFt: e.tensor_scalar(out=HFt[:], in0=HFt[:], scalar1=flg[:, 0:1], scalar2=None, op0=ALU.mult),
                           reads=[('HF', hk), 'flg'], writes=[('HF', hk)])
                sch.op('act', lambda e, HFt=HFt, HBt=HBt: e.activation(out=HBt[:], in_=HFt[:], func=AF.Copy), reads=[('HF', hk)], writes=[('HB', hk)])

            if not full:
                return

            T1, T2, YY, AST = M.T1, M.T2, M.YY, M.AST
            v8 = lambda ap: ap.rearrange("p (h d) -> p h d", h=8)
            sch.op('dve', lambda e: e.tensor_tensor(out=v8(T1[0:NT, :]), in0=v8(bko[0:NT, :]), in1=bc(M.EE[0:NT, :], [NT, 8, 64], 2), op=ALU.mult),
                   reads=[bkok, 'EE'], writes=['T1'])
            sch.op('pool', lambda e: e.tensor_tensor(out=v8(T2[0:NT, :]), in0=v8(M.XSK[0:NT, :]), in1=bc(rowp[0:NT, 16:24], [NT, 8, 64], 2), op=ALU.mult),
                   reads=['XSK', 'rowp'], writes=['T2'])
            sch.op('dve', lambda e: e.tensor_tensor(out=T2[0:NT, :], in0=T2[0:NT, :], in1=T1[0:NT, :], op=ALU.add), reads=['T1', 'T2'], writes=['T2'])
            sch.op('dve', lambda e: e.tensor_tensor(out=YY[0:NT, :], in0=bky[0:NT, :], in1=T2[0:NT, :], op=ALU.add), reads=[bkyk, 'T2'], writes=['YY'])
            dump(0, YY[0:NT, :], NT, 512, 'YY')
            dump(1, M.SZ[0:NT, :], NT, 512, 'SZ')
            dump(5, T1[0:NT, :], NT, 512, 'T1')
            dump(6, M.XSK[0:NT, :], NT, 512, 'XSK')
            dump(7, M.EE[0:NT, :], NT, 8, 'EE')
            dump(8, M.DTA[0:NT, :, :].rearrange("p a b -> p (a b)"), NT, 32, 'DTA')
            sch.op('dve', lambda e: e.tensor_tensor(out=YY[0:NT, :], in0=YY[0:NT, :], in1=M.SZ[0:NT, :], op=ALU.mult), reads=['YY', 'SZ'], writes=['YY'])
            step_attn(2)
            sch.op('act', lambda e: e.activation(out=M.JB[0:NT, :], in_=YY[0:NT, :], func=AF.Square, accum_out=SMALL[0:NT, 0:1]),
                   reads=['YY'], writes=['JB', ('SM', 0)])
            rstd_from_sumsq(SMALL[0:NT, 0:1], 512.0, NT, ('SM', 0))
            sch.op('act', lambda e: e.activation(out=YY[0:NT, :], in_=YY[0:NT, :], func=AF.Identity, scale=SMALL[0:NT, 0:1]),
                   reads=['YY', ('SM', 0)], writes=['YY'])

            def post_s(j, n, src, bkk):
                sch.op('dve', lambda e: e.tensor_tensor(out=AST[:, 4:8, 0:NT], in0=src, in1=bc(colp[:, CP_SG:CP_SG + 4], [128, 4, NT], 2), op=ALU.mult),
                       reads=[bkk, 'colp'], writes=['AST'])
            transpose_to(None, lambda j: YY[0:NT, j * 128:(j + 1) * 128], NT, 4, reads=['YY'], writes=['AST'], post=post_s)

            step_attn(1000)
            AA = T1
            AA3 = AA[0:NT, :].rearrange("p (h d) -> p h d", h=8)
            if dbg:
                sch.op('act', lambda e: e.activation(out=T2[0:NT, 0:260], in_=ob[0][0][0:NT, 0:260], func=AF.Copy), reads=[ob[0][1]], writes=['T2'])
                dump(9, T2[0:NT, 0:260], NT, 260, 'T2')
                sch.op('act', lambda e: e.activation(out=T2[:, 0:320].rearrange("p (a b) -> p a b", a=5), in_=M.PT[1][:, :, 0:64], func=AF.Copy), reads=[('PT', 1)], writes=['T2'])
                dump(10, T2[:, 0:320], 128, 320, 'T2')
            for hh in range(2):
                bo, bok = ob[hh]
                ov = bo[0:NT, 0:260].rearrange("p (h d) -> p h d", h=4)
                sch.op('dve', lambda e, ov=ov, hh=hh: e.reciprocal(out=SMALL[0:NT, 8 + 4 * hh:12 + 4 * hh], in_=ov[:, :, 64]),
                       reads=[bok], writes=[('SM', 1 + hh)])
                sch.op('dve', lambda e, ov=ov, hh=hh: e.tensor_tensor(out=AA3[:, 4 * hh:4 * hh + 4, :], in0=ov[:, :, 0:64],
                                                                      in1=bc(SMALL[0:NT, 8 + 4 * hh:12 + 4 * hh], [NT, 4, 64], 2), op=ALU.mult),
                       reads=[bok, ('SM', 1 + hh)], writes=['T1'])
            AAf = AA[0:NT, :]
            dump(2, AAf, NT, 512, 'T1')
            sch.op('act', lambda e: e.activation(out=M.JB[0:NT, :], in_=AAf, func=AF.Square, accum_out=SMALL[0:NT, 1:2]),
                   reads=['T1'], writes=['JB', ('SM', 3)])
            rstd_from_sumsq(SMALL[0:NT, 1:2], 512.0, NT, ('SM', 3))
            sch.op('act', lambda e: e.activation(out=AAf, in_=AAf, func=AF.Identity, scale=SMALL[0:NT, 1:2]),
                   reads=['T1', ('SM', 3)], writes=['T1'])

            def post_a(j, n, src, bkk):
                sch.op('dve', lambda e: e.tensor_tensor(out=AST[:, 0:4, 0:NT], in0=src, in1=bc(colp[:, CP_AG:CP_AG + 4], [128, 4, NT], 2), op=ALU.mult),
                       reads=[bkk, 'colp'], writes=['AST'])
            transpose_to(None, lambda j: AAf[:, j * 128:(j + 1) * 128], NT, 4, reads=['T1'], writes=['AST'], post=post_a)

            for c in range(2):
                bk, bkk = bank()
                for kt in range(KT):
                    sch.op('pe', lambda e, kt=kt, c=c, bk=bk: e.matmul(bk[0:NT, :], lhsT=AST[:, kt, 0:NT], rhs=Wo[:, kt, c * 512:(c + 1) * 512],
                                                                       start=(kt == 0), stop=(kt == KT - 1)),
                           reads=['AST', ('Wo', kt)], writes=[bkk], signal=(kt == KT - 1))
                sch.op('dve', lambda e, c=c, bk=bk: e.scalar_tensor_tensor(out=Xb[0:NT, c * 512:(c + 1) * 512], in0=Xb[0:NT, c * 512:(c + 1) * 512],
                                                                           scalar=ALPHA, in1=bk[0:NT, :], op0=ALU.mult, op1=ALU.add),
                       reads=[xk, bkk], writes=[xk])
            x1 = P.X1S[0:NT, j_in_super, :]
            dump(3, Xb[0:NT, :], NT, D, xk)
            layer_norm(Xb[0:NT, :], x1, NT, 0, xk, ('X1S', j_in_super))
            dump(4, x1, NT, D, ('X1S', j_in_super))
            t0 = j_in_super * 128
            def x1T_later():
                transpose_to(lambda j, n: P.X1T[:, j:j + n, t0:t0 + NT], lambda j: x1[:, j * 128:(j + 1) * 128], NT, KT,
                             reads=[('X1S', j_in_super), (('X1S', j_in_super), 0), (('X1S', j_in_super), 1)], writes=[('X1T', j_in_super)], engs=('act', 'act'))
            deferred.append(x1T_later)

        def ffn(Fb, NTs, nseg, L, jlist, ft_t, ft_key, y_dsts, fconv_out, tails_only, defer_ln2=False):
            bstate['nb'] = 8
            x1keys = [('X1T', j) for j in range(4)]
            pend_gate = [None]
            for f in range(NF):
                ui = f % 3
                sch.dma('sp', Fb.Wup[ui][:, :, :], wb_up[:, f * 256:(f + 1) * 256].rearrange("(kt p) n -> p kt n", p=128),
                        reads=WUPK, writes=[('Wup', ui)])
                hbs = []
                for part in range(2):
                    fp = part * NF + f
                    bk, bkk = bank()
                    for kt in range(KT):
                        sch.op('pe', lambda e, kt=kt, bk=bk, ui=ui, part=part: e.matmul(bk[:, 0:NTs], lhsT=Fb.Wup[ui][:, kt, part * 128:(part + 1) * 128],
                                                                                        rhs=P.X1T[:, kt, 0:NTs], start=(kt == 0), stop=(kt == KT - 1)),
                               reads=x1keys + [('Wup', ui)], writes=[bkk], signal=(kt == KT - 1))
                    hi_ = part * 2 + (f % 2)
                    ps3 = bk[:, 0:NTs].rearrange("p (s l) -> p s l", s=nseg)
                    hd = Fb.HEAD[hi_][:, 0:nseg * 4].rearrange("p (s c) -> p s c", s=nseg)
                    hdk = ('HEAD', hi_)
                    if not tails_only:
                        sch.op('pool', lambda e, hd=hd, fp=fp: e.tensor_copy(out=hd[:, :, 0:2], in_=ft_t[:, fp, :, :]), reads=[(ft_key, fp)], writes=[hdk])
                        sch.op('act', lambda e, hd=hd, ps3=ps3: e.activation(out=hd[:, :, 2:4], in_=ps3[:, :, 0:2], func=AF.Copy), reads=[bkk], writes=[hdk])
                    sch.op('act', lambda e, ps3=ps3, fp=fp: e.activation(out=ft_t[:, fp, :, :], in_=ps3[:, :, L - 2:L], func=AF.Copy),
                           reads=[bkk], writes=[(ft_key, fp)])
                    if not tails_only:
                        cw = CP_FW + 3 * fp
                        sch.op('act', lambda e, bk=bk, hi_=hi_, cw=cw, fp=fp: e.activation(
                            out=Fb.FACC[hi_][:, 0:NTs], in_=bk[:, 0:NTs], func=AF.Identity, scale=colp[:, cw + 2:cw + 3],
                            bias=colp[:, CP_FB + fp:CP_FB + fp + 1]), reads=[bkk, 'colp'], writes=[('FACC', hi_), ('FACCh', hi_)])
                    hbs.append((ps3, bkk, hd, hdk, fp, hi_))
                if tails_only:
                    continue
                for jj in (1, 0):
                    for (ps3, bkk, hd, hdk, fp, hi_) in hbs:
                        fa = Fb.FACC[hi_][:, 0:NTs].rearrange("p (s l) -> p s l", s=nseg)
                        cw = CP_FW + 3 * fp
                        sch.op('dve', lambda e, ps3=ps3, fa=fa, cw=cw, jj=jj: e.scalar_tensor_tensor(out=fa[:, :, 2:L], in0=ps3[:, :, jj:jj + L - 2],
                                                                                                scalar=colp[:, cw + jj:cw + jj + 1], in1=fa[:, :, 2:L],
                                                                                                op0=ALU.mult, op1=ALU.add),
                               reads=[bkk, 'colp', ('FACC', hi_)], writes=[('FACC', hi_)])
                for jj in (1, 0):
                    for (ps3, bkk, hd, hdk, fp, hi_) in hbs:
                        fa = Fb.FACC[hi_][:, 0:NTs].rearrange("p (s l) -> p s l", s=nseg)
                        cw = CP_FW + 3 * fp
                        sch.op('dve', lambda e, hd=hd, fa=fa, cw=cw, jj=jj: e.scalar_tensor_tensor(out=fa[:, :, 0:2], in0=hd[:, :, jj:jj + 2],
                                                                                              scalar=colp[:, cw + jj:cw + jj + 1], in1=fa[:, :, 0:2],
                                                                                              op0=ALU.mult, op1=ALU.add),
                               reads=[hdk, 'colp', ('FACCh', hi_)], writes=[('FACCh', hi_)])
                def gate(f=f):
                    fv, fg = Fb.FACC[f % 2], Fb.FACC[2 + f % 2]
                    kv_, kg_ = ('FACC', f % 2), ('FACC', 2 + f % 2)
                    kvh, kgh = ('FACCh', f % 2), ('FACCh', 2 + f % 2)
                    sch.op('act', lambda e: e.activation(out=fg[:, 0:NTs], in_=fg[:, 0:NTs], func=AF.Silu), reads=[kg_, kgh], writes=[kg_, kgh])
                    sch.op('pool', lambda e: e.tensor_tensor(out=Fb.GT[:, f, 0:NTs], in0=fv[:, 0:NTs], in1=fg[:, 0:NTs], op=ALU.mult),
                           reads=[kv_, kg_, kvh, kgh], writes=[('GT', f)])
                if pend_gate[0] is not None:
                    pend_gate[0]()
                pend_gate[0] = gate
            if pend_gate[0] is not None:
                pend_gate[0]()
                pend_gate[0] = None
            if fconv_out is not None:
                nr = 2 * nseg
                per = 128 // nr // 2 * 2
                per = min(per, 42)
                f0 = 0
                while f0 < 42:
                    nf_ = min(per, 42 - f0)
                    ftf = ft_t[:, f0:f0 + nf_, :, :].rearrange("p f s r -> p (f s r)")
                    bkt, bktk = bank()
                    sch.op('pe', lambda e, ftf=ftf, bkt=bkt, nf_=nf_: e.transpose(out=bkt[0:nf_ * nr, 0:128], in_=ftf, identity=ident),
                           reads=[(ft_key, q_) for q_ in range(42)] + ['consts'], writes=[bktk])
                    sch.op('act', lambda e, bkt=bkt, nf_=nf_: e.activation(out=P.TT[0:nf_ * nr, :], in_=bkt[0:nf_ * nr, 0:128], func=AF.Copy),
                           reads=[bktk], writes=['TT'])
                    for q in range(nf_):
                        fp = f0 + q
                        sch.dma('sp', fconv_out[:, fp * 128:(fp + 1) * 128], P.TT[q * nr:(q + 1) * nr, :], reads=['TT'])
                    f0 += nf_
            if tails_only:
                return
            gkeys = [('GT', f) for f in range(NF)]
            for cb in range(4):
                di = cb % 2
                sch.dma('sp', Fb.Wdn[di][:, :, :], wb_down[:, cb * 256:(cb + 1) * 256].rearrange("(f p) n -> p f n", p=128),
                        reads=WDNK, writes=[('Wdn', di)])
                for (j, t0, nt) in jlist:
                    bk, bkk = bank()
                    for f in range(NF):
                        sch.op('pe', lambda e, f=f, bk=bk, di=di, t0=t0, nt=nt: e.matmul(bk[0:nt, 0:256], lhsT=Fb.GT[:, f, t0:t0 + nt], rhs=Fb.Wdn[di][:, f, :],
                                                                                         start=(f == 0), stop=(f == NF - 1)),
                               reads=gkeys + [('Wdn', di)], writes=[bkk], signal=(f == NF - 1))
                    xs_ = P.X1S[0:nt, j, cb * 256:(cb + 1) * 256]
                    sch.op('dve', lambda e, xs_=xs_, bk=bk, nt=nt: e.scalar_tensor_tensor(out=xs_, in0=xs_, scalar=ALPHA, in1=bk[0:nt, 0:256],
                                                                                         op0=ALU.mult, op1=ALU.add),
                           reads=[('X1S', j), (('X1S', j), 0), (('X1S', j), 1), bkk], writes=[('X1S', j)])
            def ln2_later():
                for n_, (j, t0, nt) in enumerate(jlist):
                    layer_norm(P.X1S[0:nt, j, :], P.X1S[0:nt, j, :], nt, 2, ('X1S', j), ('X1S', j))
                    sch.dma('sp', y_dsts[n_], P.X1S[0:nt, j, :], reads=[('X1S', j), (('X1S', j), 0), (('X1S', j), 1)])
            if defer_ln2:
                return ln2_later
            ln2_later()

        def final_state(HFt, hk, dst, CK):
            def post_h(j, n, src, bkk):
                sch.op('act', lambda e: e.activation(out=CK[:, 0:4, 0:128], in_=src, func=AF.Copy), reads=[bkk], writes=['CKo'])
            transpose_to(None, lambda j: HFt[:, j * 128:(j + 1) * 128], 128, 4, reads=[('HF', hk)], writes=['CKo'], post=post_h)
            sch.dma('sp', dst.rearrange("(a p) n -> p a n", p=128), CK[:, 0:4, 0:128], reads=['CKo'])


        def light_bufs(st):
            Lb = NS()
            Lb.XBCL = sbuf(st, "XBCL", [128, 8, 515])
            Lb.ACCL = sbuf(st, "ACCL", [128, 8, 512])
            Lb.SILL = sbuf(st, "SILL", [128, 8, 512])
            Lb.DTAL = sbuf(st, "DTAL", [128, 4, 4, 8])
            Lb.WDSL = sbuf(st, "WDSL", [128, 4, 8])
            Lb.CDL = sbuf(st, "CDL", [128, 8])
            Lb.XDDL = sbuf(st, "XDDL", [128, 4, 8, 64], BF16)
            Lb.BKL = sbuf(st, "BKL", [128, 4, 2, 128], BF16)
            sch.op('dve', lambda e: e.memset(Lb.XBCL[:, :, 0:3], 0.0), writes=[('XBCL', t_) for t_ in range(8)])
            return Lb

        def light_tile(Lb, b0, nb, kv, last):
            bstate['nb'] = 6
            NT = nb * 128
            XL = P.X1S
            XTL = P.X1T
            XBCL, ACCL, SILL, DTAL = Lb.XBCL, Lb.ACCL, Lb.SILL, Lb.DTAL
            sch.dma('sp', XL[:, 0:nb, :], xs[b0 * 128:(b0 + nb) * 128, :].rearrange("(c p) d -> p c d", p=128), writes=['XL'])
            for c in range(nb):
                transpose_to(lambda j, n, c=c: XTL[:, j:j + n, c * 128:(c + 1) * 128], lambda j, c=c: XL[:, c, j * 128:(j + 1) * 128], 128, KT,
                             reads=['XL'], writes=[('XTL', c)])
            xtk = [('XTL', c) for c in range(nb)]
            for t in range(8):
                bk, bkk = bank()
                cc = C_XBC + t * 128
                for kt in range(KT):
                    sch.op('pe', lambda e, kt=kt, cc=cc, bk=bk: e.matmul(bk[:, 0:NT], lhsT=Wi[:, kt, cc:cc + 128], rhs=XTL[:, kt, 0:NT],
                                                                         start=(kt == 0), stop=(kt == KT - 1)),
                           reads=xtk + [('Wi', kt)], writes=[bkk], signal=(kt == KT - 1))
                sch.op('act', lambda e, t=t, bk=bk: e.activation(out=XBCL[:, t, 3:3 + NT], in_=bk[:, 0:NT], func=AF.Copy),
                       reads=[bkk], writes=[('XBCL', t)])
                cw_ = CP_CW + 4 * t
                sch.op('act', lambda e, t=t, bk=bk, cw_=cw_: e.activation(out=ACCL[:, t, 0:NT], in_=bk[:, 0:NT], func=AF.Identity,
                                                                          scale=colp[:, cw_ + 3:cw_ + 4], bias=colp[:, CP_CB + t:CP_CB + t + 1]),
                       reads=[bkk, 'colp'], writes=[('ACCL', t)])
            bkd, bkdk = bank()
            for c in range(nb):
                for kt in range(KT):
                    sch.op('pe', lambda e, kt=kt, c=c: e.matmul(bkd[:, c * 8:(c + 1) * 8], lhsT=XTL[:, kt, c * 128:(c + 1) * 128], rhs=Wi[:, kt, C_DT:C_DT + 8],
                                                                start=(kt == 0), stop=(kt == KT - 1)),
                           reads=xtk + [('Wi', kt)], writes=[bkdk], signal=(c == nb - 1 and kt == KT - 1))
            u_, au, dt_, a_ = (DTAL[:, i, 0:nb, :] for i in range(4))
            sch.op('dve', lambda e: e.tensor_tensor(out=u_, in0=bkd[:, 0:nb * 8].rearrange("p (c h) -> p c h", c=nb),
                                                    in1=bc(rowp[:, 0:8], [128, nb, 8], 1), op=ALU.add), reads=[bkdk, 'rowp'], writes=['DTAL'])
            sch.op('act', lambda e: e.activation(out=au, in_=u_, func=AF.Abs), reads=['DTAL'], writes=['DTAL'])
            sch.op('act', lambda e: e.activation(out=au, in_=au, func=AF.Exp, scale=-1.0), reads=['DTAL'], writes=['DTAL'])
            sch.op('act', lambda e: e.activation(out=au, in_=au, func=AF.Ln, bias=1.0, scale=1.0), reads=['DTAL'], writes=['DTAL'])
            sch.op('dve', lambda e: e.scalar_tensor_tensor(out=dt_, in0=u_, scalar=0.0, in1=au, op0=ALU.max, op1=ALU.add),
                   reads=['DTAL'], writes=['DTAL'])
            sch.op('dve', lambda e: e.tensor_tensor(out=a_, in0=dt_, in1=bc(P.Aneg[:, :], [128, nb, 8], 1), op=ALU.mult),
                   reads=['DTAL', 'Aneg'], writes=['DTAL'])
            if kv:
                for hp in range(4):
                    bk, bkk = bank()
                    cc = C_K + hp * 128
                    for kt in range(KT):
                        sch.op('pe', lambda e, kt=kt, cc=cc, bk=bk: e.matmul(bk[:, 0:NT], lhsT=Wi[:, kt, cc:cc + 128], rhs=XTL[:, kt, 0:NT],
                                                                             start=(kt == 0), stop=(kt == KT - 1)),
                               reads=xtk + [('Wi', kt)], writes=[bkk], signal=(kt == KT - 1))
                    for c in range(nb):
                        sl = (b0 + c) % NKR
                        sch.op('dve', lambda e, c=c, sl=sl, hp=hp, bk=bk: e.tensor_copy(out=P.KTr[:, sl, hp, :], in_=bk[:, c * 128:(c + 1) * 128]),
                               reads=[bkk], writes=[('KTr', sl)])
                for c in range(nb):
                    sl = (b0 + c) % NKR
                    bkv, bkvk = bank()
                    for kt in range(KT):
                        sch.op('pe', lambda e, kt=kt, c=c, bkv=bkv: e.matmul(bkv[:, :], lhsT=XTL[:, kt, c * 128:(c + 1) * 128], rhs=Wi[:, kt, C_V:C_V + 512],
                                                                             start=(kt == 0), stop=(kt == KT - 1)),
                               reads=xtk + [('Wi', kt)], writes=[bkvk], signal=(kt == KT - 1))
                    sch.op('act', lambda e, sl=sl, bkv=bkv: e.activation(out=P.VAr[:, sl, :, 0:64], in_=bkv[:, :].rearrange("p (h d) -> p h d", h=8),
                                                                         func=AF.Copy), reads=[bkvk], writes=[('VAr', sl)])
                    sch.op('dve', lambda e, sl=sl: e.tensor_copy(out=P.VAr[:, sl, :, 64], in_=flg[:, 0:1].to_broadcast([128, 8])),
                           reads=['flg'], writes=[('VAr', sl)])
            for jj in (0, 1, 2):
                for t in range(8):
                    cw = CP_CW + 4 * t
                    acc = ACCL[:, t, 0:NT]
                    if jj == 3:
                        sch.op('dve', lambda e, t=t, cw=cw, acc=acc: e.tensor_scalar(out=acc, in0=XBCL[:, t, 3:3 + NT], scalar1=colp[:, cw + 3:cw + 4],
                                                                                     scalar2=colp[:, CP_CB + t:CP_CB + t + 1], op0=ALU.mult, op1=ALU.add),
                               reads=[('XBCL', t), 'colp'], writes=[('ACCL', t)])
                    else:
                        sch.op('dve', lambda e, t=t, cw=cw, acc=acc, jj=jj: e.scalar_tensor_tensor(out=acc, in0=XBCL[:, t, jj:jj + NT],
                                                                                                  scalar=colp[:, cw + jj:cw + jj + 1], in1=acc,
                                                                                                  op0=ALU.mult, op1=ALU.add),
                               reads=[('XBCL', t), 'colp', ('ACCL', t)], writes=[('ACCL', t)])
            for hf_ in range(2):
                sch.op('act', lambda e, hf_=hf_: e.activation(out=SILL[:, 4 * hf_:4 * hf_ + 4, 0:NT], in_=ACCL[:, 4 * hf_:4 * hf_ + 4, 0:NT], func=AF.Silu),
                       reads=[('ACCL', t_) for t_ in range(4 * hf_, 4 * hf_ + 4)], writes=[('SILL', hf_)])
            if last:
                sch.op('dve', lambda e: e.tensor_copy(out=P.XBC[:, :, 128:131], in_=XBCL[:, :, NT:NT + 3]),
                       reads=[('XBCL', t_) for t_ in range(8)], writes=['XBC'])
            else:
                sch.op('dve', lambda e: e.tensor_copy(out=XBCL[:, :, 0:3], in_=XBCL[:, :, NT:NT + 3]),
                       reads=[('XBCL', t_) for t_ in range(8)] + [('ACCL', t_) for t_ in range(8)], writes=[('XBCL', t_) for t_ in range(8)])
            bke, bkek = bank()
            for c in range(nb):
                ops_ = [(cU, c)] + [(cOnes, c2) for c2 in range(c + 1, nb)]
                for ii, (lm, c2) in enumerate(ops_):
                    sch.op('pe', lambda e, c=c, lm=lm, c2=c2, ii=ii, n_=len(ops_): e.matmul(bke[:, c * 8:(c + 1) * 8], lhsT=lm, rhs=DTAL[:, 3, c2, :],
                                                                                          start=(ii == 0), stop=(ii == n_ - 1)),
                           reads=['DTAL', 'consts'], writes=[bkek], signal=False)
            for c in range(nb):
                sch.op('pe', lambda e, c=c: e.matmul(bke[:, 32:40], lhsT=cOnes, rhs=DTAL[:, 3, c, :], start=(c == 0), stop=(c == nb - 1)),
                       reads=['DTAL', 'consts'], writes=[bkek], signal=(c == nb - 1))
            sch.op('act', lambda e: e.activation(out=Lb.WDSL[:, 0:nb, :], in_=bke[:, 0:nb * 8].rearrange("p (c h) -> p c h", c=nb), func=AF.Exp),
                   reads=[bkek], writes=['WDSL'])
            sch.op('act', lambda e: e.activation(out=Lb.CDL[:, :], in_=bke[:, 32:40], func=AF.Exp), reads=[bkek], writes=['CDL'])
            sch.op('dve', lambda e: e.tensor_tensor(out=Lb.WDSL[:, 0:nb, :], in0=Lb.WDSL[:, 0:nb, :], in1=dt_, op=ALU.mult),
                   reads=['WDSL', 'DTAL'], writes=['WDSL'])
            for c in range(nb):
                bkx, bkxk = bank()
                for q in range(4):
                    sch.op('pe', lambda e, q=q, c=c, bkx=bkx: e.transpose(out=bkx[:, q * 128:(q + 1) * 128], in_=SILL[:, q, c * 128:(c + 1) * 128], identity=ident),
                           reads=[('SILL', 0), 'consts'], writes=[bkxk], signal=(q == 3))
                sch.op('dve', lambda e, c=c, bkx=bkx: e.tensor_tensor(out=Lb.XDDL[:, c, :, :], in0=bkx[:, :].rearrange("p (h d) -> p h d", h=8),
                                                                      in1=bc(Lb.WDSL[:, c, :], [128, 8, 64], 2), op=ALU.mult),
                       reads=[bkxk, 'WDSL'], writes=[('XDDL', c)])
            for c0 in range(0, nb, 2):
                n2 = min(2, nb - c0)
                bkb, bkbk = bank()
                for cc_ in range(n2):
                    for g in range(2):
                        sch.op('pe', lambda e, cc_=cc_, g=g, c0=c0, bkb=bkb: e.transpose(out=bkb[:, (cc_ * 2 + g) * 128:(cc_ * 2 + g + 1) * 128],
                                                                                         in_=SILL[:, 4 + g, (c0 + cc_) * 128:(c0 + cc_ + 1) * 128], identity=ident),
                               reads=[('SILL', 1), 'consts'], writes=[bkbk], signal=(cc_ == n2 - 1 and g == 1))
                sch.op('act', lambda e, c0=c0, n2=n2, bkb=bkb: e.activation(out=Lb.BKL[:, c0:c0 + n2, :, :],
                                                                           in_=bkb[:, 0:n2 * 256].rearrange("p (c g n) -> p c g n", c=n2, g=2), func=AF.Copy),
                       reads=[bkbk], writes=[('BKL', c0 // 2)])
            bks, bksk = bank()
            for g in range(2):
                for c in range(nb):
                    sch.op('pe', lambda e, g=g, c=c: e.matmul(bks[:, g * 256:(g + 1) * 256], lhsT=Lb.BKL[:, c, g, :], rhs=Lb.XDDL[:, c, 4 * g:4 * g + 4, :],
                                                              start=(c == 0), stop=(c == nb - 1)),
                           reads=[('BKL', c // 2), ('XDDL', c)], writes=[bksk], signal=(g == 1 and c == nb - 1))
            hv = P.HF[:].rearrange("p (h d) -> p h d", h=8)
            sch.op('dve', lambda e: e.tensor_tensor(out=hv, in0=hv, in1=bc(Lb.CDL[:, :], [128, 8, 64], 2), op=ALU.mult),
                   reads=[('HF', 0), 'CDL'], writes=[('HF', 0)])
            sch.op('dve', lambda e: e.tensor_tensor(out=P.HF[:], in0=P.HF[:], in1=bks[:, 0:512], op=ALU.add),
                   reads=[('HF', 0), bksk], writes=[('HF', 0)])
            if last:
                sch.op('act', lambda e: e.activation(out=P.HB[:], in_=P.HF[:], func=AF.Copy), reads=[('HF', 0)], writes=[('HB', 0)])

        def tiles_p(b):
            def f(s, h):
                hp, po = h // 2, (h % 2) * 64
                tl = []
                for i in (2, 3, 4, 0, 1):
                    sl = (b - i) % NKR
                    t_ = dict(kt=P.KTr[po:po + 64, sl, hp, :], ktkey=('KTr', sl), va=P.VAr[:, sl, h, :], vakey=('VAr', sl), nk=128, pb=0,
                              bias=None if i >= 2 else P.biasT[:, i, h, :])
                    if i == 4:
                        t_['zero'] = [(0, 64, 64, 128)]
                    if i == 0:
                        t_['zero'] = [(64, 128, 0, 64)]
                    tl.append(t_)
                return tl
            return f

        def run_blocks(M, b_lo, b_hi):
            for b in range(b_lo, b_hi):
                kind = 'ssm' if b < B_KV else ('kv' if b < B_FULL else 'full')
                sch.op('pool', lambda e: e.tensor_copy(out=P.XBC[:, :, 0:3], in_=P.XBC[:, :, 128:131]), reads=['XBC'], writes=['XBC'])
                outs = {}
                if b >= NBLK - 4:
                    r0 = (b - (NBLK - 4)) * 128
                    outs['nk'] = nk_o[r0:r0 + 128, :]
                    outs['nv'] = nv_o[r0:r0 + 128, :]
                if b == NBLK - 1:
                    outs['sconv'] = sc_o
                if b == B_FULL:
                    outs['flag_state'] = True
                j_in_super = 0 if b <= B_FULL else (b - B_MAIN) % 4
                sl = b % NKR
                kvd = (lambda j, n, sl=sl: P.KTr[:, sl, j:j + n, :], ('KTr', sl), P.VAr[:, sl, :, :], ('VAr', sl))
                mixer_block(M, kind, 128, 1, 128, xs[b * 128:(b + 1) * 128, :], b % 2,
                            P.XBC[:].rearrange("p t (s l) -> p t s l", s=1), 'XBC', [(P.HF, P.HB, 0)], tiles_p(b), outs, kvd, j_in_super)

        if do_prompt:
            sP = ExitStack()
            P.KTr = sbuf(sP, "KTr", [128, NKR, 4, 128], BF16)
            P.VAr = sbuf(sP, "VAr", [128, NKR, 8, 65], BF16)
            P.HF = sbuf(sP, "HF0", [128, 512])
            P.HB = sbuf(sP, "HB0", [128, 512], BF16)
            P.FT = sbuf(sP, "FT", [128, 42, 2])
            sch.op('dve', lambda e: e.memset(P.HF[:], 0.0), writes=[('HF', 0)])
            sch.op('dve', lambda e: e.memset(P.HB[:], 0.0), writes=[('HB', 0)])
            sch.op('dve', lambda e: e.memset(P.FT[:], 0.0), writes=[('FT', q_) for q_ in range(42)])
            FT4 = P.FT[:].rearrange("p f (s r) -> p f s r", s=1)
            sch.barrier(skip_pool_dma=True)
            if nblk == NBLK:
                with ExitStack() as st:
                    Lb = light_bufs(st)
                    tiles_ = [(0, 3)] + [(3 + 4 * i, 4) for i in range(7)]
                    for (b0_, nb_) in tiles_:
                        light_tile(Lb, b0_, nb_, b0_ + nb_ == B_FULL, b0_ + nb_ == B_FULL)
                    sch.barrier(skip_pool_dma=True)
            with ExitStack() as st:
                M = mixer_bufs(st)
                run_blocks(M, B_FULL if nblk == NBLK else NBLK - nblk, B_FULL + 1)
                sl = B_FULL % NKR
                sch.op('pool', lambda e: e.tensor_copy(out=P.VAr[:, sl, :, 64], in_=flg[:, 0:1].to_broadcast([128, 8])),
                       reads=['flg', ('VAr', sl)], writes=[('VAr', sl)])
                sch.barrier()
            with ExitStack() as st:
                Fb = ffn_bufs(st)
                ffn(Fb, 128, 1, 128, [(0, 0, 128)], FT4, 'FT', [None], None, True)
                sch.op('dve', lambda e: e.tensor_scalar(out=P.FT[:], in0=P.FT[:], scalar1=flg[:, 0:1], scalar2=None, op0=ALU.mult),
                       reads=[('FT', q_) for q_ in range(42)] + ['flg'], writes=[('FT', q_) for q_ in range(42)])
                sch.barrier()
            pend_ln2 = None
            for b0 in range(B_MAIN, NBLK, 4):
                with ExitStack() as st:
                    M = mixer_bufs(st)
                    if pend_ln2 is not None:
                        pend_ln2()
                        pend_ln2 = None
                    run_blocks(M, b0, b0 + 4)
                    if b0 + 4 == NBLK:
                        final_state(P.HF, 0, ssm_o, M.LM[:, 0:4, :])
                    sch.barrier()
                with ExitStack() as st:
                    Fb = ffn_bufs(st)
                    jl = [(j, j * 128, 128) for j in range(4)]
                    yd = [y_o[(b0 - B_MAIN + j) * 128:(b0 - B_MAIN + j + 1) * 128, :] for j in range(4)]
                    pend_ln2 = ffn(Fb, 512, 1, 512, jl, FT4, 'FT', yd, fc_o if b0 + 4 == NBLK else None, False, defer_ln2=(b0 + 4 < NBLK))
                    sch.barrier()
            sP.close()
            sch.barrier()
        if do_sample:
            sA = ExitStack()
            S = NS()
            S.FTs = sbuf(sA, "FTs", [128, 42, 2, 2])
            with ExitStack() as st:
                M = mixer_bufs(st)
                S.biasN = sbuf(st, "biasN", [64, 8, 32])
                S.XBCs = sbuf(st, "XBCs", [128, 8, 2, 35])
                S.KTc = [sbuf(st, "KTc%d" % i, [128, 4, 512], BF16) for i in range(2)]
                S.VAc = [sbuf(st, "VAc%d" % i, [128, 4, 8, 65], BF16) for i in range(2)]
                S.CKf = sbuf(st, "CKf", [128, 512])
                S.CKg = sbuf(st, "CKg", [128, 512])
                CK3 = S.CKf[:].rearrange("p (a b) -> p a b", a=4)
                S.HF = [sbuf(st, "HFs%d" % i, [128, 512]) for i in range(2)]
                S.HB = [sbuf(st, "HBs%d" % i, [128, 512], BF16) for i in range(2)]
                S.KTn = sbuf(st, "KTn", [128, 4, 64], BF16)
                S.VAn = sbuf(st, "VAn", [64, 8, 65], BF16)
                L = 32
                if stage == 's1':
                    sch.finish()
                    st.close()
                    sA.close()
                    return nc
                sch.dma('sp', S.biasN[:], biasN_d[:, :, :], writes=['biasN'])
                Xs = M.X[1]
                sch.dma('sp', Xs[0:6, :], cs0[:, :], writes=[('X', 1)])

                def post_cs(j, n, src, bkk):
                    sch.op('act', lambda e: e.activation(out=S.XBCs[:, j:j + n, :, 0:3], in_=src.rearrange("p a (s r) -> p a s r", s=2), func=AF.Copy),
                           reads=[bkk], writes=['XBCs'])
                transpose_to(None, lambda j: Xs[0:6, j * 128:(j + 1) * 128], 6, 8, reads=[('X', 1)], writes=['XBCs'], post=post_cs)
                if stage == 's2':
                    sch.finish()
                    st.close()
                    sA.close()
                    return nc
                for c5 in range(6):
                    w = min(D, 2 * DFF - c5 * D)
                    sch.dma('sp', Xs[0:4, 0:w], fs0[:, c5 * D:c5 * D + w], writes=[('X', 1)])

                    def post_fs(j, n, src, bkk, c5=c5):
                        sch.op('act', lambda e: e.activation(out=S.FTs[:, c5 * 8 + j:c5 * 8 + j + n, :, :], in_=src.rearrange("p a (s r) -> p a s r", s=2),
                                                             func=AF.Copy), reads=[bkk], writes=[('FTs', q_) for q_ in range(42)])
                    transpose_to(None, lambda j: Xs[0:4, j * 128:(j + 1) * 128], 4, w // 128, reads=[('X', 1)], writes=['FTs'], post=post_fs)
                if stage == 's3':
                    sch.finish()
                    st.close()
                    sA.close()
                    return nc
                for s in range(2):
                    sch.dma('sp', CK3, hs0[s].rearrange("(a p) n -> p a n", p=128), writes=['CKf'])
                    if stage == 's3a':
                        sch.finish()
                        st.close()
                        sA.close()
                        return nc

                    def post_h0(j, n, src, bkk, s=s):
                        if stage == 's3c':
                            return
                        if stage == 's4v1':
                            sch.op('dve', lambda e: e.tensor_copy(out=M.T2[:], in_=src.rearrange("p a b -> p (a b)")), reads=[bkk], writes=['T2'])
                            return
                        if stage == 's4v3':
                            sch.op('dve', lambda e: e.tensor_scalar(out=M.T2[:], in0=src.rearrange("p a b -> p (a b)"), scalar1=1.0, scalar2=None, op0=ALU.mult), reads=[bkk], writes=['T2'])
                            return
                        sch.op('act', lambda e: e.activation(out=S.HF[s][:], in_=src.rearrange("p a b -> p (a b)"), func=AF.Copy),
                               reads=[bkk], writes=[('HF', 1 + s)])
                        if stage == 's3b':
                            return
                        if stage == 's4y':
                            sch.op('dve', lambda e: e.memset(M.T2[:], 0.0), reads=[bkk], writes=['T2'])
                            return
                        if stage == 's4z':
                            sch.op('dve', lambda e: e.memset(M.T2[:], 0.0), reads=[], writes=['T2'])
                            return
                        if stage == 's4x':
                            sch.op('dve', lambda e: e.tensor_copy(out=M.T2[:], in_=src.rearrange("p a b -> p (a b)")), reads=[bkk], writes=['T2'])
                            return
                        sch.op('dve', lambda e: e.tensor_copy(out=S.HB[s][:], in_=src.rearrange("p a b -> p (a b)")), reads=[bkk], writes=[('HB', 1 + s)])
                    transpose_to(None, lambda j: CK3[:, j, :], 128, 4, reads=['CKf'], writes=[], post=post_h0)
                if stage in ('s4', 's3b', 's3c', 's4x', 's4y', 's4z', 's4v1', 's4v3'):
                    sch.finish()
                    st.close()
                    sA.close()
                    return nc
                for s in range(2):
                    for m in range(4):
                        CKb, ckk = ((S.CKf, 'CKf'), (S.CKg, 'CKg'))[m % 2]
                        sch.dma('sp', CKb[:], ck[s][m * 128:(m + 1) * 128, :], writes=[ckk])
                        transpose_to(lambda j, n, m=m, s=s: S.KTc[s][:, j:j + n, m * 128:(m + 1) * 128], lambda j, CKb=CKb: CKb[:, j * 128:(j + 1) * 128], 128, 4,
                                     reads=[ckk], writes=[('KTc', s)])
                    for m in range(4):
                        CKb, ckk = ((S.CKf, 'CKf'), (S.CKg, 'CKg'))[m % 2]
                        sch.dma('sp', CKb[:], cv[s][m * 128:(m + 1) * 128, :], writes=[ckk])
                        sch.op('dve', lambda e, s=s, m=m, CKb=CKb: e.tensor_copy(out=S.VAc[s][:, m, :, 0:64], in_=CKb[:].rearrange("p (h d) -> p h d", h=8)),
                               reads=[ckk], writes=[('VAc', s)])
                    sch.op('pool', lambda e, s=s: e.memset(S.VAc[s][:, :, :, 64], 1.0), writes=[('VAc', s)])

                def tiles_s(s, h):
                    hp, po = h // 2, (h % 2) * 64
                    tl = []
                    for m in range(3):
                        tl.append(dict(kt=S.KTc[s][po:po + 64, hp, m * 128:(m + 1) * 128], ktkey=('KTc', s), va=S.VAc[s][:, m, h, :], vakey=('VAc', s),
                                       nk=128, pb=0, bias=None))
                    tl.append(dict(kt=S.KTc[s][po:po + 64, hp, 384:512], ktkey=('KTc', s), va=S.VAc[s][:, 3, h, :], vakey=('VAc', s),
                                   nk=128, pb=0, bias=P.biasT[:, 1, h, 0:32]))
                    tl.append(dict(kt=S.KTn[po:po + 64, hp, s * 32:(s + 1) * 32], ktkey='KTn', va=S.VAn[s * 32:(s + 1) * 32, h, :], vakey='VAn',
                                   nk=32, pb=s * 32, bias=S.biasN[s * 32:(s + 1) * 32, h, :]))
                    return tl

                kvd = (lambda j, n: S.KTn[:, j:j + n, 0:64], 'KTn', S.VAn[0:64, :, :], 'VAn')
                if stage == 'sample_setup':
                    sch.finish()
                    st.close()
                    sA.close()
                    return nc
                mixer_block(M, 'full', 64, 2, 32, xq[:, :], 0, S.XBCs[:], 'XBCs',
                            [(S.HF[0], S.HB[0], 1), (S.HF[1], S.HB[1], 2)], tiles_s,
                            dict(nk=nks_o[:, :], nv=nvs_o[:, :], sconv=scs_o), kvd, 0)
                if stage == 'sample_mixer':
                    sch.finish()
                    st.close()
                    sA.close()
                    return nc
                for s in range(2):
                    final_state(S.HF[s], 1 + s, ssms_o[s], CK3)
                sch.barrier()
            with ExitStack() as st2:
                Fb = ffn_bufs(st2)
                ffn(Fb, 64, 2, 32, [(0, 0, 64)], S.FTs[:], 'FTs', [ys_o[:, :]], fcs_o, False)
                sch.barrier()
            sA.close()

        sch.finish()
    return nc


_CACHE = {}


def _get_nc():
    if 'nc' not in _CACHE:
        _CACHE['nc'] = build_program()
    return _CACHE['nc']


def _host_consts(rel_bias):
    k = np.arange(128)[:, None]
    q = np.arange(128)[None, :]
    rb = rel_bias
    biasT = np.empty((128, 2, 8, 128), np.float32)
    for i in range(2):
        idx = np.clip(128 * i + q - k, -128, 128) + 128
        biasT[:, i] = np.transpose(rb[:, idx], (1, 0, 2))
    kk = np.arange(32)[:, None]
    qq = np.arange(32)[None, :]
    idx = np.clip(qq - kk, -128, 128) + 128
    bn = np.transpose(rb[:, idx], (1, 0, 2))
    biasN = np.concatenate([bn, bn], axis=0).astype(np.float32)
    cbias = rb[:, 256].reshape(1, 8).astype(np.float32)
    consts = np.zeros((128, 8, 128), np.float32)
    consts[:, 0] = np.eye(128)
    consts[:, 1] = (k <= q)
    consts[:, 2] = (k > q)
    consts[:, 3] = 1.0
    seg = np.arange(128) // 32
    same = seg[:, None] == seg[None, :]
    consts[:, 4] = (k <= q) & same
    consts[:, 5] = (k > q) & same
    consts[:, 6] = (seg == 0)[:, None]
    consts[:, 7] = (seg == 1)[:, None]
    return biasT, biasN, cbias, consts


def kernel(x_prompt, x_sample, cache_attn_k, cache_attn_v, state_ssm, state_ssm_conv, state_ffn_conv,
           w_in, rel_bias, attn_norm_g, ssm_conv_w, ssm_conv_b, ssm_dt_bias, ssm_A_log, ssm_D,
           ssm_norm_g, w_out, ln1_g, ln1_b, w_up, ffn_conv_w, ffn_conv_b, w_down, ln2_g, ln2_b):
    f32 = np.float32
    A = lambda a: np.ascontiguousarray(np.asarray(a, dtype=f32))
    x_prompt, x_sample = A(x_prompt), A(x_sample)
    ck_, cv_ = A(cache_attn_k)[0], A(cache_attn_v)[0]
    hs_, cs_, fs_ = A(state_ssm)[0], A(state_ssm_conv)[0], A(state_ffn_conv)[0]
    biasT, biasN, cbias, consts = _host_consts(A(rel_bias)[0])
    colp = np.zeros((128, CP_N), f32)
    colp[:, CP_AG:CP_AG + 4] = A(attn_norm_g)[0].reshape(4, 128).T
    colp[:, CP_SG:CP_SG + 4] = A(ssm_norm_g)[0].reshape(4, 128).T
    cw = A(ssm_conv_w)[0]
    colp[:, CP_CW:CP_CW + 32] = cw.reshape(4, 8, 128).transpose(2, 1, 0).reshape(128, 32)
    colp[:, CP_CB:CP_CB + 8] = A(ssm_conv_b)[0].reshape(8, 128).T
    fw = A(ffn_conv_w)[0]
    colp[:, CP_FW:CP_FW + 126] = fw.reshape(3, 42, 128).transpose(2, 1, 0).reshape(128, 126)
    colp[:, CP_FB:CP_FB + 42] = A(ffn_conv_b)[0].reshape(42, 128).T
    rowp = np.concatenate([A(ssm_dt_bias)[0], A(ssm_A_log)[0], A(ssm_D)[0]]).reshape(1, 24)
    lnp = np.concatenate([A(ln1_g)[0], A(ln1_b)[0], A(ln2_g)[0], A(ln2_b)[0]]).reshape(1, 4 * D)
    wu = A(w_up)[0]
    wu_perm = np.ascontiguousarray(wu.reshape(D, 2, NF, 128).transpose(0, 2, 1, 3).reshape(D, 2 * DFF))
    shared = dict(w_in=A(w_in)[0], w_out=A(w_out)[0], w_up=wu_perm, w_down=A(w_down)[0],
                  biasT=biasT, biasN=biasN, cbias=cbias, colp=colp, rowp=rowp, lnp=lnp, consts=consts)
    in_maps = []
    for c in range(8):
        b, hf = c // 2, c % 2
        if hf == 0:
            xs = np.concatenate([np.zeros((4096, D), f32), x_prompt[b, :4096]], axis=0)
        else:
            xs = x_prompt[b]
        m = dict(shared)
        m.update(xs=np.ascontiguousarray(xs), flag=np.full((128, 1), float(hf), f32),
                 xq=np.ascontiguousarray(x_sample[2 * c:2 * c + 2].reshape(64, D)),
                 ck=np.ascontiguousarray(ck_[2 * c:2 * c + 2].reshape(2, 512, 512)),
                 cv=np.ascontiguousarray(cv_[2 * c:2 * c + 2].reshape(2, 512, 512)),
                 hs0=np.ascontiguousarray(hs_[2 * c:2 * c + 2].reshape(2, 512, 128)),
                 cs0=np.ascontiguousarray(cs_[2 * c:2 * c + 2].reshape(6, D)),
                 fs0=np.ascontiguousarray(fs_[2 * c:2 * c + 2].reshape(4, 2 * DFF)))
        in_maps.append(m)
    nc = _get_nc()
    res = run_bass_kernel_spmd(nc, in_maps, core_ids=list(range(8)))
    R = res.results
    y_prompt = np.stack([np.concatenate([R[2 * b]["y_o"], R[2 * b + 1]["y_o"]], axis=0) for b in range(4)]).astype(f32)
    y_sample = np.concatenate([R[c]["ys_o"].reshape(2, 32, D) for c in range(8)], axis=0).astype(f32)
    nkp = np.stack([R[2 * b + 1]["nk_o"].reshape(512, 8, 64) for b in range(4)])[None].astype(f32)
    nvp = np.stack([R[2 * b + 1]["nv_o"].reshape(512, 8, 64) for b in range(4)])[None].astype(f32)
    nks = np.concatenate([R[c]["nks_o"].reshape(2, 32, 8, 64) for c in range(8)], axis=0)[None].astype(f32)
    nvs = np.concatenate([R[c]["nvs_o"].reshape(2, 32, 8, 64) for c in range(8)], axis=0)[None].astype(f32)
    ssmp = np.stack([R[2 * b + 1]["ssm_o"].reshape(8, 64, 128) for b in range(4)])[None].astype(f32)
    ssms = np.concatenate([R[c]["ssms_o"].reshape(2, 8, 64, 128) for c in range(8)], axis=0)[None].astype(f32)
    scp = np.stack([R[2 * b + 1]["sc_o"] for b in range(4)])[None].astype(f32)
    scs = np.concatenate([R[c]["scs_o"].reshape(2, 3, D) for c in range(8)], axis=0)[None].astype(f32)
    fcp = np.stack([R[2 * b + 1]["fc_o"] for b in range(4)])[None].astype(f32)
    fcs = np.concatenate([R[c]["fcs_o"].reshape(2, 2, 2 * DFF) for c in range(8)], axis=0)[None].astype(f32)
    return (y_prompt, y_sample, nkp, nvp, nks, nvs, ssmp, ssms, scp, scs, fcp, fcs)
```

```python
import numpy as np
from contextlib import ExitStack
import concourse.bass as bass
import concourse.mybir as mybir
from concourse.bass_utils import run_bass_kernel_spmd

F32 = mybir.dt.float32
BF16 = mybir.dt.bfloat16
AF = mybir.ActivationFunctionType
ALU = mybir.AluOpType

D = 1024
KT = 8
NPROJ = 3080
DFF = 2688
NF = 21
C_Q, C_K, C_V, C_Z, C_XBC, C_DT = 0, 512, 1024, 1536, 2048, 3072
ALPHA = 2.0 ** 0.25
EPS = 1e-5
NBLK = 64
B_KV = 27
B_FULL = 31
B_MAIN = 32
CP_AG, CP_SG, CP_CW, CP_CB, CP_FW, CP_FB, CP_N = 0, 4, 8, 40, 48, 174, 216


class Sched:
    def __init__(self, nc, es, ndma=64):
        self.nc = nc
        self.engs = {'pe': nc.tensor, 'act': nc.scalar, 'dve': nc.vector, 'pool': nc.gpsimd, 'sp': nc.sync}
        self.sem = {k: es.enter_context(nc.semaphore('s_' + k)) for k in self.engs}
        self.cnt = {k: 0 for k in self.engs}
        self.seen = {k: {} for k in self.engs}
        self.dsem = [es.enter_context(nc.semaphore('d%d' % i)) for i in range(ndma)]
        self.dval = [0] * ndma
        self.dnext = 0
        self.npool = 16
        self.pool_used = 0
        self.lastw = {}
        self.readers = {}
        self.nwait = 0
        self.pre_barrier = None

    def _wait(self, e, tok):
        kind, src, val = tok
        if kind == 'e' and src == e and e in ('pe', 'sp'):
            return
        key = (kind, src)
        if self.seen[e].get(key, 0) >= val:
            return
        sem = self.sem[src] if kind == 'e' else self.dsem[src]
        self.engs[e].wait_ge(sem, val)
        self.seen[e][key] = val
        self.nwait += 1

    def _deps(self, e, reads, writes):
        for k in reads:
            t = self.lastw.get(k)
            if t is not None:
                self._wait(e, t)
            if isinstance(k, tuple) and k[0] == 'bk':
                for (kind, src), val in self.readers.get(k, {}).items():
                    if src != e:
                        self._wait(e, (kind, src, val))
        for k in writes:
            t = self.lastw.get(k)
            if t is not None:
                self._wait(e, t)
            for (kind, src), val in self.readers.get(k, {}).items():
                self._wait(e, (kind, src, val))

    def _commit(self, tok, reads, writes):
        kind, src, val = tok
        for k in reads:
            d = self.readers.setdefault(k, {})
            if d.get((kind, src), 0) < val:
                d[(kind, src)] = val
        for k in writes:
            self.lastw[k] = tok
            self.readers[k] = {}

    def op(self, e, fn, reads=(), writes=(), signal=True):
        self._deps(e, reads, writes)
        ins = fn(self.engs[e])
        if signal:
            self.cnt[e] += 1
            ins.then_inc(self.sem[e], 1)
            tok = ('e', e, self.cnt[e])
        else:
            tok = ('e', e, self.cnt[e] + 1)
        self._commit(tok, reads, writes)

    def dma(self, q, out, in_, reads=(), writes=(), **kw):
        if q == 'pool':
            i = len(self.dsem) - 1 - self.pool_used
            self.pool_used += 1
            assert self.pool_used <= self.npool and self.dval[i] == 0
        else:
            i = self.dnext
            self.dnext = (self.dnext + 1) % (len(self.dsem) - self.npool)
        if self.dval[i] > 0:
            self._wait(q, ('d', i, self.dval[i]))
        self._deps(q, reads, writes)
        self.dval[i] += 16
        self.engs[q].dma_start(out=out, in_=in_, **kw).then_inc(self.dsem[i], 16)
        self._commit(('d', i, self.dval[i]), reads, writes)

    def barrier(self, skip_pool_dma=False):
        if self.pre_barrier is not None:
            self.pre_barrier()
        nhw = len(self.dsem) - self.npool
        for i, v in enumerate(self.dval):
            if v > 0 and not (skip_pool_dma and i >= nhw):
                self._wait('sp', ('d', i, v))
        self.cnt['sp'] += 1
        self.engs['sp'].sem_inc(self.sem['sp'], 1)
        ce = ('pe', 'act', 'dve', 'pool')
        for e in ce:
            for f in ce + ('sp',):
                if self.cnt[f] > 0:
                    self._wait(e, ('e', f, self.cnt[f]))
        for f in ce:
            if self.cnt[f] > 0:
                self._wait('sp', ('e', f, self.cnt[f]))
        keep = {}
        if skip_pool_dma:
            keep = {k: t for k, t in self.lastw.items() if t[0] == 'd' and t[1] >= nhw}
        self.lastw.clear()
        self.lastw.update(keep)
        self.readers.clear()

    def finish(self):
        for i, v in enumerate(self.dval):
            if v > 0:
                self._wait('sp', ('d', i, v))


class NS:
    pass


def build_program(do_sample=True, nblk=NBLK, do_prompt=True, dbg=False, stage=None):
    nc = bass.Bass("TRN2", target_bir_lowering=False)

    def din(name, shape, dt=F32):
        return nc.dram_tensor(name, list(shape), dt, kind="ExternalInput").ap()

    def dout(name, shape, dt=F32):
        return nc.dram_tensor(name, list(shape), dt, kind="ExternalOutput").ap()

    def dint(name, shape, dt):
        return nc.dram_tensor(name, list(shape), dt, kind="Internal").ap()

    xs = din("xs", [NBLK * 128, D])
    flag = din("flag", [128, 1])
    xq = din("xq", [64, D])
    ck = din("ck", [2, 512, 512])
    cv = din("cv", [2, 512, 512])
    hs0 = din("hs0", [2, 512, 128])
    cs0 = din("cs0", [6, D])
    fs0 = din("fs0", [4, 2 * DFF])
    w_in = din("w_in", [D, NPROJ])
    w_out = din("w_out", [D, D])
    w_up = din("w_up", [D, 2 * DFF])
    w_down = din("w_down", [DFF, D])
    biasT_d = din("biasT", [128, 2, 8, 128])
    biasN_d = din("biasN", [64, 8, 32])
    cbias_d = din("cbias", [1, 8])
    colp_d = din("colp", [128, CP_N])
    rowp_d = din("rowp", [1, 24])
    lnp_d = din("lnp", [1, 4 * D])
    consts_d = din("consts", [128, 8, 128])

    y_o = dout("y_o", [4096, D])
    ys_o = dout("ys_o", [64, D])
    nk_o = dout("nk_o", [512, 512])
    nv_o = dout("nv_o", [512, 512])
    nks_o = dout("nks_o", [64, 512])
    nvs_o = dout("nvs_o", [64, 512])
    ssm_o = dout("ssm_o", [512, 128])
    ssms_o = dout("ssms_o", [2, 512, 128])
    sc_o = dout("sc_o", [3, D])
    scs_o = dout("scs_o", [6, D])
    fc_o = dout("fc_o", [2, 2 * DFF])
    fcs_o = dout("fcs_o", [4, 2 * DFF])

    dbg_o = dout("dbg_o", [128, 16, D]) if dbg else None

    def dump(slot, ap, npart, w, key):
        if dbg:
            sch.dma('sp', dbg_o[0:npart, slot, 0:w], ap, reads=[key])

    wb_in = dint("wb_in", [D, NPROJ], BF16)
    wb_out = dint("wb_out", [D, D], BF16)
    wb_up = dint("wb_up", [D, 2 * DFF], BF16)
    wb_down = dint("wb_down", [DFF, D], BF16)

    es = ExitStack()
    with es:
        sch = Sched(nc, es)

        uid = {'n': 0}

        def sbuf(stack, name, shape, dt=F32):
            uid['n'] += 1
            return stack.enter_context(nc.sbuf_tensor("s%d_%s" % (uid['n'], name), list(shape), dt))

        banks = [es.enter_context(nc.psum_tensor("bk%d" % i, [128, 512], F32)) for i in range(8)]
        bstate = {'n': 0}

        def bank():
            nb_ = bstate.get('nb', 6)
            i = bstate['n'] % nb_
            bstate['n'] = (i + 1) % nb_
            return banks[i], ('bk', i)

        P = NS()
        P.Wi = sbuf(es, "Wi", [128, KT, NPROJ], BF16)
        P.Wo = sbuf(es, "Wo", [128, KT, D], BF16)
        P.consts = sbuf(es, "consts", [128, 8, 128])
        P.biasT = sbuf(es, "biasT", [128, 2, 8, 128])
        P.cbias = sbuf(es, "cbias", [128, 8])
        P.colp = sbuf(es, "colp", [128, CP_N])
        P.rowp = sbuf(es, "rowp", [128, 24])
        P.Aneg = sbuf(es, "Aneg", [128, 8])
        P.lnp = sbuf(es, "lnp", [128, 4, D])
        P.flg = sbuf(es, "flg", [128, 1])
        P.mhalf = sbuf(es, "mhalf", [128, 1])
        NKR = 5
        P.X1S = sbuf(es, "X1S", [128, 4, D])
        P.X1T = sbuf(es, "X1T", [128, KT, 512], BF16)
        P.XBC = sbuf(es, "XBC", [128, 8, 131])
        P.SMALL = sbuf(es, "SMALL", [128, 16])
        P.STAT = sbuf(es, "STAT", [128, 2, 6])
        P.MV = sbuf(es, "MV", [128, 2])
        P.TT = sbuf(es, "TT", [128, 128])
        P.TAIL = sbuf(es, "TAIL", [128, 96])
        ident = P.consts[:, 0, :]
        cT, cU, cOnes, cT2, cU2 = (P.consts[:, i, :] for i in (1, 2, 3, 4, 5))
        Wi, Wo, colp, rowp, lnp, flg, mhalf, SMALL, STAT, MV = P.Wi, P.Wo, P.colp, P.rowp, P.lnp, P.flg, P.mhalf, P.SMALL, P.STAT, P.MV

        def bc(ap, shape, axis):
            return ap.unsqueeze(axis).to_broadcast(list(shape))

        def mixer_bufs(st):
            M = NS()
            M.X = [sbuf(st, "X%d" % i, [128, D]) for i in range(2)]
            M.XT = [sbuf(st, "XT%d" % i, [128, KT, 128], BF16) for i in range(2)]
            M.ACC = sbuf(st, "ACC", [128, 8, 128])
            M.SIL = sbuf(st, "SIL", [128, 8, 128])
            M.RHSA = sbuf(st, "RHSA", [128, 8, 128])
            M.BCT = sbuf(st, "BCT", [128, 4, 128], BF16)
            M.BK = sbuf(st, "BK", [128, 2, 128], BF16)
            M.QT = sbuf(st, "QT", [128, 4, 128], BF16)
            M.SZ = sbuf(st, "SZ", [128, 512])
            M.PT = [sbuf(st, "PT%d" % i, [128, 5, 128], BF16) for i in range(2)]
            M.SB = [sbuf(st, "SB%d" % i, [128, 2, 128]) for i in range(2)]
            M.DTA = sbuf(st, "DTA", [128, 4, 8])
            M.LM = sbuf(st, "LM", [128, 8, 128])
            M.EE = sbuf(st, "EE", [128, 8])
            M.CD = sbuf(st, "CD", [128, 2, 8])
            M.WDS = sbuf(st, "WDS", [128, 8])
            M.CBM = sbuf(st, "CBM", [128, 2, 128])
            M.MT = sbuf(st, "MT", [128, 8, 128], BF16)
            M.XSK = sbuf(st, "XSK", [128, 512])
            M.XDT = sbuf(st, "XDT", [128, 8, 64], BF16)
            M.XDD = sbuf(st, "XDD", [128, 8, 64], BF16)
            M.T1 = sbuf(st, "T1", [128, 512])
            M.T2 = sbuf(st, "T2", [128, 512])
            M.YY = sbuf(st, "YY", [128, 512])
            M.JB = sbuf(st, "JB", [128, 512], BF16)
            M.AST = sbuf(st, "AST", [128, KT, 128], BF16)
            for i in range(2):
                sch.op('pool', lambda e, i=i: e.memset(M.PT[i][:], 0.0), writes=[('PT', i)])
            return M

        def ffn_bufs(st):
            Fb = NS()
            Fb.Wup = [sbuf(st, "Wup%d" % i, [128, KT, 256], BF16) for i in range(3)]
            Fb.Wdn = [sbuf(st, "Wdn%d" % i, [128, NF, 256], BF16) for i in range(2)]
            Fb.GT = sbuf(st, "GT", [128, NF, 512], BF16)
            Fb.HEAD = [sbuf(st, "HEAD%d" % i, [128, 8]) for i in range(4)]
            Fb.FACC = [sbuf(st, "FACC%d" % i, [128, 512]) for i in range(4)]
            return Fb

        def cast_rows(dst, src, nrows, split, key, nparts, extra_reads=()):
            nb = nrows // 128
            per = (nb + nparts - 1) // nparts
            r = 0
            while r < nb:
                n = min(per, nb - r)
                s_ = src[r * 128:(r + n) * 128, :].rearrange("r (a b) -> r a b", a=split)
                d_ = dst[r * 128:(r + n) * 128, :].rearrange("r (a b) -> r a b", a=split)
                sch.dma('pool', d_, s_, reads=list(extra_reads), writes=[(key, i) for i in range(r, r + n)])
                r += n

        sch.dma('sp', P.consts[:], consts_d[:, :, :], writes=['consts'])
        sch.dma('sp', colp[:], colp_d[:, :], writes=['colp'])
        sch.dma('sp', flg[:], flag[:, :], writes=['flg'])
        sch.dma('sp', rowp[:], bass.AP(tensor=rowp_d.tensor, offset=0, ap=[[0, 128], [1, 24]]), writes=['rowp'])
        sch.dma('sp', P.cbias[:], bass.AP(tensor=cbias_d.tensor, offset=0, ap=[[0, 128], [1, 8]]), writes=['cbias'])
        with ExitStack() as st0:
            stg = [sbuf(st0, "WSTG%d" % i, [128, NPROJ]) for i in range(3)]
            for kt in range(KT):
                si = kt % 3
                sch.dma('sp', stg[si][:], w_in[kt * 128:(kt + 1) * 128, :], writes=[('WSTG', si)])
                if kt % 2 == 0:
                    sch.op('act', lambda e, kt=kt, si=si: e.activation(out=Wi[:, kt, :], in_=stg[si][:], func=AF.Copy), reads=[('WSTG', si)], writes=[('Wi', kt)])
                else:
                    sch.op('dve', lambda e, kt=kt, si=si: e.tensor_copy(out=Wi[:, kt, :], in_=stg[si][:]), reads=[('WSTG', si)], writes=[('Wi', kt)])
            sch.dma('sp', lnp[:].rearrange("p a b -> p (a b)"),
                    bass.AP(tensor=lnp_d.tensor, offset=0, ap=[[0, 128], [1, 4 * D]]), writes=['lnp'])
            sch.dma('sp', P.biasT[:], biasT_d[:, :, :, :], writes=['biasT'])
            for k4 in range(0, KT, 4):
                sch.dma('pool', Wo[:, k4:k4 + 4, :], w_out[k4 * 128:(k4 + 4) * 128, :].rearrange("(k p) n -> p k n", p=128),
                        reads=[('WSTG', (KT - 1) % 3)], writes=[('Wo', k4 + i_) for i_ in range(4)])
            cast_rows(wb_up, w_up, D, 6, 'wb_up', 4, extra_reads=[('WSTG', (KT - 1) % 3)])
            cast_rows(wb_down, w_down, DFF, 1, 'wb_down', 3, extra_reads=[('WSTG', (KT - 1) % 3)])
            sch.barrier(skip_pool_dma=True)
        WUPK = [('wb_up', r) for r in range(8)]
        WDNK = [('wb_down', r) for r in range(NF)]

        sch.op('act', lambda e: e.activation(out=P.Aneg[:], in_=rowp[:, 8:16], func=AF.Exp), reads=['rowp'], writes=['Aneg'])
        sch.op('dve', lambda e: e.tensor_scalar(out=P.Aneg[:], in0=P.Aneg[:], scalar1=-1.0, scalar2=None, op0=ALU.mult),
               reads=['Aneg'], writes=['Aneg'])
        sch.op('dve', lambda e: e.memset(mhalf[:], -0.5), writes=['mhalf'])
        sch.op('dve', lambda e: e.memset(P.XBC[:], 0.0), writes=['XBC'])

        if stage == 'prologue':
            sch.finish()
            return nc

        def transpose_to(dst_fn, src_fn, nparts, ntiles, reads, writes, post=None, engs=('act', 'dve')):
            j = 0
            gi = 0
            while j < ntiles:
                n = min(4, ntiles - j)
                bk, bkk = bank()
                for q in range(n):
                    sch.op('pe', lambda e, q=q, j=j, bk=bk: e.transpose(out=bk[:, q * nparts:(q + 1) * nparts],
                                                                        in_=src_fn(j + q), identity=ident[0:nparts, 0:nparts]),
                           reads=list(reads) + ['consts'], writes=[bkk], signal=(q == n - 1))
                src = bk[:, 0:n * nparts].rearrange("p (a b) -> p a b", a=n)
                if post is not None:
                    post(j, n, src, bkk)
                elif engs[gi % 2] == 'act':
                    sch.op('act', lambda e, j=j, n=n, src=src: e.activation(out=dst_fn(j, n), in_=src, func=AF.Copy),
                           reads=[bkk], writes=writes)
                else:
                    sch.op('dve', lambda e, j=j, n=n, src=src: e.tensor_copy(out=dst_fn(j, n), in_=src),
                           reads=[bkk], writes=writes)
                j += n
                gi += 1

        def rstd_from_sumsq(ss_ap, n_feat, npart, key):
            sch.op('pool', lambda e: e.tensor_scalar(out=ss_ap, in0=ss_ap, scalar1=1.0 / n_feat, scalar2=EPS,
                                                     op0=ALU.mult, op1=ALU.add), reads=[key], writes=[key])
            sch.op('pool', lambda e: e.tensor_tensor(out=ss_ap, in0=ss_ap, in1=mhalf[0:npart, :], op=ALU.pow),
                   reads=[key, 'mhalf'], writes=[key])

        def layer_norm(src_ap, dst_ap, NT, gi, src_key, dst_key):
            for c in range(2):
                sch.op('dve', lambda e, c=c: e.bn_stats(out=STAT[0:NT, c, :], in_=src_ap[:, c * 512:(c + 1) * 512]),
                       reads=[src_key], writes=['STAT'])
            sch.op('dve', lambda e: e.bn_aggr(out=MV[0:NT, :], in_=STAT[0:NT, :, :].rearrange("p a b -> p (a b)")),
                   reads=['STAT'], writes=['MV'])
            sch.op('pool', lambda e: e.tensor_scalar(out=MV[0:NT, 1:2], in0=MV[0:NT, 1:2], scalar1=1.0, scalar2=EPS,
                                                     op0=ALU.mult, op1=ALU.add), reads=['MV'], writes=['MV'])
            sch.op('pool', lambda e: e.tensor_tensor(out=MV[0:NT, 1:2], in0=MV[0:NT, 1:2], in1=mhalf[0:NT, :], op=ALU.pow),
                   reads=['MV', 'mhalf'], writes=['MV'])
            sch.op('dve', lambda e: e.tensor_scalar(out=dst_ap, in0=src_ap, scalar1=MV[0:NT, 0:1], scalar2=MV[0:NT, 1:2],
                                                    op0=ALU.subtract, op1=ALU.mult), reads=[src_key, 'MV'], writes=[dst_key])
            for hf_, eng in ((0, 'dve'), (1, 'pool')):
                cs_ = slice(hf_ * 512, (hf_ + 1) * 512)
                sch.op(eng, lambda e, cs_=cs_: e.tensor_tensor(out=dst_ap[:, cs_], in0=dst_ap[:, cs_], in1=lnp[0:NT, gi, cs_], op=ALU.mult),
                       reads=[dst_key, 'lnp'], writes=[(dst_key, hf_)])
                sch.op(eng, lambda e, cs_=cs_: e.tensor_tensor(out=dst_ap[:, cs_], in0=dst_ap[:, cs_], in1=lnp[0:NT, gi + 1, cs_], op=ALU.add),
                       reads=[(dst_key, hf_), 'lnp'], writes=[(dst_key, hf_)])

        deferred = []

        def flush_deferred():
            while deferred:
                deferred.pop(0)()

        sch.pre_barrier = flush_deferred

        def mixer_block(M, kind, NT, nseg, L, x_src, xi, xbc_t, xbc_key, HFs, attn_tiles, outs, kv_dst, j_in_super):
            bstate['nb'] = 6
            full = kind == 'full'
            Xb = M.X[xi]
            XTb = M.XT[xi]
            xk, xtk = ('X', xi), ('XT', xi)
            Tm = cT if nseg == 1 else cT2
            Um = cU if nseg == 1 else cU2
            DTA, LM, SIL, ACC, BCT = M.DTA, M.LM, M.SIL, M.ACC, M.BCT
            sch.dma('sp', Xb[0:NT, :], x_src, writes=[xk])
            transpose_to(lambda j, n: XTb[:, j:j + n, 0:NT], lambda j: Xb[0:NT, j * 128:(j + 1) * 128], NT, KT,
                         reads=[xk], writes=[xtk], engs=('act', 'act'))

            def proj_fm(c0, ntiles, evac):
                j = 0
                while j < ntiles:
                    n = min(4, ntiles - j)
                    bk, bkk = bank()
                    for q in range(n):
                        cc = c0 + (j + q) * 128
                        for kt in range(KT):
                            sch.op('pe', lambda e, q=q, kt=kt, cc=cc, bk=bk: e.matmul(bk[:, q * NT:(q + 1) * NT], lhsT=Wi[:, kt, cc:cc + 128],
                                                                                       rhs=XTb[:, kt, 0:NT], start=(kt == 0), stop=(kt == KT - 1)),
                                   reads=[xtk, ('Wi', kt)], writes=[bkk], signal=(q == n - 1 and kt == KT - 1))
                    evac(bk, bkk, j, n)
                    j += n

            def proj_tm(c0, ncols, bk, bkk):
                for kt in range(KT):
                    sch.op('pe', lambda e, kt=kt: e.matmul(bk[0:NT, 0:ncols], lhsT=XTb[:, kt, 0:NT], rhs=Wi[:, kt, c0:c0 + ncols],
                                                           start=(kt == 0), stop=(kt == KT - 1)),
                           reads=[xtk, ('Wi', kt)], writes=[bkk], signal=(kt == KT - 1))

            bkd, bkdk = bank()
            proj_tm(C_DT, 8, bkd, bkdk)
            u_, au, dt_, a_ = (DTA[0:NT, i, :] for i in range(4))
            sch.op('dve', lambda e: e.tensor_tensor(out=u_, in0=bkd[0:NT, 0:8], in1=rowp[0:NT, 0:8], op=ALU.add),
                   reads=[bkdk, 'rowp'], writes=['DTA'])
            sch.op('act', lambda e: e.activation(out=au, in_=u_, func=AF.Abs), reads=['DTA'], writes=['DTA'])
            sch.op('act', lambda e: e.activation(out=au, in_=au, func=AF.Exp, scale=-1.0), reads=['DTA'], writes=['DTA'])
            sch.op('act', lambda e: e.activation(out=au, in_=au, func=AF.Ln, bias=1.0, scale=1.0), reads=['DTA'], writes=['DTA'])
            sch.op('dve', lambda e: e.scalar_tensor_tensor(out=dt_, in0=u_, scalar=0.0, in1=au, op0=ALU.max, op1=ALU.add),
                   reads=['DTA'], writes=['DTA'])
            sch.op('dve', lambda e: e.tensor_tensor(out=a_, in0=dt_, in1=P.Aneg[0:NT, :], op=ALU.mult), reads=['DTA', 'Aneg'], writes=['DTA'])

            RHSA = M.RHSA
            for h in range(8):
                sch.op('pool', lambda e, h=h: e.tensor_scalar(out=RHSA[0:NT, h, 0:NT], in0=Tm[0:NT, 0:NT], scalar1=DTA[0:NT, 3, h:h + 1],
                                                              scalar2=0.0, op0=ALU.mult, op1=ALU.add), reads=['DTA', 'consts'], writes=[('RHSA', h)])
            def evac_xbc(bk, bkk, j, n):
                src = bk[:, 0:n * NT].rearrange("p (a s l) -> p a s l", a=n, s=nseg)
                sch.op('act', lambda e: e.activation(out=xbc_t[:, j:j + n, :, 3:3 + L], in_=src, func=AF.Copy),
                       reads=[bkk], writes=[xbc_key])
            proj_fm(C_XBC, 8, evac_xbc)

            if outs.get('sconv') is not None:
                nr = 3 * nseg
                tl = P.TAIL[:, 0:8 * nr].rearrange("p (t s r) -> p t s r", t=8, s=nseg)
                sch.op('pool', lambda e: e.tensor_copy(out=tl, in_=xbc_t[:, :, :, L:L + 3]), reads=[xbc_key], writes=['TAIL'])
                bkt, bktk = bank()
                sch.op('pe', lambda e: e.transpose(out=bkt[0:8 * nr, 0:128], in_=P.TAIL[:, 0:8 * nr], identity=ident),
                       reads=['TAIL', 'consts'], writes=[bktk])
                sch.op('act', lambda e: e.activation(out=P.TT[0:8 * nr, :], in_=bkt[0:8 * nr, 0:128], func=AF.Copy),
                       reads=[bktk], writes=['TT'])
                for t in range(8):
                    sch.dma('sp', outs['sconv'][:, t * 128:(t + 1) * 128], P.TT[t * nr:(t + 1) * nr, :], reads=['TT'])

            for jj in (3, 0, 1, 2):
                for t in range(8):
                    cw = CP_CW + 4 * t
                    acc = ACC[:, t, 0:NT].rearrange("p (s l) -> p s l", s=nseg)
                    if jj == 3:
                        sch.op('dve', lambda e, t=t, cw=cw, acc=acc: e.tensor_scalar(out=acc, in0=xbc_t[:, t, :, 3:3 + L], scalar1=colp[:, cw + 3:cw + 4],
                                                                                     scalar2=colp[:, CP_CB + t:CP_CB + t + 1], op0=ALU.mult, op1=ALU.add),
                               reads=[xbc_key, 'colp'], writes=[('ACC', t)])
                    else:
                        sch.op('dve', lambda e, t=t, cw=cw, acc=acc, jj=jj: e.scalar_tensor_tensor(out=acc, in0=xbc_t[:, t, :, jj:jj + L],
                                                                                                  scalar=colp[:, cw + jj:cw + jj + 1], in1=acc,
                                                                                                  op0=ALU.mult, op1=ALU.add),
                               reads=[xbc_key, 'colp', ('ACC', t)], writes=[('ACC', t)])

            if kind != 'ssm':
                kt_dst, kt_key, va_dst, va_key = kv_dst

                def evac_k(bk, bkk, j, n):
                    src = bk[:, 0:n * NT].rearrange("p (a b) -> p a b", a=n)
                    sch.op('act', lambda e: e.activation(out=kt_dst(j, n), in_=src, func=AF.Copy), reads=[bkk], writes=[kt_key])
                proj_fm(C_K, 4, evac_k)
                bkv, bkvk = bank()
                proj_tm(C_V, 512, bkv, bkvk)
                sch.op('act', lambda e: e.activation(out=va_dst[:, :, 0:64], in_=bkv[0:NT, :].rearrange("p (h d) -> p h d", h=8),
                                                     func=AF.Copy), reads=[bkvk], writes=[va_key])
                if kind == 'kv':
                    sch.op('pool', lambda e: e.tensor_copy(out=va_dst[:, :, 64], in_=flg[0:NT, 0:1].to_broadcast([NT, 8])),
                           reads=['flg'], writes=[va_key])
                else:
                    sch.op('pool', lambda e: e.memset(va_dst[:, :, 64], 1.0), writes=[va_key])
                if outs.get('nv') is not None:
                    sch.op('act', lambda e: e.activation(out=M.T1[0:NT, :], in_=bkv[0:NT, :], func=AF.Copy), reads=[bkvk], writes=['T1'])
                    sch.dma('sp', outs['nv'], M.T1[0:NT, :], reads=['T1'])
                if outs.get('nk') is not None:
                    bkk_, bkkk = bank()
                    proj_tm(C_K, 512, bkk_, bkkk)
                    sch.op('act', lambda e: e.activation(out=M.T2[0:NT, :], in_=bkk_[0:NT, :], func=AF.Copy), reads=[bkkk], writes=['T2'])
                    sch.dma('sp', outs['nk'], M.T2[0:NT, :], reads=['T2'])

            if full:
                def evac_q(bk, bkk, j, n):
                    src = bk[:, 0:n * NT].rearrange("p (a b) -> p a b", a=n)
                    sch.op('act', lambda e: e.activation(out=M.QT[:, j:j + n, 0:NT], in_=src, func=AF.Copy), reads=[bkk], writes=['QT'])
                proj_fm(C_Q, 4, evac_q)
                bkz, bkzk = bank()
                proj_tm(C_Z, 512, bkz, bkzk)
                sch.op('act', lambda e: e.activation(out=M.SZ[0:NT, :], in_=bkz[0:NT, :], func=AF.Silu), reads=[bkzk], writes=['SZ'])

            step_attn = lambda n: None
            if full:
                ob = [(banks[6], ('bk', 6)), (banks[7], ('bk', 7))]
                units = [(h, s_) for h in range(8) for s_ in range(nseg)]

                def attn_front(ui):
                    h, s = units[ui]
                    hp, po = h // 2, (h % 2) * 64
                    pti = ui % 2
                    PTb, ptk, SBb, sbk = M.PT[pti], ('PT', pti), M.SB[pti], ('SB', pti)
                    tiles = attn_tiles(s, h)
                    ctiles = [t_ for t_ in tiles if t_['bias'] is None]
                    btiles = [t_ for t_ in tiles if t_['bias'] is not None]
                    nc_ = len(ctiles)
                    bkA, bkAk = bank()
                    bkB, bkBk = bank()
                    for lst, bk, bkk in ((ctiles, bkA, bkAk), (btiles, bkB, bkBk)):
                        for ii, t_ in enumerate(lst):
                            nk, pb = t_['nk'], t_['pb']
                            sch.op('pe', lambda e, t_=t_, ii=ii, bk=bk, nk=nk, pb=pb, s=s: e.matmul(
                                bk[pb:pb + nk, ii * L:(ii + 1) * L], lhsT=t_['kt'], rhs=M.QT[po:po + 64, hp, s * L:(s + 1) * L], start=True, stop=True),
                                reads=[t_['ktkey'], 'QT'], writes=[bkk], signal=(ii == len(lst) - 1))
                    sch.op('act', lambda e, bkA=bkA, PTb=PTb, h=h: e.activation(
                        out=PTb[:, 0:nc_, 0:L], in_=bkA[:, 0:nc_ * L].rearrange("p (a l) -> p a l", a=nc_), func=AF.Exp,
                        bias=P.cbias[:, h:h + 1], scale=0.125), reads=[bkAk, 'cbias'], writes=[ptk])
                    for ii, t_ in enumerate(btiles):
                        nk, pb = t_['nk'], t_['pb']
                        sch.op('dve', lambda e, t_=t_, ii=ii, nk=nk, pb=pb, bkB=bkB, SBb=SBb: e.scalar_tensor_tensor(
                            out=SBb[pb:pb + nk, ii, 0:L], in0=bkB[pb:pb + nk, ii * L:(ii + 1) * L], scalar=0.125, in1=t_['bias'],
                            op0=ALU.mult, op1=ALU.add), reads=[bkBk, 'biasT', 'biasN'], writes=[sbk])
                        sch.op('act', lambda e, ii=ii, nk=nk, pb=pb, SBb=SBb, PTb=PTb: e.activation(
                            out=PTb[pb:pb + nk, nc_ + ii, 0:L], in_=SBb[pb:pb + nk, ii, 0:L], func=AF.Exp), reads=[sbk], writes=[ptk])
                    alltiles = ctiles + btiles
                    for ii, t_ in enumerate(alltiles):
                        for (p0, p1, q0, q1) in t_.get('zero', ()):
                            sch.op('pool', lambda e, ii=ii, p0=p0, p1=p1, q0=q0, q1=q1, PTb=PTb: e.memset(PTb[p0:p1, ii, q0:q1], 0.0),
                                   reads=[ptk], writes=[ptk])
                    return alltiles

                def attn_back(ui, alltiles):
                    h, s = units[ui]
                    bo, bok = ob[h // 4]
                    pti = ui % 2
                    PTb, ptk = M.PT[pti], ('PT', pti)
                    for ii, t_ in enumerate(alltiles):
                        nk, pb = t_['nk'], t_['pb']
                        sch.op('pe', lambda e, t_=t_, ii=ii, nk=nk, pb=pb, PTb=PTb, bo=bo, s=s, h=h: e.matmul(
                            bo[s * L:(s + 1) * L, (h % 4) * 65:(h % 4) * 65 + 65], lhsT=PTb[pb:pb + nk, ii, 0:L], rhs=t_['va'],
                            start=(ii == 0), stop=(ii == len(alltiles) - 1)),
                            reads=[ptk, t_['vakey']], writes=[bok], signal=(ii == len(alltiles) - 1))

                def attn_gen():
                    pend = attn_front(0)
                    for ui in range(len(units)):
                        nxt = attn_front(ui + 1) if ui + 1 < len(units) else None
                        attn_back(ui, pend)
                        pend = nxt
                        yield
                agen = [attn_gen()]

                def step_attn(n):
                    for _ in range(n * nseg):
                        try:
                            next(agen[0])
                        except StopIteration:
                            return

            while deferred:
                deferred.pop(0)()
            hpb = 512 // NT
            for g0 in range(0, 8, hpb):
                bk, bkk = bank()
                sch.op('pe', lambda e, bk=bk, g0=g0: e.matmul(bk[0:NT, 0:hpb * NT], lhsT=Um[0:NT, 0:NT],
                                                              rhs=RHSA[0:NT, g0:g0 + hpb, 0:NT], start=True, stop=True),
                       reads=[('RHSA', t_) for t_ in range(8)] + ['consts'], writes=[bkk])
                sch.op('act', lambda e, bk=bk, g0=g0: e.activation(out=LM[0:NT, g0:g0 + hpb, 0:NT],
                                                                  in_=bk[0:NT, 0:hpb * NT].rearrange("p (h l) -> p h l", h=hpb), func=AF.Exp),
                       reads=[bkk], writes=['LM'])
            bke, bkek = bank()
            sch.op('pe', lambda e: e.matmul(bke[0:NT, 0:8], lhsT=Tm[0:NT, 0:NT], rhs=DTA[0:NT, 3, :], start=True, stop=True),
                   reads=['DTA', 'consts'], writes=[bkek], signal=False)
            for s in range(nseg):
                osm = cOnes if nseg == 1 else P.consts[:, 6 + s, :]
                sch.op('pe', lambda e, s=s, osm=osm: e.matmul(bke[:, 8 + 8 * s:16 + 8 * s], lhsT=osm[0:NT, :], rhs=DTA[0:NT, 3, :], start=True, stop=True),
                       reads=['DTA', 'consts'], writes=[bkek], signal=(s == nseg - 1))
            sch.op('act', lambda e: e.activation(out=M.EE[0:NT, :], in_=bke[0:NT, 0:8], func=AF.Exp), reads=[bkek], writes=['EE'])
            sch.op('act', lambda e: e.activation(out=M.CD[:, 0:nseg, :], in_=bke[:, 8:8 + 8 * nseg].rearrange("p (s h) -> p s h", s=nseg), func=AF.Exp),
                   reads=[bkek], writes=['CD'])
            for s in range(nseg):
                sch.op('dve', lambda e, s=s: e.tensor_tensor(out=M.WDS[s * L:(s + 1) * L, :], in0=DTA[s * L:(s + 1) * L, 2, :],
                                                             in1=LM[s * L:(s + 1) * L, :, (s + 1) * L - 1], op=ALU.mult),
                       reads=['DTA', 'LM'], writes=['WDS'])

            sch.op('act', lambda e: e.activation(out=SIL[:, :, 0:NT], in_=ACC[:, :, 0:NT], func=AF.Silu), reads=[('ACC', t_) for t_ in range(8)], writes=['SIL'])
            sch.op('act', lambda e: e.activation(out=BCT[:, :, 0:NT], in_=ACC[:, 4:8, 0:NT], func=AF.Silu), reads=[('ACC', t_) for t_ in range(4, 8)], writes=['BCT'])
            step_attn(2)

            bkx, bkxk = bank()
            for q in range(4):
                sch.op('pe', lambda e, q=q: e.transpose(out=bkx[0:NT, q * 128:(q + 1) * 128], in_=SIL[:, q, 0:NT], identity=ident),
                       reads=['SIL', 'consts'], writes=[bkxk], signal=(q == 3))
            xsv = bkx[0:NT, :].rearrange("p (h d) -> p h d", h=8)
            if full:
                sch.op('act', lambda e: e.activation(out=M.XSK[0:NT, :], in_=bkx[0:NT, :], func=AF.Copy), reads=[bkxk], writes=['XSK'])
                sch.op('dve', lambda e: e.tensor_tensor(out=M.XDT[0:NT, :, :], in0=xsv, in1=bc(DTA[0:NT, 2, :], [NT, 8, 64], 2), op=ALU.mult),
                       reads=[bkxk, 'DTA'], writes=['XDT'])
            sch.op('dve', lambda e: e.tensor_tensor(out=M.XDD[0:NT, :, :], in0=xsv, in1=bc(M.WDS[0:NT, :], [NT, 8, 64], 2), op=ALU.mult),
                   reads=[bkxk, 'WDS'], writes=['XDD'])
            step_attn(2)
            bkb, bkbk = bank()
            for g in range(2):
                sch.op('pe', lambda e, g=g: e.transpose(out=bkb[0:NT, g * 128:(g + 1) * 128], in_=SIL[:, 4 + g, 0:NT], identity=ident),
                       reads=['SIL', 'consts'], writes=[bkbk], signal=(g == 1))
            sch.op('act', lambda e: e.activation(out=M.BK[0:NT, :, :], in_=bkb[0:NT, 0:256].rearrange("p (g n) -> p g n", g=2), func=AF.Copy),
                   reads=[bkbk], writes=['BK'])
            step_attn(2)

            if full:
                bkc, bkck = bank()
                for g in range(2):
                    sch.op('pe', lambda e, g=g: e.matmul(bkc[0:NT, g * NT:(g + 1) * NT], lhsT=BCT[:, g, 0:NT], rhs=BCT[:, 2 + g, 0:NT],
                                                         start=True, stop=True), reads=['BCT'], writes=[bkck], signal=(g == 1))
                sch.op('dve', lambda e: e.tensor_tensor(out=M.CBM[0:NT, :, 0:NT], in0=bkc[0:NT, 0:2 * NT].rearrange("p (g l) -> p g l", g=2),
                                                        in1=bc(Tm[0:NT, 0:NT], [NT, 2, NT], 1), op=ALU.mult),
                       reads=[bkck, 'consts'], writes=['CBM'])
                for g in range(2):
                    sch.op(('dve', 'pool')[g], lambda e, g=g: e.tensor_tensor(out=M.MT[0:NT, 4 * g:4 * g + 4, 0:NT], in0=LM[0:NT, 4 * g:4 * g + 4, 0:NT],
                                                                  in1=bc(M.CBM[0:NT, g, 0:NT], [NT, 4, NT], 1), op=ALU.mult),
                           reads=['LM', 'CBM'], writes=[('MT', g)])
                bko, bkok = bank()
                for s in range(nseg):
                    _, HBs, hk = HFs[s]
                    for g in range(2):
                        sch.op('pe', lambda e, s=s, g=g, HBs=HBs: e.matmul(bko[s * L:(s + 1) * L, g * 256:(g + 1) * 256], lhsT=BCT[:, 2 + g, s * L:(s + 1) * L],
                                                                           rhs=HBs[:, g * 256:(g + 1) * 256], start=True, stop=True),
                               reads=['BCT', ('HB', hk)], writes=[bkok], signal=(s == nseg - 1 and g == 1))
                bky, bkyk = bank()
                for h in range(8):
                    sch.op('pe', lambda e, h=h: e.matmul(bky[0:NT, h * 64:(h + 1) * 64], lhsT=M.MT[0:NT, h, 0:NT], rhs=M.XDT[0:NT, h, :],
                                                         start=True, stop=True), reads=[('MT', h // 4), 'XDT'], writes=[bkyk], signal=(h == 7))

            for s in range(nseg):
                HFt, HBt, hk = HFs[s]
                bks, bksk = bank()
                for g in range(2):
                    sch.op('pe', lambda e, s=s, g=g, bks=bks: e.matmul(bks[:, g * 256:(g + 1) * 256], lhsT=M.BK[s * L:(s + 1) * L, g, :],
                                                                       rhs=M.XDD[s * L:(s + 1) * L, 4 * g:4 * g + 4, :], start=True, stop=True),
                           reads=['BK', 'XDD'], writes=[bksk], signal=(g == 1))
                hv = HFt[:].rearrange("p (h d) -> p h d", h=8)
                sch.op('pool', lambda e, s=s, hv=hv: e.tensor_tensor(out=hv, in0=hv, in1=bc(M.CD[:, s, :], [128, 8, 64], 2), op=ALU.mult),
                       reads=[('HF', hk), 'CD', ('HB', hk)], writes=[('HF', hk)])
                sch.op('dve', lambda e, HFt=HFt, bks=bks: e.tensor_tensor(out=HFt[:], in0=HFt[:], in1=bks[:, 0:512], op=ALU.add),
                       reads=[('HF', hk), bksk], writes=[('HF', hk)])
                if outs.get('flag_state'):
                    sch.op('dve', lambda e, HFt=HFt: e.tensor_scalar(out=HFt[:], in0=HFt[:], scalar1=flg[:, 0:1], scalar2=None, op0=ALU.mult),
                           reads=[('HF', hk), 'flg'], writes=[('HF', hk)])
                sch.op('act', lambda e, HFt=HFt, HBt=HBt: e.activation(out=HBt[:], in_=HFt[:], func=AF.Copy), reads=[('HF', hk)], writes=[('HB', hk)])

            if not full:
                return

            T1, T2, YY, AST = M.T1, M.T2, M.YY, M.AST
            v8 = lambda ap: ap.rearrange("p (h d) -> p h d", h=8)
            sch.op('dve', lambda e: e.tensor_tensor(out=v8(T1[0:NT, :]), in0=v8(bko[0:NT, :]), in1=bc(M.EE[0:NT, :], [NT, 8, 64], 2), op=ALU.mult),
                   reads=[bkok, 'EE'], writes=['T1'])
            sch.op('pool', lambda e: e.tensor_tensor(out=v8(T2[0:NT, :]), in0=v8(M.XSK[0:NT, :]), in1=bc(rowp[0:NT, 16:24], [NT, 8, 64], 2), op=ALU.mult),
                   reads=['XSK', 'rowp'], writes=['T2'])
            sch.op('dve', lambda e: e.tensor_tensor(out=T2[0:NT, :], in0=T2[0:NT, :], in1=T1[0:NT, :], op=ALU.add), reads=['T1', 'T2'], writes=['T2'])
            sch.op('dve', lambda e: e.tensor_tensor(out=YY[0:NT, :], in0=bky[0:NT, :], in1=T2[0:NT, :], op=ALU.add), reads=[bkyk, 'T2'], writes=['YY'])
            dump(0, YY[0:NT, :], NT, 512, 'YY')
            dump(1, M.SZ[0:NT, :], NT, 512, 'SZ')
            dump(5, T1[0:NT, :], NT, 512, 'T1')
            dump(6, M.XSK[0:NT, :], NT, 512, 'XSK')
            dump(7, M.EE[0:NT, :], NT, 8, 'EE')
            dump(8, M.DTA[0:NT, :, :].rearrange("p a b -> p (a b)"), NT, 32, 'DTA')
            sch.op('dve', lambda e: e.tensor_tensor(out=YY[0:NT, :], in0=YY[0:NT, :], in1=M.SZ[0:NT, :], op=ALU.mult), reads=['YY', 'SZ'], writes=['YY'])
            step_attn(2)
            sch.op('act', lambda e: e.activation(out=M.JB[0:NT, :], in_=YY[0:NT, :], func=AF.Square, accum_out=SMALL[0:NT, 0:1]),
                   reads=['YY'], writes=['JB', ('SM', 0)])
            rstd_from_sumsq(SMALL[0:NT, 0:1], 512.0, NT, ('SM', 0))
            sch.op('dve', lambda e: e.tensor_scalar(out=YY[0:NT, :], in0=YY[0:NT, :], scalar1=SMALL[0:NT, 0:1], scalar2=None, op0=ALU.mult),
                   reads=['YY', ('SM', 0)], writes=['YY'])

            def post_s(j, n, src, bkk):
                sch.op('dve', lambda e: e.tensor_tensor(out=AST[:, 4:8, 0:NT], in0=src, in1=bc(colp[:, CP_SG:CP_SG + 4], [128, 4, NT], 2), op=ALU.mult),
                       reads=[bkk, 'colp'], writes=['AST'])
            transpose_to(None, lambda j: YY[0:NT, j * 128:(j + 1) * 128], NT, 4, reads=['YY'], writes=['AST'], post=post_s)

            step_attn(1000)
            AA = T1
            AA3 = AA[0:NT, :].rearrange("p (h d) -> p h d", h=8)
            if dbg:
                sch.op('act', lambda e: e.activation(out=T2[0:NT, 0:260], in_=ob[0][0][0:NT, 0:260], func=AF.Copy), reads=[ob[0][1]], writes=['T2'])
                dump(9, T2[0:NT, 0:260], NT, 260, 'T2')
                sch.op('act', lambda e: e.activation(out=T2[:, 0:320].rearrange("p (a b) -> p a b", a=5), in_=M.PT[1][:, :, 0:64], func=AF.Copy), reads=[('PT', 1)], writes=['T2'])
                dump(10, T2[:, 0:320], 128, 320, 'T2')
            for hh in range(2):
                bo, bok = ob[hh]
                ov = bo[0:NT, 0:260].rearrange("p (h d) -> p h d", h=4)
                sch.op('dve', lambda e, ov=ov, hh=hh: e.reciprocal(out=SMALL[0:NT, 8 + 4 * hh:12 + 4 * hh], in_=ov[:, :, 64]),
                       reads=[bok], writes=[('SM', 1 + hh)])
                sch.op('dve', lambda e, ov=ov, hh=hh: e.tensor_tensor(out=AA3[:, 4 * hh:4 * hh + 4, :], in0=ov[:, :, 0:64],
                                                                      in1=bc(SMALL[0:NT, 8 + 4 * hh:12 + 4 * hh], [NT, 4, 64], 2), op=ALU.mult),
                       reads=[bok, ('SM', 1 + hh)], writes=['T1'])
            AAf = AA[0:NT, :]
            dump(2, AAf, NT, 512, 'T1')
            sch.op('act', lambda e: e.activation(out=M.JB[0:NT, :], in_=AAf, func=AF.Square, accum_out=SMALL[0:NT, 1:2]),
                   reads=['T1'], writes=['JB', ('SM', 3)])
            rstd_from_sumsq(SMALL[0:NT, 1:2], 512.0, NT, ('SM', 3))
            sch.op('dve', lambda e: e.tensor_scalar(out=AAf, in0=AAf, scalar1=SMALL[0:NT, 1:2], scalar2=None, op0=ALU.mult),
                   reads=['T1', ('SM', 3)], writes=['T1'])

            def post_a(j, n, src, bkk):
                sch.op('dve', lambda e: e.tensor_tensor(out=AST[:, 0:4, 0:NT], in0=src, in1=bc(colp[:, CP_AG:CP_AG + 4], [128, 4, NT], 2), op=ALU.mult),
                       reads=[bkk, 'colp'], writes=['AST'])
            transpose_to(None, lambda j: AAf[:, j * 128:(j + 1) * 128], NT, 4, reads=['T1'], writes=['AST'], post=post_a)

            for c in range(2):
                bk, bkk = bank()
                for kt in range(KT):
                    sch.op('pe', lambda e, kt=kt, c=c, bk=bk: e.matmul(bk[0:NT, :], lhsT=AST[:, kt, 0:NT], rhs=Wo[:, kt, c * 512:(c + 1) * 512],
                                                                       start=(kt == 0), stop=(kt == KT - 1)),
                           reads=['AST', ('Wo', kt)], writes=[bkk], signal=(kt == KT - 1))
                sch.op('dve', lambda e, c=c, bk=bk: e.scalar_tensor_tensor(out=Xb[0:NT, c * 512:(c + 1) * 512], in0=Xb[0:NT, c * 512:(c + 1) * 512],
                                                                           scalar=ALPHA, in1=bk[0:NT, :], op0=ALU.mult, op1=ALU.add),
                       reads=[xk, bkk], writes=[xk])
            x1 = P.X1S[0:NT, j_in_super, :]
            dump(3, Xb[0:NT, :], NT, D, xk)
            layer_norm(Xb[0:NT, :], x1, NT, 0, xk, ('X1S', j_in_super))
            dump(4, x1, NT, D, ('X1S', j_in_super))
            t0 = j_in_super * 128
            def x1T_later():
                transpose_to(lambda j, n: P.X1T[:, j:j + n, t0:t0 + NT], lambda j: x1[:, j * 128:(j + 1) * 128], NT, KT,
                             reads=[('X1S', j_in_super), (('X1S', j_in_super), 0), (('X1S', j_in_super), 1)], writes=[('X1T', j_in_super)], engs=('act', 'act'))
            deferred.append(x1T_later)

        def ffn(Fb, NTs, nseg, L, jlist, ft_t, ft_key, y_dsts, fconv_out, tails_only, defer_ln2=False):
            bstate['nb'] = 8
            x1keys = [('X1T', j) for j in range(4)]
            pend_gate = [None]
            for f in range(NF):
                ui = f % 3
                sch.dma('sp', Fb.Wup[ui][:, :, :], wb_up[:, f * 256:(f + 1) * 256].rearrange("(kt p) n -> p kt n", p=128),
                        reads=WUPK, writes=[('Wup', ui)])
                hbs = []
                for part in range(2):
                    fp = part * NF + f
                    bk, bkk = bank()
                    for kt in range(KT):
                        sch.op('pe', lambda e, kt=kt, bk=bk, ui=ui, part=part: e.matmul(bk[:, 0:NTs], lhsT=Fb.Wup[ui][:, kt, part * 128:(part + 1) * 128],
                                                                                        rhs=P.X1T[:, kt, 0:NTs], start=(kt == 0), stop=(kt == KT - 1)),
                               reads=x1keys + [('Wup', ui)], writes=[bkk], signal=(kt == KT - 1))
                    hi_ = part * 2 + (f % 2)
                    ps3 = bk[:, 0:NTs].rearrange("p (s l) -> p s l", s=nseg)
                    hd = Fb.HEAD[hi_][:, 0:nseg * 4].rearrange("p (s c) -> p s c", s=nseg)
                    hdk = ('HEAD', hi_)
                    if not tails_only:
                        sch.op('pool', lambda e, hd=hd, fp=fp: e.tensor_copy(out=hd[:, :, 0:2], in_=ft_t[:, fp, :, :]), reads=[(ft_key, fp)], writes=[hdk])
                        sch.op('act', lambda e, hd=hd, ps3=ps3: e.activation(out=hd[:, :, 2:4], in_=ps3[:, :, 0:2], func=AF.Copy), reads=[bkk], writes=[hdk])
                    sch.op('act', lambda e, ps3=ps3, fp=fp: e.activation(out=ft_t[:, fp, :, :], in_=ps3[:, :, L - 2:L], func=AF.Copy),
                           reads=[bkk], writes=[(ft_key, fp)])
                    if not tails_only:
                        cw = CP_FW + 3 * fp
                        sch.op('act', lambda e, bk=bk, hi_=hi_, cw=cw, fp=fp: e.activation(
                            out=Fb.FACC[hi_][:, 0:NTs], in_=bk[:, 0:NTs], func=AF.Identity, scale=colp[:, cw + 2:cw + 3],
                            bias=colp[:, CP_FB + fp:CP_FB + fp + 1]), reads=[bkk, 'colp'], writes=[('FACC', hi_), ('FACCh', hi_)])
                    hbs.append((ps3, bkk, hd, hdk, fp, hi_))
                if tails_only:
                    continue
                for jj in (1, 0):
                    for (ps3, bkk, hd, hdk, fp, hi_) in hbs:
                        fa = Fb.FACC[hi_][:, 0:NTs].rearrange("p (s l) -> p s l", s=nseg)
                        cw = CP_FW + 3 * fp
                        sch.op('dve', lambda e, ps3=ps3, fa=fa, cw=cw, jj=jj: e.scalar_tensor_tensor(out=fa[:, :, 2:L], in0=ps3[:, :, jj:jj + L - 2],
                                                                                                scalar=colp[:, cw + jj:cw + jj + 1], in1=fa[:, :, 2:L],
                                                                                                op0=ALU.mult, op1=ALU.add),
                               reads=[bkk, 'colp', ('FACC', hi_)], writes=[('FACC', hi_)])
                for jj in (1, 0):
                    for (ps3, bkk, hd, hdk, fp, hi_) in hbs:
                        fa = Fb.FACC[hi_][:, 0:NTs].rearrange("p (s l) -> p s l", s=nseg)
                        cw = CP_FW + 3 * fp
                        sch.op('dve', lambda e, hd=hd, fa=fa, cw=cw, jj=jj: e.scalar_tensor_tensor(out=fa[:, :, 0:2], in0=hd[:, :, jj:jj + 2],
                                                                                              scalar=colp[:, cw + jj:cw + jj + 1], in1=fa[:, :, 0:2],
                                                                                              op0=ALU.mult, op1=ALU.add),
                               reads=[hdk, 'colp', ('FACCh', hi_)], writes=[('FACCh', hi_)])
                def gate(f=f):
                    fv, fg = Fb.FACC[f % 2], Fb.FACC[2 + f % 2]
                    kv_, kg_ = ('FACC', f % 2), ('FACC', 2 + f % 2)
                    kvh, kgh = ('FACCh', f % 2), ('FACCh', 2 + f % 2)
                    sch.op('act', lambda e: e.activation(out=fg[:, 0:NTs], in_=fg[:, 0:NTs], func=AF.Silu), reads=[kg_, kgh], writes=[kg_, kgh])
                    sch.op('pool', lambda e: e.tensor_tensor(out=Fb.GT[:, f, 0:NTs], in0=fv[:, 0:NTs], in1=fg[:, 0:NTs], op=ALU.mult),
                           reads=[kv_, kg_, kvh, kgh], writes=[('GT', f)])
                if pend_gate[0] is not None:
                    pend_gate[0]()
                pend_gate[0] = gate
            if pend_gate[0] is not None:
                pend_gate[0]()
                pend_gate[0] = None
            if fconv_out is not None:
                nr = 2 * nseg
                per = 128 // nr // 2 * 2
                per = min(per, 42)
                f0 = 0
                while f0 < 42:
                    nf_ = min(per, 42 - f0)
                    ftf = ft_t[:, f0:f0 + nf_, :, :].rearrange("p f s r -> p (f s r)")
                    bkt, bktk = bank()
                    sch.op('pe', lambda e, ftf=ftf, bkt=bkt, nf_=nf_: e.transpose(out=bkt[0:nf_ * nr, 0:128], in_=ftf, identity=ident),
                           reads=[(ft_key, q_) for q_ in range(42)] + ['consts'], writes=[bktk])
                    sch.op('act', lambda e, bkt=bkt, nf_=nf_: e.activation(out=P.TT[0:nf_ * nr, :], in_=bkt[0:nf_ * nr, 0:128], func=AF.Copy),
                           reads=[bktk], writes=['TT'])
                    for q in range(nf_):
                        fp = f0 + q
                        sch.dma('sp', fconv_out[:, fp * 128:(fp + 1) * 128], P.TT[q * nr:(q + 1) * nr, :], reads=['TT'])
                    f0 += nf_
            if tails_only:
                return
            gkeys = [('GT', f) for f in range(NF)]
            for cb in range(4):
                di = cb % 2
                sch.dma('sp', Fb.Wdn[di][:, :, :], wb_down[:, cb * 256:(cb + 1) * 256].rearrange("(f p) n -> p f n", p=128),
                        reads=WDNK, writes=[('Wdn', di)])
                for (j, t0, nt) in jlist:
                    bk, bkk = bank()
                    for f in range(NF):
                        sch.op('pe', lambda e, f=f, bk=bk, di=di, t0=t0, nt=nt: e.matmul(bk[0:nt, 0:256], lhsT=Fb.GT[:, f, t0:t0 + nt], rhs=Fb.Wdn[di][:, f, :],
                                                                                         start=(f == 0), stop=(f == NF - 1)),
                               reads=gkeys + [('Wdn', di)], writes=[bkk], signal=(f == NF - 1))
                    xs_ = P.X1S[0:nt, j, cb * 256:(cb + 1) * 256]
                    sch.op('dve', lambda e, xs_=xs_, bk=bk, nt=nt: e.scalar_tensor_tensor(out=xs_, in0=xs_, scalar=ALPHA, in1=bk[0:nt, 0:256],
                                                                                         op0=ALU.mult, op1=ALU.add),
                           reads=[('X1S', j), (('X1S', j), 0), (('X1S', j), 1), bkk], writes=[('X1S', j)])
            def ln2_later():
                for n_, (j, t0, nt) in enumerate(jlist):
                    layer_norm(P.X1S[0:nt, j, :], P.X1S[0:nt, j, :], nt, 2, ('X1S', j), ('X1S', j))
                    sch.dma('sp', y_dsts[n_], P.X1S[0:nt, j, :], reads=[('X1S', j), (('X1S', j), 0), (('X1S', j), 1)])
            if defer_ln2:
                return ln2_later
            ln2_later()

        def final_state(HFt, hk, dst, CK):
            def post_h(j, n, src, bkk):
                sch.op('act', lambda e: e.activation(out=CK[:, 0:4, 0:128], in_=src, func=AF.Copy), reads=[bkk], writes=['CKo'])
            transpose_to(None, lambda j: HFt[:, j * 128:(j + 1) * 128], 128, 4, reads=[('HF', hk)], writes=['CKo'], post=post_h)
            sch.dma('sp', dst.rearrange("(a p) n -> p a n", p=128), CK[:, 0:4, 0:128], reads=['CKo'])


        def light_bufs(st):
            Lb = NS()
            Lb.XBCL = sbuf(st, "XBCL", [128, 8, 515])
            Lb.ACCL = sbuf(st, "ACCL", [128, 8, 512])
            Lb.SILL = sbuf(st, "SILL", [128, 8, 512])
            Lb.DTAL = sbuf(st, "DTAL", [128, 4, 4, 8])
            Lb.WDSL = sbuf(st, "WDSL", [128, 4, 8])
            Lb.CDL = sbuf(st, "CDL", [128, 8])
            Lb.XDDL = sbuf(st, "XDDL", [128, 4, 8, 64], BF16)
            Lb.BKL = sbuf(st, "BKL", [128, 4, 2, 128], BF16)
            sch.op('dve', lambda e: e.memset(Lb.XBCL[:, :, 0:3], 0.0), writes=[('XBCL', t_) for t_ in range(8)])
            return Lb

        def light_tile(Lb, b0, nb, kv, last):
            bstate['nb'] = 6
            NT = nb * 128
            XL = P.X1S
            XTL = P.X1T
            XBCL, ACCL, SILL, DTAL = Lb.XBCL, Lb.ACCL, Lb.SILL, Lb.DTAL
            sch.dma('sp', XL[:, 0:nb, :], xs[b0 * 128:(b0 + nb) * 128, :].rearrange("(c p) d -> p c d", p=128), writes=['XL'])
            for c in range(nb):
                transpose_to(lambda j, n, c=c: XTL[:, j:j + n, c * 128:(c + 1) * 128], lambda j, c=c: XL[:, c, j * 128:(j + 1) * 128], 128, KT,
                             reads=['XL'], writes=[('XTL', c)])
            xtk = [('XTL', c) for c in range(nb)]
            for t in range(8):
                bk, bkk = bank()
                cc = C_XBC + t * 128
                for kt in range(KT):
                    sch.op('pe', lambda e, kt=kt, cc=cc, bk=bk: e.matmul(bk[:, 0:NT], lhsT=Wi[:, kt, cc:cc + 128], rhs=XTL[:, kt, 0:NT],
                                                                         start=(kt == 0), stop=(kt == KT - 1)),
                           reads=xtk + [('Wi', kt)], writes=[bkk], signal=(kt == KT - 1))
                sch.op('act', lambda e, t=t, bk=bk: e.activation(out=XBCL[:, t, 3:3 + NT], in_=bk[:, 0:NT], func=AF.Copy),
                       reads=[bkk], writes=[('XBCL', t)])
                cw_ = CP_CW + 4 * t
                sch.op('act', lambda e, t=t, bk=bk, cw_=cw_: e.activation(out=ACCL[:, t, 0:NT], in_=bk[:, 0:NT], func=AF.Identity,
                                                                          scale=colp[:, cw_ + 3:cw_ + 4], bias=colp[:, CP_CB + t:CP_CB + t + 1]),
                       reads=[bkk, 'colp'], writes=[('ACCL', t)])
            bkd, bkdk = bank()
            for c in range(nb):
                for kt in range(KT):
                    sch.op('pe', lambda e, kt=kt, c=c: e.matmul(bkd[:, c * 8:(c + 1) * 8], lhsT=XTL[:, kt, c * 128:(c + 1) * 128], rhs=Wi[:, kt, C_DT:C_DT + 8],
                                                                start=(kt == 0), stop=(kt == KT - 1)),
                           reads=xtk + [('Wi', kt)], writes=[bkdk], signal=(c == nb - 1 and kt == KT - 1))
            u_, au, dt_, a_ = (DTAL[:, i, 0:nb, :] for i in range(4))
            sch.op('dve', lambda e: e.tensor_tensor(out=u_, in0=bkd[:, 0:nb * 8].rearrange("p (c h) -> p c h", c=nb),
                                                    in1=bc(rowp[:, 0:8], [128, nb, 8], 1), op=ALU.add), reads=[bkdk, 'rowp'], writes=['DTAL'])
            sch.op('act', lambda e: e.activation(out=au, in_=u_, func=AF.Abs), reads=['DTAL'], writes=['DTAL'])
            sch.op('act', lambda e: e.activation(out=au, in_=au, func=AF.Exp, scale=-1.0), reads=['DTAL'], writes=['DTAL'])
            sch.op('act', lambda e: e.activation(out=au, in_=au, func=AF.Ln, bias=1.0, scale=1.0), reads=['DTAL'], writes=['DTAL'])
            sch.op('dve', lambda e: e.scalar_tensor_tensor(out=dt_, in0=u_, scalar=0.0, in1=au, op0=ALU.max, op1=ALU.add),
                   reads=['DTAL'], writes=['DTAL'])
            sch.op('dve', lambda e: e.tensor_tensor(out=a_, in0=dt_, in1=bc(P.Aneg[:, :], [128, nb, 8], 1), op=ALU.mult),
                   reads=['DTAL', 'Aneg'], writes=['DTAL'])
            if kv:
                for hp in range(4):
                    bk, bkk = bank()
                    cc = C_K + hp * 128
                    for kt in range(KT):
                        sch.op('pe', lambda e, kt=kt, cc=cc, bk=bk: e.matmul(bk[:, 0:NT], lhsT=Wi[:, kt, cc:cc + 128], rhs=XTL[:, kt, 0:NT],
                                                                             start=(kt == 0), stop=(kt == KT - 1)),
                               reads=xtk + [('Wi', kt)], writes=[bkk], signal=(kt == KT - 1))
                    for c in range(nb):
                        sl = (b0 + c) % NKR
                        sch.op('dve', lambda e, c=c, sl=sl, hp=hp, bk=bk: e.tensor_copy(out=P.KTr[:, sl, hp, :], in_=bk[:, c * 128:(c + 1) * 128]),
                               reads=[bkk], writes=[('KTr', sl)])
                for c in range(nb):
                    sl = (b0 + c) % NKR
                    bkv, bkvk = bank()
                    for kt in range(KT):
                        sch.op('pe', lambda e, kt=kt, c=c, bkv=bkv: e.matmul(bkv[:, :], lhsT=XTL[:, kt, c * 128:(c + 1) * 128], rhs=Wi[:, kt, C_V:C_V + 512],
                                                                             start=(kt == 0), stop=(kt == KT - 1)),
                               reads=xtk + [('Wi', kt)], writes=[bkvk], signal=(kt == KT - 1))
                    sch.op('act', lambda e, sl=sl, bkv=bkv: e.activation(out=P.VAr[:, sl, :, 0:64], in_=bkv[:, :].rearrange("p (h d) -> p h d", h=8),
                                                                         func=AF.Copy), reads=[bkvk], writes=[('VAr', sl)])
                    sch.op('dve', lambda e, sl=sl: e.tensor_copy(out=P.VAr[:, sl, :, 64], in_=flg[:, 0:1].to_broadcast([128, 8])),
                           reads=['flg'], writes=[('VAr', sl)])
            for jj in (0, 1, 2):
                for t in range(8):
                    cw = CP_CW + 4 * t
                    acc = ACCL[:, t, 0:NT]
                    if jj == 3:
                        sch.op('dve', lambda e, t=t, cw=cw, acc=acc: e.tensor_scalar(out=acc, in0=XBCL[:, t, 3:3 + NT], scalar1=colp[:, cw + 3:cw + 4],
                                                                                     scalar2=colp[:, CP_CB + t:CP_CB + t + 1], op0=ALU.mult, op1=ALU.add),
                               reads=[('XBCL', t), 'colp'], writes=[('ACCL', t)])
                    else:
                        sch.op('dve', lambda e, t=t, cw=cw, acc=acc, jj=jj: e.scalar_tensor_tensor(out=acc, in0=XBCL[:, t, jj:jj + NT],
                                                                                                  scalar=colp[:, cw + jj:cw + jj + 1], in1=acc,
                                                                                                  op0=ALU.mult, op1=ALU.add),
                               reads=[('XBCL', t), 'colp', ('ACCL', t)], writes=[('ACCL', t)])
            for hf_ in range(2):
                sch.op('act', lambda e, hf_=hf_: e.activation(out=SILL[:, 4 * hf_:4 * hf_ + 4, 0:NT], in_=ACCL[:, 4 * hf_:4 * hf_ + 4, 0:NT], func=AF.Silu),
                       reads=[('ACCL', t_) for t_ in range(4 * hf_, 4 * hf_ + 4)], writes=[('SILL', hf_)])
            if last:
                sch.op('dve', lambda e: e.tensor_copy(out=P.XBC[:, :, 128:131], in_=XBCL[:, :, NT:NT + 3]),
                       reads=[('XBCL', t_) for t_ in range(8)], writes=['XBC'])
            else:
                sch.op('dve', lambda e: e.tensor_copy(out=XBCL[:, :, 0:3], in_=XBCL[:, :, NT:NT + 3]),
                       reads=[('XBCL', t_) for t_ in range(8)] + [('ACCL', t_) for t_ in range(8)], writes=[('XBCL', t_) for t_ in range(8)])
            bke, bkek = bank()
            for c in range(nb):
                ops_ = [(cU, c)] + [(cOnes, c2) for c2 in range(c + 1, nb)]
                for ii, (lm, c2) in enumerate(ops_):
                    sch.op('pe', lambda e, c=c, lm=lm, c2=c2, ii=ii, n_=len(ops_): e.matmul(bke[:, c * 8:(c + 1) * 8], lhsT=lm, rhs=DTAL[:, 3, c2, :],
                                                                                          start=(ii == 0), stop=(ii == n_ - 1)),
                           reads=['DTAL', 'consts'], writes=[bkek], signal=False)
            for c in range(nb):
                sch.op('pe', lambda e, c=c: e.matmul(bke[:, 32:40], lhsT=cOnes, rhs=DTAL[:, 3, c, :], start=(c == 0), stop=(c == nb - 1)),
                       reads=['DTAL', 'consts'], writes=[bkek], signal=(c == nb - 1))
            sch.op('act', lambda e: e.activation(out=Lb.WDSL[:, 0:nb, :], in_=bke[:, 0:nb * 8].rearrange("p (c h) -> p c h", c=nb), func=AF.Exp),
                   reads=[bkek], writes=['WDSL'])
            sch.op('act', lambda e: e.activation(out=Lb.CDL[:, :], in_=bke[:, 32:40], func=AF.Exp), reads=[bkek], writes=['CDL'])
            sch.op('dve', lambda e: e.tensor_tensor(out=Lb.WDSL[:, 0:nb, :], in0=Lb.WDSL[:, 0:nb, :], in1=dt_, op=ALU.mult),
                   reads=['WDSL', 'DTAL'], writes=['WDSL'])
            for c in range(nb):
                bkx, bkxk = bank()
                for q in range(4):
                    sch.op('pe', lambda e, q=q, c=c, bkx=bkx: e.transpose(out=bkx[:, q * 128:(q + 1) * 128], in_=SILL[:, q, c * 128:(c + 1) * 128], identity=ident),
                           reads=[('SILL', 0), 'consts'], writes=[bkxk], signal=(q == 3))
                sch.op('dve', lambda e, c=c, bkx=bkx: e.tensor_tensor(out=Lb.XDDL[:, c, :, :], in0=bkx[:, :].rearrange("p (h d) -> p h d", h=8),
                                                                      in1=bc(Lb.WDSL[:, c, :], [128, 8, 64], 2), op=ALU.mult),
                       reads=[bkxk, 'WDSL'], writes=[('XDDL', c)])
            for c0 in range(0, nb, 2):
                n2 = min(2, nb - c0)
                bkb, bkbk = bank()
                for cc_ in range(n2):
                    for g in range(2):
                        sch.op('pe', lambda e, cc_=cc_, g=g, c0=c0, bkb=bkb: e.transpose(out=bkb[:, (cc_ * 2 + g) * 128:(cc_ * 2 + g + 1) * 128],
                                                                                         in_=SILL[:, 4 + g, (c0 + cc_) * 128:(c0 + cc_ + 1) * 128], identity=ident),
                               reads=[('SILL', 1), 'consts'], writes=[bkbk], signal=(cc_ == n2 - 1 and g == 1))
                sch.op('act', lambda e, c0=c0, n2=n2, bkb=bkb: e.activation(out=Lb.BKL[:, c0:c0 + n2, :, :],
                                                                           in_=bkb[:, 0:n2 * 256].rearrange("p (c g n) -> p c g n", c=n2, g=2), func=AF.Copy),
                       reads=[bkbk], writes=[('BKL', c0 // 2)])
            bks, bksk = bank()
            for g in range(2):
                for c in range(nb):
                    sch.op('pe', lambda e, g=g, c=c: e.matmul(bks[:, g * 256:(g + 1) * 256], lhsT=Lb.BKL[:, c, g, :], rhs=Lb.XDDL[:, c, 4 * g:4 * g + 4, :],
                                                              start=(c == 0), stop=(c == nb - 1)),
                           reads=[('BKL', c // 2), ('XDDL', c)], writes=[bksk], signal=(g == 1 and c == nb - 1))
            hv = P.HF[:].rearrange("p (h d) -> p h d", h=8)
            sch.op('dve', lambda e: e.tensor_tensor(out=hv, in0=hv, in1=bc(Lb.CDL[:, :], [128, 8, 64], 2), op=ALU.mult),
                   reads=[('HF', 0), 'CDL'], writes=[('HF', 0)])
            sch.op('dve', lambda e: e.tensor_tensor(out=P.HF[:], in0=P.HF[:], in1=bks[:, 0:512], op=ALU.add),
                   reads=[('HF', 0), bksk], writes=[('HF', 0)])
            if last:
                sch.op('act', lambda e: e.activation(out=P.HB[:], in_=P.HF[:], func=AF.Copy), reads=[('HF', 0)], writes=[('HB', 0)])

        def tiles_p(b):
            def f(s, h):
                hp, po = h // 2, (h % 2) * 64
                tl = []
                for i in (2, 3, 4, 0, 1):
                    sl = (b - i) % NKR
                    t_ = dict(kt=P.KTr[po:po + 64, sl, hp, :], ktkey=('KTr', sl), va=P.VAr[:, sl, h, :], vakey=('VAr', sl), nk=128, pb=0,
                              bias=None if i >= 2 else P.biasT[:, i, h, :])
                    if i == 4:
                        t_['zero'] = [(0, 64, 64, 128)]
                    if i == 0:
                        t_['zero'] = [(64, 128, 0, 64)]
                    tl.append(t_)
                return tl
            return f

        def run_blocks(M, b_lo, b_hi):
            for b in range(b_lo, b_hi):
                kind = 'ssm' if b < B_KV else ('kv' if b < B_FULL else 'full')
                sch.op('pool', lambda e: e.tensor_copy(out=P.XBC[:, :, 0:3], in_=P.XBC[:, :, 128:131]), reads=['XBC'], writes=['XBC'])
                outs = {}
                if b >= NBLK - 4:
                    r0 = (b - (NBLK - 4)) * 128
                    outs['nk'] = nk_o[r0:r0 + 128, :]
                    outs['nv'] = nv_o[r0:r0 + 128, :]
                if b == NBLK - 1:
                    outs['sconv'] = sc_o
                if b == B_FULL:
                    outs['flag_state'] = True
                j_in_super = 0 if b <= B_FULL else (b - B_MAIN) % 4
                sl = b % NKR
                kvd = (lambda j, n, sl=sl: P.KTr[:, sl, j:j + n, :], ('KTr', sl), P.VAr[:, sl, :, :], ('VAr', sl))
                mixer_block(M, kind, 128, 1, 128, xs[b * 128:(b + 1) * 128, :], b % 2,
                            P.XBC[:].rearrange("p t (s l) -> p t s l", s=1), 'XBC', [(P.HF, P.HB, 0)], tiles_p(b), outs, kvd, j_in_super)

        if do_prompt:
            sP = ExitStack()
            P.KTr = sbuf(sP, "KTr", [128, NKR, 4, 128], BF16)
            P.VAr = sbuf(sP, "VAr", [128, NKR, 8, 65], BF16)
            P.HF = sbuf(sP, "HF0", [128, 512])
            P.HB = sbuf(sP, "HB0", [128, 512], BF16)
            P.FT = sbuf(sP, "FT", [128, 42, 2])
            sch.op('dve', lambda e: e.memset(P.HF[:], 0.0), writes=[('HF', 0)])
            sch.op('dve', lambda e: e.memset(P.HB[:], 0.0), writes=[('HB', 0)])
            sch.op('dve', lambda e: e.memset(P.FT[:], 0.0), writes=[('FT', q_) for q_ in range(42)])
            FT4 = P.FT[:].rearrange("p f (s r) -> p f s r", s=1)
            sch.barrier(skip_pool_dma=True)
            if nblk == NBLK:
                with ExitStack() as st:
                    Lb = light_bufs(st)
                    tiles_ = [(0, 3)] + [(3 + 4 * i, 4) for i in range(7)]
                    for (b0_, nb_) in tiles_:
                        light_tile(Lb, b0_, nb_, b0_ + nb_ == B_FULL, b0_ + nb_ == B_FULL)
                    sch.barrier(skip_pool_dma=True)
            with ExitStack() as st:
                M = mixer_bufs(st)
                run_blocks(M, B_FULL if nblk == NBLK else NBLK - nblk, B_FULL + 1)
                sl = B_FULL % NKR
                sch.op('pool', lambda e: e.tensor_copy(out=P.VAr[:, sl, :, 64], in_=flg[:, 0:1].to_broadcast([128, 8])),
                       reads=['flg', ('VAr', sl)], writes=[('VAr', sl)])
                sch.barrier()
            with ExitStack() as st:
                Fb = ffn_bufs(st)
                ffn(Fb, 128, 1, 128, [(0, 0, 128)], FT4, 'FT', [None], None, True)
                sch.op('dve', lambda e: e.tensor_scalar(out=P.FT[:], in0=P.FT[:], scalar1=flg[:, 0:1], scalar2=None, op0=ALU.mult),
                       reads=[('FT', q_) for q_ in range(42)] + ['flg'], writes=[('FT', q_) for q_ in range(42)])
                sch.barrier()
            pend_ln2 = None
            for b0 in range(B_MAIN, NBLK, 4):
                with ExitStack() as st:
                    M = mixer_bufs(st)
                    if pend_ln2 is not None:
                        pend_ln2()
                        pend_ln2 = None
                    run_blocks(M, b0, b0 + 4)
                    if b0 + 4 == NBLK:
                        final_state(P.HF, 0, ssm_o, M.LM[:, 0:4, :])
                    sch.barrier()
                with ExitStack() as st:
                    Fb = ffn_bufs(st)
                    jl = [(j, j * 128, 128) for j in range(4)]
                    yd = [y_o[(b0 - B_MAIN + j) * 128:(b0 - B_MAIN + j + 1) * 128, :] for j in range(4)]
                    pend_ln2 = ffn(Fb, 512, 1, 512, jl, FT4, 'FT', yd, fc_o if b0 + 4 == NBLK else None, False, defer_ln2=(b0 + 4 < NBLK))
                    sch.barrier()
            sP.close()
            sch.barrier()
        if do_sample:
            sA = ExitStack()
            S = NS()
            S.FTs = sbuf(sA, "FTs", [128, 42, 2, 2])
            with ExitStack() as st:
                M = mixer_bufs(st)
                S.biasN = sbuf(st, "biasN", [64, 8, 32])
                S.XBCs = sbuf(st, "XBCs", [128, 8, 2, 35])
                S.KTc = [sbuf(st, "KTc%d" % i, [128, 4, 512], BF16) for i in range(2)]
                S.VAc = [sbuf(st, "VAc%d" % i, [128, 4, 8, 65], BF16) for i in range(2)]
                S.CKf = sbuf(st, "CKf", [128, 512])
                S.CKg = sbuf(st, "CKg", [128, 512])
                CK3 = S.CKf[:].rearrange("p (a b) -> p a b", a=4)
                S.HF = [sbuf(st, "HFs%d" % i, [128, 512]) for i in range(2)]
                S.HB = [sbuf(st, "HBs%d" % i, [128, 512], BF16) for i in range(2)]
                S.KTn = sbuf(st, "KTn", [128, 4, 64], BF16)
                S.VAn = sbuf(st, "VAn", [64, 8, 65], BF16)
                L = 32
                if stage == 's1':
                    sch.finish()
                    st.close()
                    sA.close()
                    return nc
                sch.dma('sp', S.biasN[:], biasN_d[:, :, :], writes=['biasN'])
                Xs = M.X[1]
                sch.dma('sp', Xs[0:6, :], cs0[:, :], writes=[('X', 1)])

                def post_cs(j, n, src, bkk):
                    sch.op('act', lambda e: e.activation(out=S.XBCs[:, j:j + n, :, 0:3], in_=src.rearrange("p a (s r) -> p a s r", s=2), func=AF.Copy),
                           reads=[bkk], writes=['XBCs'])
                transpose_to(None, lambda j: Xs[0:6, j * 128:(j + 1) * 128], 6, 8, reads=[('X', 1)], writes=['XBCs'], post=post_cs)
                if stage == 's2':
                    sch.finish()
                    st.close()
                    sA.close()
                    return nc
                for c5 in range(6):
                    w = min(D, 2 * DFF - c5 * D)
                    sch.dma('sp', Xs[0:4, 0:w], fs0[:, c5 * D:c5 * D + w], writes=[('X', 1)])

                    def post_fs(j, n, src, bkk, c5=c5):
                        sch.op('act', lambda e: e.activation(out=S.FTs[:, c5 * 8 + j:c5 * 8 + j + n, :, :], in_=src.rearrange("p a (s r) -> p a s r", s=2),
                                                             func=AF.Copy), reads=[bkk], writes=[('FTs', q_) for q_ in range(42)])
                    transpose_to(None, lambda j: Xs[0:4, j * 128:(j + 1) * 128], 4, w // 128, reads=[('X', 1)], writes=['FTs'], post=post_fs)
                if stage == 's3':
                    sch.finish()
                    st.close()
                    sA.close()
                    return nc
                for s in range(2):
                    sch.dma('sp', CK3, hs0[s].rearrange("(a p) n -> p a n", p=128), writes=['CKf'])
                    if stage == 's3a':
                        sch.finish()
                        st.close()
                        sA.close()
                        return nc

                    def post_h0(j, n, src, bkk, s=s):
                        if stage == 's3c':
                            return
                        if stage == 's4v1':
                            sch.op('dve', lambda e: e.tensor_copy(out=M.T2[:], in_=src.rearrange("p a b -> p (a b)")), reads=[bkk], writes=['T2'])
                            return
                        if stage == 's4v3':
                            sch.op('dve', lambda e: e.tensor_scalar(out=M.T2[:], in0=src.rearrange("p a b -> p (a b)"), scalar1=1.0, scalar2=None, op0=ALU.mult), reads=[bkk], writes=['T2'])
                            return
                        sch.op('act', lambda e: e.activation(out=S.HF[s][:], in_=src.rearrange("p a b -> p (a b)"), func=AF.Copy),
                               reads=[bkk], writes=[('HF', 1 + s)])
                        if stage == 's3b':
                            return
                        if stage == 's4y':
                            sch.op('dve', lambda e: e.memset(M.T2[:], 0.0), reads=[bkk], writes=['T2'])
                            return
                        if stage == 's4z':
                            sch.op('dve', lambda e: e.memset(M.T2[:], 0.0), reads=[], writes=['T2'])
                            return
                        if stage == 's4x':
                            sch.op('dve', lambda e: e.tensor_copy(out=M.T2[:], in_=src.rearrange("p a b -> p (a b)")), reads=[bkk], writes=['T2'])
                            return
                        sch.op('dve', lambda e: e.tensor_copy(out=S.HB[s][:], in_=src.rearrange("p a b -> p (a b)")), reads=[bkk], writes=[('HB', 1 + s)])
                    transpose_to(None, lambda j: CK3[:, j, :], 128, 4, reads=['CKf'], writes=[], post=post_h0)
                if stage in ('s4', 's3b', 's3c', 's4x', 's4y', 's4z', 's4v1', 's4v3'):
                    sch.finish()
                    st.close()
                    sA.close()
                    return nc
                for s in range(2):
                    for m in range(4):
                        CKb, ckk = ((S.CKf, 'CKf'), (S.CKg, 'CKg'))[m % 2]
                        sch.dma('sp', CKb[:], ck[s][m * 128:(m + 1) * 128, :], writes=[ckk])
                        transpose_to(lambda j, n, m=m, s=s: S.KTc[s][:, j:j + n, m * 128:(m + 1) * 128], lambda j, CKb=CKb: CKb[:, j * 128:(j + 1) * 128], 128, 4,
                                     reads=[ckk], writes=[('KTc', s)])
                    for m in range(4):
                        CKb, ckk = ((S.CKf, 'CKf'), (S.CKg, 'CKg'))[m % 2]
                        sch.dma('sp', CKb[:], cv[s][m * 128:(m + 1) * 128, :], writes=[ckk])
                        sch.op('dve', lambda e, s=s, m=m, CKb=CKb: e.tensor_copy(out=S.VAc[s][:, m, :, 0:64], in_=CKb[:].rearrange("p (h d) -> p h d", h=8)),
                               reads=[ckk], writes=[('VAc', s)])
                    sch.op('pool', lambda e, s=s: e.memset(S.VAc[s][:, :, :, 64], 1.0), writes=[('VAc', s)])

                def tiles_s(s, h):
                    hp, po = h // 2, (h % 2) * 64
                    tl = []
                    for m in range(3):
                        tl.append(dict(kt=S.KTc[s][po:po + 64, hp, m * 128:(m + 1) * 128], ktkey=('KTc', s), va=S.VAc[s][:, m, h, :], vakey=('VAc', s),
                                       nk=128, pb=0, bias=None))
                    tl.append(dict(kt=S.KTc[s][po:po + 64, hp, 384:512], ktkey=('KTc', s), va=S.VAc[s][:, 3, h, :], vakey=('VAc', s),
                                   nk=128, pb=0, bias=P.biasT[:, 1, h, 0:32]))
                    tl.append(dict(kt=S.KTn[po:po + 64, hp, s * 32:(s + 1) * 32], ktkey='KTn', va=S.VAn[s * 32:(s + 1) * 32, h, :], vakey='VAn',
                                   nk=32, pb=s * 32, bias=S.biasN[s * 32:(s + 1) * 32, h, :]))
                    return tl

                kvd = (lambda j, n: S.KTn[:, j:j + n, 0:64], 'KTn', S.VAn[0:64, :, :], 'VAn')
                if stage == 'sample_setup':
                    sch.finish()
                    st.close()
                    sA.close()
                    return nc
                mixer_block(M, 'full', 64, 2, 32, xq[:, :], 0, S.XBCs[:], 'XBCs',
                            [(S.HF[0], S.HB[0], 1), (S.HF[1], S.HB[1], 2)], tiles_s,
                            dict(nk=nks_o[:, :], nv=nvs_o[:, :], sconv=scs_o), kvd, 0)
                if stage == 'sample_mixer':
                    sch.finish()
                    st.close()
                    sA.close()
                    return nc
                for s in range(2):
                    final_state(S.HF[s], 1 + s, ssms_o[s], CK3)
                sch.barrier()
            with ExitStack() as st2:
                Fb = ffn_bufs(st2)
                ffn(Fb, 64, 2, 32, [(0, 0, 64)], S.FTs[:], 'FTs', [ys_o[:, :]], fcs_o, False)
                sch.barrier()
            sA.close()

        sch.finish()
    return nc


_CACHE = {}


def _get_nc():
    if 'nc' not in _CACHE:
        _CACHE['nc'] = build_program()
    return _CACHE['nc']


def _host_consts(rel_bias):
    k = np.arange(128)[:, None]
    q = np.arange(128)[None, :]
    rb = rel_bias
    biasT = np.empty((128, 2, 8, 128), np.float32)
    for i in range(2):
        idx = np.clip(128 * i + q - k, -128, 128) + 128
        biasT[:, i] = np.transpose(rb[:, idx], (1, 0, 2))
    kk = np.arange(32)[:, None]
    qq = np.arange(32)[None, :]
    idx = np.clip(qq - kk, -128, 128) + 128
    bn = np.transpose(rb[:, idx], (1, 0, 2))
    biasN = np.concatenate([bn, bn], axis=0).astype(np.float32)
    cbias = rb[:, 256].reshape(1, 8).astype(np.float32)
    consts = np.zeros((128, 8, 128), np.float32)
    consts[:, 0] = np.eye(128)
    consts[:, 1] = (k <= q)
    consts[:, 2] = (k > q)
    consts[:, 3] = 1.0
    seg = np.arange(128) // 32
    same = seg[:, None] == seg[None, :]
    consts[:, 4] = (k <= q) & same
    consts[:, 5] = (k > q) & same
    consts[:, 6] = (seg == 0)[:, None]
    consts[:, 7] = (seg == 1)[:, None]
    return biasT, biasN, cbias, consts


def kernel(x_prompt, x_sample, cache_attn_k, cache_attn_v, state_ssm, state_ssm_conv, state_ffn_conv,
           w_in, rel_bias, attn_norm_g, ssm_conv_w, ssm_conv_b, ssm_dt_bias, ssm_A_log, ssm_D,
           ssm_norm_g, w_out, ln1_g, ln1_b, w_up, ffn_conv_w, ffn_conv_b, w_down, ln2_g, ln2_b):
    f32 = np.float32
    A = lambda a: np.ascontiguousarray(np.asarray(a, dtype=f32))
    x_prompt, x_sample = A(x_prompt), A(x_sample)
    ck_, cv_ = A(cache_attn_k)[0], A(cache_attn_v)[0]
    hs_, cs_, fs_ = A(state_ssm)[0], A(state_ssm_conv)[0], A(state_ffn_conv)[0]
    biasT, biasN, cbias, consts = _host_consts(A(rel_bias)[0])
    colp = np.zeros((128, CP_N), f32)
    colp[:, CP_AG:CP_AG + 4] = A(attn_norm_g)[0].reshape(4, 128).T
    colp[:, CP_SG:CP_SG + 4] = A(ssm_norm_g)[0].reshape(4, 128).T
    cw = A(ssm_conv_w)[0]
    colp[:, CP_CW:CP_CW + 32] = cw.reshape(4, 8, 128).transpose(2, 1, 0).reshape(128, 32)
    colp[:, CP_CB:CP_CB + 8] = A(ssm_conv_b)[0].reshape(8, 128).T
    fw = A(ffn_conv_w)[0]
    colp[:, CP_FW:CP_FW + 126] = fw.reshape(3, 42, 128).transpose(2, 1, 0).reshape(128, 126)
    colp[:, CP_FB:CP_FB + 42] = A(ffn_conv_b)[0].reshape(42, 128).T
    rowp = np.concatenate([A(ssm_dt_bias)[0], A(ssm_A_log)[0], A(ssm_D)[0]]).reshape(1, 24)
    lnp = np.concatenate([A(ln1_g)[0], A(ln1_b)[0], A(ln2_g)[0], A(ln2_b)[0]]).reshape(1, 4 * D)
    wu = A(w_up)[0]
    wu_perm = np.ascontiguousarray(wu.reshape(D, 2, NF, 128).transpose(0, 2, 1, 3).reshape(D, 2 * DFF))
    shared = dict(w_in=A(w_in)[0], w_out=A(w_out)[0], w_up=wu_perm, w_down=A(w_down)[0],
                  biasT=biasT, biasN=biasN, cbias=cbias, colp=colp, rowp=rowp, lnp=lnp, consts=consts)
    in_maps = []
    for c in range(8):
        b, hf = c // 2, c % 2
        if hf == 0:
            xs = np.concatenate([np.zeros((4096, D), f32), x_prompt[b, :4096]], axis=0)
        else:
            xs = x_prompt[b]
        m = dict(shared)
        m.update(xs=np.ascontiguousarray(xs), flag=np.full((128, 1), float(hf), f32),
                 xq=np.ascontiguousarray(x_sample[2 * c:2 * c + 2].reshape(64, D)),
                 ck=np.ascontiguousarray(ck_[2 * c:2 * c + 2].reshape(2, 512, 512)),
                 cv=np.ascontiguousarray(cv_[2 * c:2 * c + 2].reshape(2, 512, 512)),
                 hs0=np.ascontiguousarray(hs_[2 * c:2 * c + 2].reshape(2, 512, 128)),
                 cs0=np.ascontiguousarray(cs_[2 * c:2 * c + 2].reshape(6, D)),
                 fs0=np.ascontiguousarray(fs_[2 * c:2 * c + 2].reshape(4, 2 * DFF)))
        in_maps.append(m)
    nc = _get_nc()
    res = run_bass_kernel_spmd(nc, in_maps, core_ids=list(range(8)))
    R = res.results
    y_prompt = np.stack([np.concatenate([R[2 * b]["y_o"], R[2 * b + 1]["y_o"]], axis=0) for b in range(4)]).astype(f32)
    y_sample = np.concatenate([R[c]["ys_o"].reshape(2, 32, D) for c in range(8)], axis=0).astype(f32)
    nkp = np.stack([R[2 * b + 1]["nk_o"].reshape(512, 8, 64) for b in range(4)])[None].astype(f32)
    nvp = np.stack([R[2 * b + 1]["nv_o"].reshape(512, 8, 64) for b in range(4)])[None].astype(f32)
    nks = np.concatenate([R[c]["nks_o"].reshape(2, 32, 8, 64) for c in range(8)], axis=0)[None].astype(f32)
    nvs = np.concatenate([R[c]["nvs_o"].reshape(2, 32, 8, 64) for c in range(8)], axis=0)[None].astype(f32)
    ssmp = np.stack([R[2 * b + 1]["ssm_o"].reshape(8, 64, 128) for b in range(4)])[None].astype(f32)
    ssms = np.concatenate([R[c]["ssms_o"].reshape(2, 8, 64, 128) for c in range(8)], axis=0)[None].astype(f32)
    scp = np.stack([R[2 * b + 1]["sc_o"] for b in range(4)])[None].astype(f32)
    scs = np.concatenate([R[c]["scs_o"].reshape(2, 3, D) for c in range(8)], axis=0)[None].astype(f32)
    fcp = np.stack([R[2 * b + 1]["fc_o"] for b in range(4)])[None].astype(f32)
    fcs = np.concatenate([R[c]["fcs_o"].reshape(2, 2, 2 * DFF) for c in range(8)], axis=0)[None].astype(f32)
    return (y_prompt, y_sample, nkp, nvp, nks, nvs, ssmp, ssms, scp, scs, fcp, fcs)
```
